# Optimizing a Trainium2 kernel written in Bass

```python
import math
import jax, jax.numpy as jnp
from jax import lax
import numpy as np

D_MODEL = 1024
BATCH = 8
SEQ = 2048
DEPTH = 2
DEC_BATCH = 128
DEC_SEQ = 8
PAST_LEN = 8192
PAGE_SIZE = 128

D_MIX = D_MODEL
SSD_HEADS = 8
SSD_HEAD_DIM = 64
SSD_WIDTH = SSD_HEADS * SSD_HEAD_DIM
SSD_GROUPS = 2
SSD_STATE = 64
SSD_CONV = 4
SSD_CONV_DIM = SSD_WIDTH + 2 * SSD_GROUPS * SSD_STATE
SSD_CHUNK = 128
GLA_HEADS = 4
GLA_DK = 32
GLA_DV = 64
GLA_WIDTH = GLA_HEADS * GLA_DV
GLA_RANK = 16
GLA_GATE_NORM = 16.0
GLA_CHUNK = 64
SWA_HEADS = 4
SWA_KV_HEADS = 2
SWA_HEAD_DIM = 64
SWA_WIDTH = SWA_HEADS * SWA_HEAD_DIM
SWA_REP = SWA_HEADS // SWA_KV_HEADS
WINDOW = 128
SWA_BUF = min(WINDOW, PAST_LEN)
REL_BUCKETS = 32
REL_MAX_DIST = 128
PLE_DIM = 256
EPS = 1e-6

IN_SIZES = (SSD_WIDTH, SSD_CONV_DIM, SSD_HEADS,
            GLA_HEADS * GLA_DK, GLA_HEADS * GLA_DK, GLA_WIDTH, GLA_WIDTH, GLA_RANK,
            SWA_WIDTH, SWA_KV_HEADS * SWA_HEAD_DIM, SWA_KV_HEADS * SWA_HEAD_DIM, SWA_WIDTH)
D_IN = sum(IN_SIZES)

kernel_name = 'hybrid_ssd_gla_swa_step'


def _split_points():
    return np.cumsum(IN_SIZES)[:-1].tolist()


def rmsnorm(x, w):
    xf = x.astype(jnp.float32)
    y = xf * lax.rsqrt(jnp.mean(xf * xf, axis=-1, keepdims=True) + EPS)
    return (y * w.astype(jnp.float32)).astype(x.dtype)


def _chunk_len(L, c):
    return c if L % c == 0 else L


def causal_conv(xbc, buf, w, b):
    L = xbc.shape[1]
    xp = jnp.concatenate([buf.astype(xbc.dtype), xbc], axis=1)
    out = sum(xp[:, k:k + L] * w[k] for k in range(SSD_CONV)) + b
    return jax.nn.silu(out), xp[:, -(SSD_CONV - 1):]


def ssd_scan(x, dt, a, bm, cm, s0):
    Bsz, L = x.shape[:2]
    Q = _chunk_len(L, SSD_CHUNK)
    nc = L // Q
    G, R, P, N = SSD_GROUPS, SSD_HEADS // SSD_GROUPS, SSD_HEAD_DIM, SSD_STATE
    x = x.reshape(Bsz, nc, Q, G, R, P)
    dt = dt.reshape(Bsz, nc, Q, G, R)
    bm = bm.reshape(Bsz, nc, Q, G, N)
    cm = cm.reshape(Bsz, nc, Q, G, N)
    acum = jnp.cumsum(dt * a.reshape(G, R), axis=2)
    causal = jnp.tril(jnp.ones((Q, Q), dtype=bool))
    seg = acum[:, :, :, None] - acum[:, :, None, :]
    decay = jnp.exp(jnp.where(causal[:, :, None, None], seg, -jnp.inf))
    cb = jnp.einsum('bctgn,bcsgn->bctsg', cm, bm)
    xdt = x * dt[..., None]
    y_intra = jnp.einsum('bctsgr,bcsgrp->bctgrp', cb[..., None] * decay, xdt)
    tail = jnp.exp(acum[:, :, -1:] - acum)
    st = jnp.einsum('bcsgn,bcsgrp->bcgrpn', bm, xdt * tail[..., None])
    chunk_decay = jnp.exp(acum[:, :, -1])

    def step(s, inp):
        dec, stc = inp
        return dec[..., None, None] * s + stc, s

    s_final, s_prev = lax.scan(step, s0.reshape(Bsz, G, R, P, N),
                               (jnp.moveaxis(chunk_decay, 1, 0), jnp.moveaxis(st, 1, 0)))
    s_prev = jnp.moveaxis(s_prev, 0, 1)
    y_inter = jnp.einsum('bctgn,bcgrpn->bctgrp', cm, s_prev) * jnp.exp(acum)[..., None]
    y = (y_intra + y_inter).reshape(Bsz, L, SSD_HEADS, P)
    return y, s_final.reshape(Bsz, SSD_HEADS, P, N)


def gla_scan(q, k, v, g, s0):
    Bsz, L, H, Dk = q.shape
    Dv = v.shape[-1]
    Q = _chunk_len(L, GLA_CHUNK)
    nc = L // Q
    q = q.reshape(Bsz, nc, Q, H, Dk)
    k = k.reshape(Bsz, nc, Q, H, Dk)
    v = v.reshape(Bsz, nc, Q, H, Dv)
    b = jnp.cumsum(g.reshape(Bsz, nc, Q, H, Dk), axis=2)
    qe = q * jnp.exp(b)
    ke = k * jnp.exp(-b)
    causal = jnp.tril(jnp.ones((Q, Q), dtype=bool))
    att = jnp.where(causal, jnp.einsum('bcthd,bcshd->bchts', qe, ke), 0.0)
    o_intra = jnp.einsum('bchts,bcshv->bcthv', att, v)
    kd = k * jnp.exp(b[:, :, -1:] - b)
    U = jnp.einsum('bcshd,bcshv->bchdv', kd, v)
    dec = jnp.exp(b[:, :, -1])

    def step(s, inp):
        d, u = inp
        return d[..., None] * s + u, s

    s_final, s_prev = lax.scan(step, s0, (jnp.moveaxis(dec, 1, 0), jnp.moveaxis(U, 1, 0)))
    s_prev = jnp.moveaxis(s_prev, 0, 1)
    o_inter = jnp.einsum('bcthd,bchdv->bcthv', qe, s_prev)
    return (o_intra + o_inter).reshape(Bsz, L, H, Dv), s_final


def rel_bucket(dist):
    n = jnp.maximum(dist, 0)
    exact = REL_BUCKETS // 2
    nf = jnp.maximum(n, 1).astype(jnp.float32)
    large = exact + (jnp.log(nf / exact) / math.log(REL_MAX_DIST / exact)
                     * (REL_BUCKETS - exact)).astype(jnp.int32)
    large = jnp.minimum(large, REL_BUCKETS - 1)
    return jnp.where(n < exact, n, large)


def swa_attend(q, k, v, dist, valid, rel_bias, sinks):
    G, R = SWA_KV_HEADS, SWA_REP
    Tq, Tk = dist.shape
    logits = jnp.einsum('bnqgrd,bnkgd->bngrqk', q, k).astype(jnp.float32) * (SWA_HEAD_DIM ** -0.5)
    bias = rel_bias.astype(jnp.float32)[rel_bucket(dist)]
    bias = jnp.moveaxis(bias, -1, 0).reshape(G, R, Tq, Tk)
    mask = valid[None, :, None, None] & (dist >= 0) & (dist < WINDOW)
    logits = jnp.where(mask, logits + bias, -jnp.inf)
    sink = jnp.broadcast_to(sinks.astype(jnp.float32).reshape(G, R, 1, 1), logits.shape[:-1] + (1,))
    probs = jax.nn.softmax(jnp.concatenate([logits, sink], axis=-1), axis=-1)[..., :-1]
    return jnp.einsum('bngrqk,bnkgd->bnqgrd', probs.astype(v.dtype), v)


def swa_prompt(q, k, v, rel_bias, sinks):
    Bsz, L = q.shape[:2]
    nb = L // WINDOW
    qb = q.reshape(Bsz, nb, WINDOW, SWA_KV_HEADS, SWA_REP, SWA_HEAD_DIM)
    kb = k.reshape(Bsz, nb, WINDOW, SWA_KV_HEADS, SWA_HEAD_DIM)
    vb = v.reshape(Bsz, nb, WINDOW, SWA_KV_HEADS, SWA_HEAD_DIM)
    pad = ((0, 0), (1, 0), (0, 0), (0, 0), (0, 0))
    k_ext = jnp.concatenate([jnp.pad(kb[:, :-1], pad), kb], axis=2)
    v_ext = jnp.concatenate([jnp.pad(vb[:, :-1], pad), vb], axis=2)
    dist = WINDOW + jnp.arange(WINDOW)[:, None] - jnp.arange(2 * WINDOW)[None, :]
    valid = (jnp.arange(nb)[:, None, None] > 0) | (jnp.arange(2 * WINDOW)[None, None, :] >= WINDOW)
    o = swa_attend(qb, k_ext, v_ext, dist, valid, rel_bias, sinks)
    return o.reshape(Bsz, L, SWA_WIDTH)


def swa_sample(q, k, v, k_buf, v_buf, rel_bias, sinks):
    Bsz, L = q.shape[:2]
    kk = jnp.concatenate([k_buf.astype(k.dtype), k], axis=1)
    vv = jnp.concatenate([v_buf.astype(v.dtype), v], axis=1)
    Tk = kk.shape[1]
    dist = SWA_BUF + jnp.arange(L)[:, None] - jnp.arange(Tk)[None, :]
    valid = jnp.ones((1, 1, Tk), dtype=bool)
    qb = q.reshape(Bsz, 1, L, SWA_KV_HEADS, SWA_REP, SWA_HEAD_DIM)
    o = swa_attend(qb, kk[:, None], vv[:, None], dist, valid, rel_bias, sinks)
    return o.reshape(Bsz, L, SWA_WIDTH), kk[:, -SWA_BUF:], vv[:, -SWA_BUF:]


def layer(h, p, conv_buf, ssm0, gla0, k_buf, v_buf, rel_bias, norm_w, w_in, conv_w, conv_b,
          dt_bias, a_log, d_skip, ssd_norm_w, gla_w_gk, gla_b_gk, gla_norm_w, q_norm_w,
          k_norm_w, sinks, w_out, w_pe, w_pg):
    f32 = jnp.float32
    Bsz, L, _ = h.shape
    u = rmsnorm(h, norm_w)
    proj = u @ w_in
    (z, xbc, dt, gq, gk, gv, gg, glr, sq, sk, sv, sg) = jnp.split(proj, _split_points(), axis=-1)

    xbc_c, conv_new = causal_conv(xbc, conv_buf, conv_w, conv_b)
    xs, bm, cm = jnp.split(xbc_c, [SSD_WIDTH, SSD_WIDTH + SSD_GROUPS * SSD_STATE], axis=-1)
    xs = xs.reshape(Bsz, L, SSD_HEADS, SSD_HEAD_DIM).astype(f32)
    dtv = jax.nn.softplus(dt.astype(f32) + dt_bias.astype(f32))
    a = -jnp.exp(a_log.astype(f32))
    y, ssm_new = ssd_scan(xs, dtv, a,
                          bm.reshape(Bsz, L, SSD_GROUPS, SSD_STATE).astype(f32),
                          cm.reshape(Bsz, L, SSD_GROUPS, SSD_STATE).astype(f32),
                          ssm0.astype(f32))
    y = y + d_skip.astype(f32)[:, None] * xs
    y = (y.reshape(Bsz, L, SSD_WIDTH) * jax.nn.silu(z.astype(f32)))
    y = y.reshape(Bsz, L, SSD_GROUPS, SSD_WIDTH // SSD_GROUPS)
    y_ssd = rmsnorm(y, ssd_norm_w.reshape(SSD_GROUPS, -1)).reshape(Bsz, L, SSD_WIDTH)

    q = gq.reshape(Bsz, L, GLA_HEADS, GLA_DK).astype(f32) * (GLA_DK ** -0.5)
    k = gk.reshape(Bsz, L, GLA_HEADS, GLA_DK).astype(f32)
    v = gv.reshape(Bsz, L, GLA_HEADS, GLA_DV).astype(f32)
    glog = jax.nn.log_sigmoid((glr @ gla_w_gk + gla_b_gk).astype(f32)) / GLA_GATE_NORM
    o, gla_new = gla_scan(q, k, v, glog.reshape(Bsz, L, GLA_HEADS, GLA_DK), gla0.astype(f32))
    y_gla = rmsnorm(o, gla_norm_w).reshape(Bsz, L, GLA_WIDTH) * jax.nn.silu(gg.astype(f32))

    qa = rmsnorm(sq.reshape(Bsz, L, SWA_HEADS, SWA_HEAD_DIM), q_norm_w)
    ka = rmsnorm(sk.reshape(Bsz, L, SWA_KV_HEADS, SWA_HEAD_DIM), k_norm_w)
    va = sv.reshape(Bsz, L, SWA_KV_HEADS, SWA_HEAD_DIM)
    if k_buf is None:
        oa = swa_prompt(qa, ka, va, rel_bias, sinks)
        k_new, v_new = ka[:, -SWA_BUF:], va[:, -SWA_BUF:]
    else:
        oa, k_new, v_new = swa_sample(qa, ka, va, k_buf, v_buf, rel_bias, sinks)
    y_swa = oa.astype(f32) * jax.nn.silu(sg.astype(f32))

    mix = jnp.concatenate([y_ssd.astype(f32), y_gla, y_swa], axis=-1).astype(h.dtype) @ w_out
    h = h + mix
    h = h + jax.nn.sigmoid(h @ w_pg) * (p.astype(h.dtype) @ w_pe)
    dt_ = h.dtype
    return (h, ssm_new.astype(dt_), conv_new.astype(dt_), gla_new.astype(dt_),
            k_new.astype(dt_), v_new.astype(dt_))


def setup_inputs(seed: int = 0) -> dict:
    key = jax.random.key(seed)
    ks = jax.random.split(key, 32)
    f32 = jnp.float32

    def nrm(k, shape, scale):
        return jax.random.normal(k, shape, f32) * scale

    dt_init = jnp.exp(jax.random.uniform(ks[10], (DEPTH, SSD_HEADS), f32,
                                         math.log(1e-3), math.log(1e-1)))
    return dict(
        x_prompt=nrm(ks[0], (BATCH, SEQ, D_MODEL), 1.0),
        x_sample=nrm(ks[1], (DEC_BATCH, DEC_SEQ, D_MODEL), 1.0),
        state_ssm=nrm(ks[2], (DEPTH, DEC_BATCH, SSD_HEADS, SSD_HEAD_DIM, SSD_STATE), 0.1),
        state_conv=nrm(ks[3], (DEPTH, DEC_BATCH, SSD_CONV - 1, SSD_CONV_DIM), 1.0),
        state_gla=nrm(ks[4], (DEPTH, DEC_BATCH, GLA_HEADS, GLA_DK, GLA_DV), 0.3),
        cache_swa_k=nrm(ks[5], (DEPTH, DEC_BATCH, SWA_BUF, SWA_KV_HEADS, SWA_HEAD_DIM), 1.0),
        cache_swa_v=nrm(ks[6], (DEPTH, DEC_BATCH, SWA_BUF, SWA_KV_HEADS, SWA_HEAD_DIM), 1.0),
        p_prompt=nrm(ks[7], (DEPTH, BATCH, SEQ, PLE_DIM), 1.0),
        p_sample=nrm(ks[8], (DEPTH, DEC_BATCH, DEC_SEQ, PLE_DIM), 1.0),
        rel_bias=nrm(ks[9], (REL_BUCKETS, SWA_HEADS), 0.5),
        norm_w=1.0 + nrm(ks[11], (DEPTH, D_MODEL), 0.01),
        w_in=nrm(ks[12], (DEPTH, D_MODEL, D_IN), D_MODEL ** -0.5),
        conv_w=nrm(ks[13], (DEPTH, SSD_CONV, SSD_CONV_DIM), SSD_CONV ** -0.5),
        conv_b=nrm(ks[14], (DEPTH, SSD_CONV_DIM), 0.01),
        dt_bias=dt_init + jnp.log(-jnp.expm1(-dt_init)),
        a_log=jnp.log(jax.random.uniform(ks[15], (DEPTH, SSD_HEADS), f32, 1.0, 16.0)),
        d_skip=1.0 + nrm(ks[16], (DEPTH, SSD_HEADS), 0.01),
        ssd_norm_w=1.0 + nrm(ks[17], (DEPTH, SSD_WIDTH), 0.01),
        gla_w_gk=nrm(ks[18], (DEPTH, GLA_RANK, GLA_HEADS * GLA_DK), GLA_RANK ** -0.5),
        gla_b_gk=nrm(ks[19], (DEPTH, GLA_HEADS * GLA_DK), 0.01),
        gla_norm_w=1.0 + nrm(ks[20], (DEPTH, GLA_DV), 0.01),
        q_norm_w=1.0 + nrm(ks[21], (DEPTH, SWA_HEAD_DIM), 0.01),
        k_norm_w=1.0 + nrm(ks[22], (DEPTH, SWA_HEAD_DIM), 0.01),
        attn_sinks=nrm(ks[23], (DEPTH, SWA_HEADS), 0.5),
        w_out=nrm(ks[24], (DEPTH, D_MIX, D_MODEL), 0.5 * D_MIX ** -0.5),
        w_pe=nrm(ks[25], (DEPTH, PLE_DIM, D_MODEL), 0.5 * PLE_DIM ** -0.5),
        w_pg=nrm(ks[26], (DEPTH, D_MODEL, D_MODEL), D_MODEL ** -0.5),
    )


def reference(x_prompt, x_sample, state_ssm, state_conv, state_gla, cache_swa_k, cache_swa_v,
              p_prompt, p_sample, rel_bias, norm_w, w_in, conv_w, conv_b, dt_bias, a_log,
              d_skip, ssd_norm_w, gla_w_gk, gla_b_gk, gla_norm_w, q_norm_w, k_norm_w,
              attn_sinks, w_out, w_pe, w_pg):
    hp, hs = x_prompt, x_sample
    bp = x_prompt.shape[0]
    dtp = x_prompt.dtype
    sp_ssm, sp_conv, sp_gla, sp_k, sp_v = [], [], [], [], []
    ss_ssm, ss_conv, ss_gla, ss_k, ss_v = [], [], [], [], []
    for i in range(DEPTH):
        wts = (rel_bias, norm_w[i], w_in[i], conv_w[i], conv_b[i], dt_bias[i], a_log[i], d_skip[i],
               ssd_norm_w[i], gla_w_gk[i], gla_b_gk[i], gla_norm_w[i], q_norm_w[i], k_norm_w[i],
               attn_sinks[i], w_out[i], w_pe[i], w_pg[i])
        conv0 = jnp.zeros((bp, SSD_CONV - 1, SSD_CONV_DIM), dtp)
        ssm0 = jnp.zeros((bp, SSD_HEADS, SSD_HEAD_DIM, SSD_STATE), dtp)
        gla0 = jnp.zeros((bp, GLA_HEADS, GLA_DK, GLA_DV), dtp)
        hp, a1, a2, a3, a4, a5 = layer(hp, p_prompt[i], conv0, ssm0, gla0, None, None, *wts)
        sp_ssm.append(a1); sp_conv.append(a2); sp_gla.append(a3); sp_k.append(a4); sp_v.append(a5)
        hs, b1, b2, b3, b4, b5 = layer(hs, p_sample[i], state_conv[i], state_ssm[i], state_gla[i],
                                       cache_swa_k[i], cache_swa_v[i], *wts)
        ss_ssm.append(b1); ss_conv.append(b2); ss_gla.append(b3); ss_k.append(b4); ss_v.append(b5)
    return (hp, hs,
            jnp.stack(sp_ssm), jnp.stack(sp_conv), jnp.stack(sp_gla), jnp.stack(sp_k), jnp.stack(sp_v),
            jnp.stack(ss_ssm), jnp.stack(ss_conv), jnp.stack(ss_gla), jnp.stack(ss_k), jnp.stack(ss_v))
```

```python
import numpy as np, math
from contextlib import ExitStack
import concourse.bass as bass
import concourse.mybir as mybir
from concourse.bass_utils import run_bass_kernel_spmd

F32 = mybir.dt.float32
BF16 = mybir.dt.bfloat16
AF = mybir.ActivationFunctionType
ALU = mybir.AluOpType
AX = mybir.AxisListType


NO_ELIDE = False


class Tok:
    __slots__ = ("w", "r", "const", "excl")

    def __init__(self, const=False, excl=False):
        self.w = None
        self.r = {}
        self.const = const
        self.excl = excl


class Op:
    __slots__ = ("eng", "fn", "deps", "sig", "val", "kind", "key", "seq")
    _n = 0

    def __init__(self, eng, fn, kind="op", key=None):
        Op._n += 1
        self.seq = Op._n
        self.eng = eng
        self.fn = fn
        self.deps = {}
        self.sig = False
        self.val = None
        self.kind = kind
        self.key = key


class Ctx:
    ENGS = ("pe", "act", "dve", "pool", "sp")

    def __init__(self, nc, es):
        self.nc = nc
        self.es = es
        self.sem = {k: es.enter_context(nc.semaphore("s_" + k)) for k in self.ENGS}
        self.ops = {k: [] for k in self.ENGS}
        self.dsem = {}
        self.dcnt = {}
        self.free_deps = {}

    def _skey(self, op):
        return op.key if op.kind == "dma" else op.eng

    def _adddep(self, op, p):
        if p is None or p is op:
            return
        k = self._skey(p)
        q = op.deps.get(k)
        if q is None or q.seq < p.seq:
            op.deps[k] = p

    def _track(self, op, reads, writes):
        k0 = self._skey(op)
        for t in reads:
            self._adddep(op, t.w)
            if t.excl:
                for kk, p in t.r.items():
                    if kk != k0:
                        self._adddep(op, p)
        for t in writes:
            self._adddep(op, t.w)
            for p in t.r.values():
                self._adddep(op, p)
        k = self._skey(op)
        for t in reads:
            if not t.const:
                t.r[k] = op
        for t in writes:
            t.w = op
            t.r = {}

    def op(self, eng, fn, reads=(), writes=()):
        o = Op(eng, fn)
        self._track(o, reads, writes)
        self.ops[eng].append(o)
        return o

    def dma(self, q, out_ap, in_ap, reads=(), writes=(), key=None, nc_ok=False):
        if key not in self.dsem:
            self.dsem[key] = self.es.enter_context(self.nc.semaphore("d_" + key))
            self.dcnt[key] = 0
        if nc_ok:
            fn = lambda e: e.dma_start(out=out_ap, in_=in_ap, allow_slow_non_contiguous=True)
        else:
            fn = lambda e: e.dma_start(out=out_ap, in_=in_ap)
        o = Op(q, fn, kind="dma", key=key)
        self._track(o, reads, writes)
        self.dcnt[key] += 16
        o.val = self.dcnt[key]
        self.ops[q].append(o)
        return o

    def new_tok(self, const=False):
        t = Tok(const)
        t.r = dict(self.free_deps)
        return t

    def release(self, toks):
        for t in toks:
            for p in list(t.r.values()) + ([t.w] if t.w is not None else []):
                k = self._skey(p)
                q = self.free_deps.get(k)
                if q is None or q.seq < p.seq:
                    self.free_deps[k] = p

    def emit(self, final_eng="sp"):
        for e in self.ENGS:
            for o in self.ops[e]:
                for p in o.deps.values():
                    if p.kind == "op" and not (p.eng == "pe" and o.eng == "pe" and o.kind == "op"):
                        p.sig = True
        for e in self.ENGS:
            n = 0
            for o in self.ops[e]:
                if o.kind == "op" and o.sig:
                    n += 1
                    o.val = n
        fin = [(self.dsem[k], self.dcnt[k]) for k in self.dsem]
        nsig = sum(1 for e in self.ENGS for o in self.ops[e] if o.sig)
        nops = sum(len(self.ops[e]) for e in self.ENGS)
        print("ops", {e: len(self.ops[e]) for e in self.ENGS}, "signals", nsig, "dma keys", len(self.dsem))

        class _PEProxy:
            def __init__(self, e):
                self._e = e

            def __getattr__(self, n):
                return getattr(self._e, n)

            def matmul(self, *a, **k):
                k.setdefault("skip_group_check", True)
                return self._e.matmul(*a, **k)

        def run(ename, e):
            if ename == "pe":
                e = _PEProxy(e)
            waited = {}
            for o in self.ops[ename]:
                for p in o.deps.values():
                    if p.kind == "op":
                        if p.eng == "pe" and ename == "pe" and o.kind == "op":
                            continue
                        sem = self.sem[p.eng]
                        sk = p.eng
                    else:
                        sem = self.dsem[p.key]
                        sk = p.key
                    if NO_ELIDE or waited.get(sk, 0) < p.val:
                        waited[sk] = max(waited.get(sk, 0), p.val)
                        e.wait_ge(sem, p.val)
                ins = o.fn(e)
                if o.kind == "dma":
                    ins.then_inc(self.dsem[o.key], 16)
                elif o.sig:
                    ins.then_inc(self.sem[ename], 1)
            if ename == final_eng:
                for (s, v) in fin:
                    e.wait_ge(s, v)

        with self.nc.Block() as block:
            @block.tensor
            def _(e):
                run("pe", e)

            @block.scalar
            def _(e):
                run("act", e)

            @block.vector
            def _(e):
                run("dve", e)

            @block.gpsimd
            def _(e):
                run("pool", e)

            @block.sync
            def _(e):
                run("sp", e)


NEGB = -240000.0
EPS = 1e-6
ENABLE = {"swa": True, "gla": True, "ssd": True}
PIPE = {"swa": True, "gla": True, "ssd": True}
NOY1 = False
DEBUG = {}

CF = dict(identf=(0, 128), triP=(128, 256), triS=(256, 384), supP=(384, 512), supS=(512, 640), onesf=(640, 768),
          sameS=(768, 896), smtok=(896, 912), bd2=(912, 914), bd4=(914, 918))
NF = 920
CB = dict(ident=(0, 128), triP=(128, 256), triS=(256, 384), ntriP=(384, 512), ntriS=(512, 640), negP=(640, 768),
          negS=(768, 896), ones=(896, 1024), smbt=(1024, 3072), smtok=(3072, 3088), bd2=(3088, 3090), bd4=(3090, 3094))
NB = 3096


def make_consts():
    k = np.arange(128)[:, None]
    t = np.arange(128)[None, :]
    same = (k // 8 == t // 8)
    f = np.zeros((128, NF), np.float32)
    b = np.zeros((128, NB), np.float32)

    def put(arr, d, name, val):
        a, e = d[name]
        arr[:, a:e] = val

    put(f, CF, "identf", (k == t))
    put(f, CF, "triP", (k <= t))
    put(f, CF, "triS", same & (k <= t))
    put(f, CF, "supP", (k > t))
    put(f, CF, "supS", same & (k > t))
    put(f, CF, "onesf", 1.0)
    put(f, CF, "sameS", same)
    put(f, CF, "smtok", (k // 8 == np.arange(16)[None, :]))
    put(f, CF, "bd2", (k // 64 == np.arange(2)[None, :]))
    put(f, CF, "bd4", (k // 32 == np.arange(4)[None, :]))
    put(b, CB, "ident", (k == t))
    put(b, CB, "triP", (k <= t))
    put(b, CB, "triS", same & (k <= t))
    put(b, CB, "ntriP", -1.0 * (k <= t))
    put(b, CB, "ntriS", -1.0 * (same & (k <= t)))
    put(b, CB, "negP", np.where(k <= t, 0.0, -30000.0))
    put(b, CB, "negS", np.where(same & (k <= t), 0.0, -30000.0))
    put(b, CB, "ones", 1.0)
    smbt = (np.arange(16)[:, None] == (np.arange(128)[None, :] // 8)).astype(np.float32).reshape(1, 2048)
    put(b, CB, "smbt", np.broadcast_to(smbt, (128, 2048)))
    put(b, CB, "smtok", (k // 8 == np.arange(16)[None, :]))
    put(b, CB, "bd2", (k // 64 == np.arange(2)[None, :]))
    put(b, CB, "bd4", (k // 32 == np.arange(4)[None, :]))
    oh = np.zeros((33, 384), np.float32)
    for i in range(384):
        dist = i - 127
        if 0 <= dist < 128:
            n = dist
            if n < 16:
                bk = n
            else:
                nf = np.float32(max(n, 1))
                v = np.log(nf / np.float32(16)) / np.float32(math.log(128 / 16)) * np.float32(16)
                bk = min(16 + int(np.int32(v)), 31)
            oh[bk, i] = 1.0
        else:
            oh[32, i] = 1.0
    return f, b, oh


class K:
    def __init__(self):
        self.nc = bass.Bass("TRN2", target_bir_lowering=False)
        self.es = ExitStack()

    def din(self, name, shape):
        return self.nc.dram_tensor(name, list(shape), F32, kind="ExternalInput")

    def dout(self, name, shape):
        return self.nc.dram_tensor(name, list(shape), F32, kind="ExternalOutput")

    def sb(self, name, shape, dt):
        return self.es.enter_context(self.nc.sbuf_tensor(name, list(shape), dt))

    def av(self, off, shape, dt):
        n = int(np.prod(shape[1:]))
        nb = n * (4 if dt == F32 else 2)
        assert off % 4 == 0 and off + nb <= self.ABYTES, (off, nb, self.ABYTES)
        ap = self.arena[:, off // 2:(off + nb) // 2]
        if dt == F32:
            ap = ap.bitcast(F32)
        if len(shape) == 3:
            ap = ap.rearrange("p (a b) -> p a b", a=shape[1])
        elif len(shape) == 4:
            ap = ap.rearrange("p (a b c) -> p a b c", a=shape[1], b=shape[2])
        elif len(shape) == 5:
            ap = ap.rearrange("p (a b c d) -> p a b c d", a=shape[1], b=shape[2], c=shape[3])
        return ap

    def cf(self, name):
        a, e = CF[name]
        return self.cstf[:, a:e]

    def cb(self, name):
        a, e = CB[name]
        return self.cstb[:, a:e]

    def psb(self, bank, dt=F32):
        t = self.psum[bank]
        ap = t[:, :]
        if dt == BF16:
            ap = ap.bitcast(BF16)
        return ap

    def build(self):
        nc, es = self.nc, self.es
        c = self.c = Ctx(nc, es)
        d = self.d = {}
        d["x"] = self.din("x_tok", [2176, 1024])
        d["p"] = self.din("p_tok", [2, 2176, 256])
        d["st_ssm"] = self.din("st_ssm", [2, 16, 8, 64, 64])
        d["st_conv"] = self.din("st_conv", [2, 16, 3, 768])
        d["st_gla"] = self.din("st_gla", [2, 16, 4, 32, 64])
        d["ck"] = self.din("ck", [2, 16, 128, 128])
        d["cv"] = self.din("cv", [2, 16, 128, 128])
        d["rel_bias"] = self.din("rel_bias", [32, 4])
        d["norm_w"] = self.din("norm_w", [2, 1024])
        d["w_in"] = self.din("w_in", [2, 1024, 2968])
        d["conv_w"] = self.din("conv_w", [2, 4, 768])
        d["conv_b"] = self.din("conv_b", [2, 768])
        d["rep8"] = self.din("rep8", [2, 24])
        d["ssd_norm_w"] = self.din("ssd_norm_w", [2, 512])
        d["wgk"] = self.din("wgk", [2, 17, 128])
        d["hdw"] = self.din("hdw", [2, 7 * 64])
        d["sinks"] = self.din("sinks", [2, 4])
        d["w_out"] = self.din("w_out", [2, 1024, 1024])
        d["w_pe"] = self.din("w_pe", [2, 256, 1024])
        d["w_pg"] = self.din("w_pg", [2, 1024, 1024])
        d["cstf"] = self.din("cstf", [128, NF])
        d["cstb"] = self.din("cstb", [128, NB])
        d["oh"] = self.din("oh", [33, 384])
        o = self.o = {}
        o["y"] = self.dout("y_tok", [2176, 1024])
        o["ssm_p"] = self.dout("ssm_p", [2, 8, 64, 64])
        o["conv_p"] = self.dout("conv_p", [2, 3, 768])
        o["gla_p"] = self.dout("gla_p", [2, 4, 32, 64])
        o["swak_p"] = self.dout("swak_p", [2, 128, 128])
        o["swav_p"] = self.dout("swav_p", [2, 128, 128])
        o["ssm_s"] = self.dout("ssm_s", [2, 16, 8, 64, 64])
        o["conv_s"] = self.dout("conv_s", [2, 16, 3, 768])
        o["gla_s"] = self.dout("gla_s", [2, 16, 4, 32, 64])
        o["swak_s"] = self.dout("swak_s", [2, 16, 128, 128])
        o["swav_s"] = self.dout("swav_s", [2, 16, 128, 128])
        self.scr = nc.dram_tensor("scr_bias", [128, 1536], F32)
        self.scr_ac = nc.dram_tensor("scr_acum", [128, 72], F32)
        self.t_scr_ac = Tok()
        for nm, shp in DEBUG.items():
            o[nm] = self.dout(nm, shp)

        self.hT = self.sb("hT", [128, 8, 1152], F32)
        self.uT = self.sb("uT", [128, 8, 1152], BF16)
        self.pT = self.sb("pT", [128, 2, 1152], BF16)
        self.W = [self.sb("W0", [128, 8, 1024], BF16), self.sb("W1", [128, 8, 1024], BF16)]
        self.Wt = [Tok(), Tok()]
        self.wi = 0
        self.cstf = self.sb("cstf_sb", [128, NF], F32)
        self.cstb = self.sb("cstb_sb", [128, NB], BF16)
        self.CT = Tok(const=True)
        self.biasP = [self.sb("biasP_hi", [128, 2, 2, 2, 128], BF16), self.sb("biasP_lo", [128, 2, 2, 2, 128], BF16)]
        self.biasS = [self.sb("biasS_hi", [128, 2, 2, 128], BF16), self.sb("biasS_lo", [128, 2, 2, 128], BF16)]
        self.biasC = [self.sb("biasC_hi", [128, 16, 2, 2, 8], BF16), self.sb("biasC_lo", [128, 16, 2, 2, 8], BF16)]
        self.nwT = self.sb("nwT", [128, 8], F32)
        self.cwT = self.sb("cwT", [128, 6, 4], F32)
        self.cbT = self.sb("cbT", [128, 6], F32)
        self.rep8 = self.sb("rep8_sb", [128, 24], F32)
        self.arep = self.sb("arep", [128, 8], F32)
        self.ssdnw = self.sb("ssdnw", [128, 512], F32)
        self.hdw = self.sb("hdw_sb", [128, 7, 64], F32)
        self.esink = self.sb("esink", [128, 4], F32)
        self.wgkf = self.sb("wgkf", [17, 128], F32)
        self.wgk = self.sb("wgkb", [17, 128], BF16)
        self.PT_ = Tok()
        self.Esel = self.sb("Esel", [128, 9, 128], BF16)
        self.S = []
        for l in range(2):
            st = dict(
                ssd=self.sb("Sssd%d" % l, [128, 512], F32), ssd_b=self.sb("Sssdb%d" % l, [128, 512], BF16),
                gla=self.sb("Sgla%d" % l, [128, 256], F32), gla_b=self.sb("Sglab%d" % l, [128, 256], BF16),
                ctail=self.sb("ctail%d" % l, [128, 6, 3], BF16),
                KTz=[self.sb("KTz%d_%d" % (l, i), [128, 2, 128], BF16) for i in range(3)],
                vext=[self.sb("vext%d_%d" % (l, i), [128, 2, 65], BF16) for i in range(4)],
                t_ssd=Tok(), t_gla=Tok(), t_ctail=Tok(), t_kv=[Tok(), Tok(), Tok()], t_vx=[Tok(), Tok(), Tok(), Tok()],
            )
            self.S.append(st)
        self.psum = [es.enter_context(nc.psum_tensor("ps%d" % i, [128, 512], F32)) for i in range(8)]
        self.pst = [Tok(excl=True) for _ in range(8)]
        self.ABYTES = 78 * 1024
        self.arena = self.sb("arena", [128, self.ABYTES // 2], BF16)

        self.marks = []
        self.wq = {}
        self.setup()
        import os
        self.trunc = os.environ.get("TRUNC")
        order = [(0, 0), (1, 0), (0, 1), (1, 1)]
        for ui, (l, hf) in enumerate(order):
            self.next_unit = order[ui + 1] if ui + 1 < len(order) else None
            self.unit(l, hf)
            if self.trunc:
                break
        c.emit()
        return nc

    def setup(self):
        c, d = self.c, self.d
        c.dma("sp", self.cstf[:, :], d["cstf"].ap(), writes=[self.CT], key="cst")
        c.dma("pool", self.cstb[:, :], d["cstb"].ap(), writes=[self.CT], key="cstb")
        A = Tok()
        for l in range(2):
            st = self.S[l]
            c.op("pool", lambda e, st=st: e.memset(st["ssd"][:, :], 0.0), writes=[st["t_ssd"]])
            c.op("pool", lambda e, st=st: e.memset(st["ssd_b"][:, :], 0.0), writes=[st["t_ssd"]])
            c.op("pool", lambda e, st=st: e.memset(st["gla"][:, :], 0.0), writes=[st["t_gla"]])
            c.op("pool", lambda e, st=st: e.memset(st["gla_b"][:, :], 0.0), writes=[st["t_gla"]])
            c.op("pool", lambda e, st=st: e.memset(st["ctail"][:, :, :], 0.0), writes=[st["t_ctail"]])
            for i in range(4):
                c.op("pool", lambda e, st=st, i=i: e.memset(st["vext"][i][:, :, :], 1.0), writes=[st["t_vx"][i]])
        c.op("dve", lambda e: e.tensor_tensor(self.Esel[:, :, :], self.cb("ident")[:, 0:9].unsqueeze(2).to_broadcast([128, 9, 128]),
                                              self.cb("ident")[:, 32:41].unsqueeze(2).to_broadcast([128, 9, 128]), ALU.add), reads=[self.CT], writes=[self.CT])
        rb = self.av(0, [128, 4], F32)
        ohs = self.av(16, [128, 384], F32)
        R = self.av(16 + 1536, [128, 384, 4], F32)
        Frep = self.av(16 + 1536 + 6144, [128, 1536], F32)
        Tt = self.av(16 + 1536 + 6144 + 6144, [128, 256, 4], F32)
        tmp = self.av(16 + 1536 + 6144 + 6144 + 4096, [128, 2048], F32)
        nseq = self.av(16 + 1536 + 6144 + 6144 + 4096 + 8192, [128, 128], F32)
        c.op("dve", lambda e: e.memset(rb[0:64, :], NEGB / 8.0), writes=[A])
        c.dma("sp", rb[0:32, :], d["rel_bias"].ap(), writes=[A], key="su1")
        c.dma("sp", ohs[0:33, :], d["oh"].ap(), writes=[A], key="su2")
        c.op("dve", lambda e: e.tensor_scalar_mul(rb[0:33, :], rb[0:33, :], 8.0), reads=[A], writes=[A])
        c.op("dve", lambda e: e.tensor_tensor(R[0:33, :, :], ohs[0:33, :].unsqueeze(2).to_broadcast([33, 384, 4]),
                                              rb[0:33, :].unsqueeze(1).to_broadcast([33, 384, 4]), ALU.mult),
             reads=[A], writes=[A])
        Rf = R[0:33, :, :].rearrange("p a b -> p (a b)")
        for i in range(3):
            ps = self.psb(i)
            c.op("pe", lambda e, i=i, ps=ps: e.matmul(ps[:, :], lhsT=self.cf("onesf")[0:33, :], rhs=Rf[:, i * 512:(i + 1) * 512],
                                                      start=True, stop=True), reads=[A, self.CT], writes=[self.pst[i]])
            c.op("act", lambda e, i=i, ps=ps: e.copy(Frep[:, i * 512:(i + 1) * 512], ps[:, :]), reads=[self.pst[i]], writes=[A])
        SC = Tok()
        c.dma("sp", self.scr.ap(), Frep, reads=[A], writes=[SC], key="su3")
        c.dma("sp", Tt.rearrange("p a b -> p (a b)"), bass.AP(self.scr, 127 * 4, [[1532, 128], [1, 1024]]), reads=[SC], writes=[A], key="su4")

        def hilo(dst, src_ap, shape_desc, tmp_ap):
            c.op("dve", lambda e, dst=dst, src_ap=src_ap: e.tensor_copy(dst[0], src_ap), reads=[A], writes=[A])
            c.op("dve", lambda e, dst=dst, src_ap=src_ap, tmp_ap=tmp_ap: e.tensor_tensor(tmp_ap, src_ap, dst[0], ALU.subtract), reads=[A], writes=[A])
            c.op("dve", lambda e, dst=dst, tmp_ap=tmp_ap: e.tensor_copy(dst[1], tmp_ap), reads=[A], writes=[A])

        for blk, j0 in ((0, 128), (1, 0)):
            src = Tt[:, j0:j0 + 128, :].rearrange("k q (g r) -> k g r q", g=2)
            t4 = tmp[:, 0:512].rearrange("k (g r q) -> k g r q", g=2, r=2)
            hilo([self.biasP[0][:, :, blk, :, :], self.biasP[1][:, :, blk, :, :]], src, None, t4)
        c.op("dve", lambda e: e.tensor_scalar(nseq, self.cf("sameS"), -NEGB, NEGB, ALU.mult, ALU.add), reads=[self.CT], writes=[A])
        t4 = tmp[:, 512:1024].rearrange("k (g r q) -> k g r q", g=2, r=2)
        src = Tt[:, 0:128, :].rearrange("k q (g r) -> k g r q", g=2)
        c.op("dve", lambda e, t4=t4, src=src: e.tensor_tensor(t4, src, nseq.unsqueeze(1).unsqueeze(1).to_broadcast([128, 2, 2, 128]), ALU.add),
             reads=[A], writes=[A])
        t4b = tmp[:, 1024:1536].rearrange("k (g r q) -> k g r q", g=2, r=2)
        hilo([self.biasS[0][:, :, :, :], self.biasS[1][:, :, :, :]], t4, None, t4b)
        src = Tt[:, 128:136, :].rearrange("k i (g r) -> k g r i", g=2)
        t5 = tmp[:, 1536:1536 + 32].rearrange("k (g r i) -> k g r i", g=2, r=2)
        c.op("dve", lambda e, t5=t5, src=src: e.tensor_copy(t5, src), reads=[A], writes=[A])
        srcb = t5.unsqueeze(1).to_broadcast([128, 16, 2, 2, 8])
        t6 = tmp[:, 0:512].rearrange("k (b g r i) -> k b g r i", b=16, g=2, r=2)
        hilo([self.biasC[0][:, :, :, :, :], self.biasC[1][:, :, :, :, :]], srcb, None, t6)
        c.release([A])
        self.BT = Tok(const=True)
        self.BT.w = A.w

    def wload(self, src_ap, ncols, nk=8):
        i = self.wi
        self.wi ^= 1
        W, t = self.W[i], self.Wt[i]
        self.c.dma("pool", W[:, 0:nk, 0:ncols], src_ap.rearrange("(k p) n -> p k n", p=128), writes=[t], key="w%d" % i)
        return W, t

    def wspec(self, key, l):
        d = self.d
        return {"swa": (d["w_in"].ap()[l][:, 0:768], 768, 8), "gla": (d["w_in"].ap()[l][:, 768:1680], 912, 8),
                "ssd_x": (d["w_in"].ap()[l][:, 1680:2448], 768, 8), "ssd_z": (d["w_in"].ap()[l][:, 2448:2968], 520, 8),
                "w_out": (d["w_out"].ap()[l], 1024, 8), "w_pg": (d["w_pg"].ap()[l], 1024, 8), "w_pe": (d["w_pe"].ap()[l], 1024, 2)}[key]

    def prefetch(self, key, l):
        src, ncols, nk = self.wspec(key, l)
        self.wq[(key, l)] = self.wload(src, ncols, nk)

    def getw(self, key, l):
        if (key, l) not in self.wq:
            self.prefetch(key, l)
        return self.wq.pop((key, l))

    def load_params(self, l):
        c, d = self.c, self.d
        P = self.PT_
        ncq = dict(nc_ok=True)
        c.dma("sp", self.nwT[:, :], d["norm_w"].ap()[l].rearrange("(k p) -> p k", p=128), writes=[P], key="pa", **ncq)
        for t in range(4):
            c.dma("sp", self.cwT[:, :, t], d["conv_w"].ap()[l][t].rearrange("(k p) -> p k", p=128), writes=[P], key="pa", **ncq)
        c.dma("sp", self.cbT[:, :], d["conv_b"].ap()[l].rearrange("(k p) -> p k", p=128), writes=[P], key="pa", **ncq)

        def bc(name, n):
            t = d[name]
            return bass.AP(t, l * n, [[0, 128], [1, n]])

        c.dma("sp", self.rep8[:, :], bc("rep8", 24), writes=[P], key="pa")
        c.dma("sp", self.ssdnw[:, :], bc("ssd_norm_w", 512), writes=[P], key="pa")
        c.dma("sp", self.hdw[:, :, :].rearrange("p a b -> p (a b)"), bc("hdw", 448), writes=[P], key="pa")
        c.dma("sp", self.esink[:, :], bc("sinks", 4), writes=[P], key="pa")
        c.dma("sp", self.wgkf[:, :], d["wgk"].ap()[l], writes=[P], key="pa")
        c.op("act", lambda e: e.activation(self.esink[:, :], self.esink[:, :], AF.Exp), reads=[P], writes=[P])
        c.op("act", lambda e: e.activation(self.arep[:, :], self.rep8[:, 8:16], AF.Exp), reads=[P], writes=[P])
        c.op("dve", lambda e: e.tensor_scalar_mul(self.arep[:, :], self.arep[:, :], -1.0), reads=[P], writes=[P])
        c.op("dve", lambda e: e.tensor_copy(self.wgk[:, :], self.wgkf[:, :]), reads=[P], writes=[P])

    def unit(self, l, hf):
        c, d = self.c, self.d
        row0, NT, T, has_s = [(0, 1024, 8, False), (1024, 1152, 9, True)][hf]
        self.l, self.hf, self.NT, self.T, self.has_s, self.row0 = l, hf, NT, T, has_s, row0
        groups = [(g * 512, min(NT, (g + 1) * 512)) for g in range((NT + 511) // 512)]
        self.groups = groups
        mk = lambda nm: self.marks.append((l, hf, nm, len(c.ops["pe"]), len(c.ops["dve"]), len(c.ops["act"])))
        self.mk = mk
        mk("start")
        if ("swa", l) not in self.wq:
            self.prefetch("swa", l)
        self.load_params(l)
        if l == 0:
            self.load_x()
        mk("rmsnorm")
        self.rmsnorm()
        self.ytok = self.av(0, [128, 9, 512], BF16)
        self.t_ytok = c.new_tok()
        MOFF = 9216
        if not (ENABLE["swa"] and ENABLE["gla"]):
            c.op("pool", lambda e: e.memset(self.ytok[:, :, :], 0.0), writes=[self.t_ytok])
        if ENABLE["swa"]:
            mk("swa")
            self.swa(MOFF)
            if self.trunc:
                return
        if ENABLE["gla"]:
            mk("gla")
            self.gla(MOFF)
        if ENABLE["ssd"]:
            mk("ssd")
            self.ssd(MOFF)
        else:
            c.op("pool", lambda e: e.memset(self.uT[:, 0:4, 0:NT], 0.0), writes=self.t_uTg)
        mk("ytr")
        if ENABLE["ssd"]:
            c.release([self.t_ytok])
        else:
            self.y_transposes()
        mk("out_proj")
        self.out_proj(MOFF)
        mk("ple")
        self.ple(MOFF)
        if l == 1:
            mk("store_y")
            self.store_y(MOFF)
        mk("end")

    def load_x(self):
        c, d = self.c, self.d
        self.t_hT = [[c.new_tok() for _ in range(3)] for _ in range(8)]
        xs = [self.av(i * 4096, [128, 1024], F32) for i in range(2)]
        xt = [c.new_tok() for _ in range(2)]
        for j in range(self.T):
            s = j % 2
            g = j // 4
            c.dma("sp", xs[s], d["x"].ap()[self.row0 + j * 128:self.row0 + (j + 1) * 128, :], writes=[xt[s]], key="xs%d" % s)
            for half in range(2):
                bank = half
                ps = self.psb(bank)
                for kk in range(4):
                    k = half * 4 + kk
                    c.op("pe", lambda e, ps=ps, kk=kk, k=k, s=s: e.transpose(ps[:, kk * 128:(kk + 1) * 128], xs[s][:, k * 128:(k + 1) * 128],
                                                                              self.cf("identf")), reads=[xt[s], self.CT], writes=[self.pst[bank]])
                dst = self.hT[:, half * 4:(half + 1) * 4, j * 128:(j + 1) * 128]
                src = ps[:, :].rearrange("p (a b) -> p a b", a=4)
                eng = "act" if half == 0 else "dve"
                if eng == "act":
                    c.op("act", lambda e, dst=dst, src=src: e.copy(dst, src), reads=[self.pst[bank]], writes=[self.t_hT[k][g] for k in range(half * 4, half * 4 + 4)])
                else:
                    c.op("dve", lambda e, dst=dst, src=src: e.tensor_copy(dst, src), reads=[self.pst[bank]], writes=[self.t_hT[k][g] for k in range(half * 4, half * 4 + 4)])
        c.release(xt)

    def rmsnorm(self):
        c = self.c
        if hasattr(self, "t_uTg"):
            c.release(self.t_uTg)
        self.t_uTg = [c.new_tok() for _ in self.groups]
        sq = [self.av(i * 8192, [128, 8, 512], BF16) for i in range(2)]
        rs = [self.av(16384 + i * 2048, [128, 512], F32) for i in range(2)]
        tq = [c.new_tok() for _ in range(2)]
        tr = [c.new_tok() for _ in range(2)]
        for gi, (c0, c1) in enumerate(self.groups):
            n = c1 - c0
            s = gi % 2
            c.op("act", lambda e, s=s, c0=c0, c1=c1, n=n: e.activation(sq[s][:, :, 0:n], self.hT[:, :, c0:c1], AF.Square),
                 reads=[self.t_hT[k][gi] for k in range(8)], writes=[tq[s]])
            ps = self.psb(2)
            for k in range(8):
                c.op("pe", lambda e, k=k, s=s, n=n, ps=ps: e.matmul(ps[:, 0:n], lhsT=self.cb("ones"), rhs=sq[s][:, k, 0:n], start=(k == 0), stop=(k == 7)),
                     reads=[tq[s], self.CT], writes=[self.pst[2]])
            c.op("act", lambda e, s=s, n=n, ps=ps: e.activation(rs[s][:, 0:n], ps[:, 0:n], AF.Ln, bias=EPS, scale=1.0 / 1024), reads=[self.pst[2]], writes=[tr[s]])
            c.op("act", lambda e, s=s, n=n: e.activation(rs[s][:, 0:n], rs[s][:, 0:n], AF.Exp, scale=-0.5), reads=[tr[s]], writes=[tr[s]])
            for k in range(8):
                eng = "dve"
                c.op(eng, lambda e, k=k, s=s, n=n, c0=c0, c1=c1: e.scalar_tensor_tensor(self.uT[:, k, c0:c1], self.hT[:, k, c0:c1], self.nwT[:, k:k + 1],
                                                                                         rs[s][:, 0:n], ALU.mult, ALU.mult),
                     reads=[self.t_hT[k][gi], tr[s], self.PT_], writes=[self.t_uTg[gi]])
        c.release(tq + tr)

    def proj_fm(self, W, wt, j0, M, xin, xin_toks, evac, nk=8, banks=(0, 1)):
        c = self.c
        for gi, (c0, c1) in enumerate(self.groups):
            n = c1 - c0
            bank = banks[self._rr % len(banks)]
            self._rr += 1
            ps = self.psb(bank)
            for k in range(nk):
                c.op("pe", lambda e, k=k, ps=ps, n=n, c0=c0, c1=c1: e.matmul(ps[0:M, 0:n], lhsT=W[:, k, j0:j0 + M], rhs=xin[:, k, c0:c1],
                                                                             start=(k == 0), stop=(k == nk - 1)),
                     reads=[wt] + xin_toks(gi), writes=[self.pst[bank]])
            evac(ps[0:M, 0:n], gi, c0, c1, self.pst[bank])

    def proj_tm(self, W, wt, j0, N, evac, tiles=None, banks=(0, 1)):
        c = self.c
        for j in (tiles if tiles is not None else range(self.T)):
            bank = banks[self._rr % len(banks)]
            self._rr += 1
            ps = self.psb(bank)
            for k in range(8):
                c.op("pe", lambda e, k=k, ps=ps, j=j: e.matmul(ps[:, 0:N], lhsT=self.uT[:, k, j * 128:(j + 1) * 128], rhs=W[:, k, j0:j0 + N],
                                                               start=(k == 0), stop=(k == 7)),
                     reads=[wt, self.t_uTg[j // 4]], writes=[self.pst[bank]])
            evac(ps[:, 0:N], j, self.pst[bank])

    _rr = 0

    def pipeline_fine(self, gens):
        import os
        mode = os.environ.get("PMODE", "fine")
        prev = None
        for g in list(gens) + [None]:
            a_done = g is None
            b_done = prev is None
            if mode == "newest":
                kmax = int(os.environ.get("KMAX", "99"))
                kk_ = 0
                while not a_done:
                    try:
                        if next(g) == "S":
                            a_done = True
                    except StopIteration:
                        a_done = True
                        g = None
                    kk_ += 1
                    if prev is not None and kk_ >= kmax:
                        a_done = True
                        g = None
            while not (a_done and b_done):
                if not a_done:
                    try:
                        if next(g) == "S":
                            a_done = True
                    except StopIteration:
                        a_done = True
                        g = None
                if not b_done:
                    try:
                        next(prev)
                    except StopIteration:
                        b_done = True
            prev = g

    def keepwarm(self, bank, n):
        c = self.c
        for _w in range(n):
            c.op("pe", lambda e: e.matmul(self.psb(bank)[:, :], lhsT=self.cb("ident"), rhs=self.biasP[0][:, 0, :, :, :].rearrange("p b r q -> p (b r q)"), start=True, stop=True),
                 reads=[self.BT, self.CT], writes=[self.pst[bank]])

    def pipeline_sched(self, gens, sched):
        gens = list(gens)
        n = len(gens)
        maxlag = max(l for _, l in sched)
        pos = [0] * n
        for r in range(n + maxlag):
            for seg, lag in sched:
                t = r - lag
                if 0 <= t < n:
                    assert pos[t] == seg, (t, pos[t], seg)
                    try:
                        next(gens[t])
                    except StopIteration:
                        pass
                    pos[t] += 1

    def pipeline_rr(self, gens, nstage):
        gens = list(gens)
        n = len(gens)
        done = [False] * n
        for r in range(n + nstage - 1):
            act = [t for t in range(r - nstage + 1, r + 1) if 0 <= t < n and not done[t]]
            atb = {t: False for t in act}
            while not all(atb.values()):
                for t in act:
                    if atb[t]:
                        continue
                    try:
                        if next(gens[t]) == "S":
                            atb[t] = True
                    except StopIteration:
                        atb[t] = True
                        done[t] = True

    def pipeline(self, gens, on=True):
        if not on:
            for g in gens:
                for _ in g:
                    pass
            return
        active = []
        it = iter(gens)
        while True:
            nxt = next(it, None)
            if nxt is not None:
                active.append(nxt)
            if not active:
                break
            for g in list(active):
                try:
                    next(g)
                except StopIteration:
                    active.remove(g)

    def swa(self, MOFF):
        c, d, l, T, NT = self.c, self.d, self.l, self.T, self.NT
        st = self.S[l]
        o = MOFF
        qkv = self.av(o, [128, 9, 512], F32); o += 18432
        ssg = self.av(o, [128, 9, 256], BF16); o += 4608
        tmp = self.av(o, [128, 3, 6, 64], F32); o += 4608
        rst = self.av(o, [128, 3, 6], F32); o += 128
        qkn = self.av(o, [128, 9, 6, 64], BF16); o += 6912
        kn32 = self.av(o, [128, 2, 128], F32); o += 1024
        QT = self.av(o, [128, 2, 2, 128], BF16); o += 1024
        PT = self.av(o, [128, 2, 2, 2, 256], BF16); o += 4096
        ytmp = self.av(o, [128, 2, 4, 64], F32); o += 2048
        den = self.av(o, [128, 2, 8], F32); o += 64
        Kc = self.av(o, [128, 16, 128], BF16); o += 4096
        Vc = self.av(o, [128, 16, 2, 65], BF16); o += 4160
        KcTz = self.av(o, [128, 16, 2, 128], BF16); o += 8192
        PTc = self.av(o, [128, 16, 2, 16], BF16); o += 1024
        oTs = self.av(o, [128, 4, 128], F32); o += 2048
        t_qkv = [c.new_tok() for _ in range(T)]
        t_ssg = [c.new_tok() for _ in range(T)]
        t_tmp, t_rst, t_kn32 = c.new_tok(), c.new_tok(), c.new_tok()
        t_qkn = [c.new_tok() for _ in range(T)]
        t_QT = [c.new_tok(), c.new_tok()]
        t_PT = [c.new_tok(), c.new_tok()]
        t_ytmp = [c.new_tok(), c.new_tok()]
        t_den = [c.new_tok(), c.new_tok()]
        t_s = c.new_tok()
        alltoks = t_qkv + t_ssg + [t_tmp, t_rst, t_kn32] + t_qkn + t_QT + t_PT + t_ytmp + t_den + [t_s]
        W, wt = self.getw("swa", l)

        def ev_qkv(ps, j, bt):
            c.op("act", lambda e: e.copy(qkv[:, j, :], ps), reads=[bt], writes=[t_qkv[j]])

        def ev_sg(ps, j, bt):
            c.op("act", lambda e: e.activation(ssg[:, j, :], ps, AF.Silu), reads=[bt], writes=[t_ssg[j]])

        self.proj_tm(W, wt, 0, 512, ev_qkv, tiles=list(range(0, min(3, T))))
        if self.has_s:
            c.dma("pool", Kc, d["ck"].ap()[l].rearrange("b k c -> k b c"), writes=[t_s], key="swc")
            c.op("dve", lambda e: e.memset(Vc[:, :, :, 64:65], 1.0), writes=[t_s])
            for g in range(2):
                c.dma("pool", Vc[:, :, g, 0:64], d["cv"].ap()[l][:, :, 64 * g:64 * g + 64].rearrange("b k c -> k b c"), writes=[t_s], key="swc")
        for j0 in range(0, T, 3):
            nj = min(3, T - j0)
            if j0 + 3 < T:
                self.proj_tm(W, wt, 0, 512, ev_qkv, tiles=list(range(j0 + 3, min(j0 + 6, T))))
            else:
                self.proj_tm(W, wt, 512, 256, ev_sg)
            qk = qkv[:, j0:j0 + nj, 0:384].rearrange("p j (h d) -> p j h d", d=64)
            tm = tmp[:, 0:nj, :, :]
            rs = rst[:, 0:nj, :]
            rd = [t_qkv[j] for j in range(j0, j0 + nj)]
            c.op("dve", lambda e, tm=tm, qk=qk: e.tensor_tensor(tm, qk, qk, ALU.mult), reads=rd, writes=[t_tmp])
            c.op("dve", lambda e, tm=tm, rs=rs: e.tensor_reduce(rs, tm, AX.X, ALU.add), reads=[t_tmp], writes=[t_rst])
            c.op("act", lambda e, rs=rs: e.activation(rs, rs, AF.Ln, bias=EPS, scale=1.0 / 64), reads=[t_rst], writes=[t_rst])
            c.op("act", lambda e, rs=rs: e.activation(rs, rs, AF.Exp, scale=-0.5), reads=[t_rst], writes=[t_rst])
            c.op("dve", lambda e, tm=tm, qk=qk, rs=rs, nj=nj: e.tensor_tensor(tm, qk, rs.unsqueeze(3).to_broadcast([128, nj, 6, 64]), ALU.mult),
                 reads=rd + [t_rst], writes=[t_tmp])
            c.op("dve", lambda e, tm=tm, nj=nj, j0=j0: e.tensor_tensor(qkn[:, j0:j0 + nj, :, :], tm, self.hdw[:, 0:6, :].unsqueeze(1).to_broadcast([128, nj, 6, 64]), ALU.mult),
                 reads=[t_tmp, self.PT_], writes=[t_qkn[j] for j in range(j0, j0 + nj)])
            for j in range(j0, j0 + nj):
                slot = None
                if self.hf == 1 and j == 7:
                    slot = 0
                if self.has_s and j == 8:
                    slot = 1
                if slot is not None:
                    c.op("dve", lambda e, j=j, slot=slot, j0=j0: e.tensor_tensor(kn32[:, slot, :].rearrange("p (h d) -> p h d", d=64), tmp[:, j - j0, 4:6, :],
                                                                                   self.hdw[:, 4:6, :], ALU.mult), reads=[t_tmp, self.PT_], writes=[t_kn32])
        self.prefetch("gla", l)
        def body(j):
            is_s = self.has_s and j == 8
            gt = self.hf * 8 + j
            par = gt % 3
            par4 = gt % 4
            sl = j % 2
            first = (gt == 0) or is_s
            tp = self.psb(2, BF16)[:, 0:384].rearrange("p (a b) -> p a b", a=3)
            for a in range(3):
                src = qkn[:, j, 2 * a:2 * a + 2, :].rearrange("p h d -> p (h d)")
                c.op("pe", lambda e, a=a, src=src, tp=tp: e.transpose(tp[:, a, :], src, self.cb("ident")), reads=[t_qkn[j], self.CT], writes=[self.pst[2]])
            yield
            import os
            SKIP = os.environ.get("SKIP", "") if j >= 1 else ""
            if "qt" not in SKIP:
                c.op("act", lambda e, sl=sl, tp=tp: e.copy(QT[:, sl, :, :], tp[:, 0:2, :]), reads=[self.pst[2]], writes=[t_QT[sl]])
            KTz, vext, tkv, tvx = st["KTz"][par], st["vext"][par4], st["t_kv"][par], st["t_vx"][par4]
            if "ktz" not in SKIP:
              c.op("dve", lambda e, tp=tp, KTz=KTz: e.tensor_tensor(KTz[:, :, :], tp[:, 2:3, :].to_broadcast([128, 2, 128]),
                                                                  self.cb("bd2").unsqueeze(2).to_broadcast([128, 2, 128]), ALU.mult),
                 reads=[self.pst[2], self.CT], writes=[tkv])
            if "vext" not in SKIP:
              c.op("pool", lambda e, vext=vext, j=j: e.tensor_copy(vext[:, :, 0:64], qkv[:, j, 384:512].rearrange("p (g d) -> p g d", g=2)),
                 reads=[t_qkv[j]], writes=[tvx])
            yield "S"
            if not is_s:
                for _w in range(2):
                    c.op("pe", lambda e: e.matmul(self.psb(0)[:, :], lhsT=self.cb("ident"), rhs=self.biasP[0][:, 0, :, :, :].rearrange("p b r q -> p (b r q)"), start=True, stop=True),
                         reads=[self.BT, self.CT], writes=[self.pst[0]])
            blks = [1] if first else [0, 1]
            for g in range(2):
                yield
                bank = 4 + g
                sc = self.psb(bank).rearrange("p (b n) -> p b n", b=2)
                b0 = blks[0]
                scf = sc[:, b0:2, :].rearrange("p b n -> p (b n)")
                if is_s:
                    bh = self.biasS[0][:, g, :, :].rearrange("p r q -> p (r q)")
                    bl = self.biasS[1][:, g, :, :].rearrange("p r q -> p (r q)")
                else:
                    bh = self.biasP[0][:, g, b0:2, :, :].rearrange("p b r q -> p (b r q)")
                    bl = self.biasP[1][:, g, b0:2, :, :].rearrange("p b r q -> p (b r q)")
                c.op("pe", lambda e, scf=scf, bh=bh: e.matmul(scf, lhsT=self.cb("ident"), rhs=bh, start=True, stop=False), reads=[self.BT, self.CT], writes=[self.pst[bank]])
                c.op("pe", lambda e, scf=scf, bl=bl: e.matmul(scf, lhsT=self.cb("ident"), rhs=bl, start=False, stop=False), reads=[self.BT, self.CT], writes=[self.pst[bank]])
                for blk in blks:
                    kpar = par if blk == 1 else (par + 2) % 3
                    KT_ = st["KTz"][kpar]
                    tk_ = st["t_kv"][kpar]
                    lastb = (blk == blks[-1])
                    c.op("pe", lambda e, sc=sc, blk=blk, KT_=KT_, g=g, sl=sl, lastb=lastb: e.matmul(sc[:, blk, :], lhsT=KT_[:, g, :], rhs=QT[:, sl, :, :].rearrange("p r q -> p (r q)"),
                                                                                       start=False, stop=lastb), reads=[tk_, t_QT[sl]], writes=[self.pst[bank]])
                b0 = blks[0]
                c.op("act", lambda e, sc=sc, b0=b0, sl=sl, g=g: e.activation(PT[:, sl, g, b0:2, :], sc[:, b0:2, :], AF.Exp, scale=0.125),
                     reads=[self.pst[bank]], writes=[t_PT[sl]])
            yield "S"
            if not is_s:
                for _w in range(2):
                    c.op("pe", lambda e: e.matmul(self.psb(0)[:, :], lhsT=self.cb("ident"), rhs=self.biasP[0][:, 1, :, :, :].rearrange("p b r q -> p (b r q)"), start=True, stop=True),
                         reads=[self.BT, self.CT], writes=[self.pst[0]])
                oe = self.psb(6).rearrange("p (h n) -> p h n", h=4)
                for h in range(4):
                    yield
                    g, r = h // 2, h % 2
                    for blk in blks:
                        kpar = par4 if blk == 1 else (par4 + 3) % 4
                        f0, f1 = (blk == blks[0]), (blk == blks[-1])
                        c.op("pe", lambda e, oe=oe, h=h, g=g, r=r, blk=blk, kpar=kpar, sl=sl, f0=f0, f1=f1: e.matmul(oe[:, h, 0:65], lhsT=PT[:, sl, g, blk, r * 128:(r + 1) * 128],
                                                                                                     rhs=st["vext"][kpar][:, g, :], start=f0, stop=f1),
                             reads=[t_PT[sl], st["t_vx"][kpar]], writes=[self.pst[6]])
                self.swa_finish(oe, j, sl, ssg, t_ssg, ytmp, t_ytmp, den, t_den)
            else:
                self.swa_sample(j, sl, QT, t_QT, PT, t_PT, Kc, Vc, KcTz, PTc, oTs, t_s, vext, tvx, ssg, t_ssg, ytmp, t_ytmp, den, t_den)
            if self.hf == 1 and j == 7:
                c.dma("sp", self.o["swak_p"].ap()[l], kn32[:, 0, :], reads=[t_kn32], key="ok")
                c.dma("sp", self.o["swav_p"].ap()[l], qkv[:, 7, 384:512], reads=[t_qkv[7]], key="ok")
            if is_s:
                c.dma("sp", self.o["swak_s"].ap()[l][:, 0:120, :], d["ck"].ap()[l][:, 8:128, :], key="ok")
                c.dma("sp", self.o["swav_s"].ap()[l][:, 0:120, :], d["cv"].ap()[l][:, 8:128, :], key="ok")
                for b in range(16):
                    c.dma("sp", self.o["swak_s"].ap()[l][b, 120:128, :], kn32[8 * b:8 * b + 8, 1, :], reads=[t_kn32], key="ok")
                    c.dma("sp", self.o["swav_s"].ap()[l][b, 120:128, :], qkv[8 * b:8 * b + 8, 8, 384:512], reads=[t_qkv[8]], key="ok")
        self.pipeline_rr([body(j) for j in range(T if not self.trunc else int(self.trunc))], 3)
        c.release(alltoks)

    def swa_finish(self, oe, j, sl, ssg, t_ssg, ytmp, t_ytmp, den, t_den):
        c = self.c
        c.op("dve", lambda e: e.tensor_tensor(den[:, sl, 0:4], oe[:, :, 64], self.esink[:, :], ALU.add), reads=[self.pst[6], self.PT_], writes=[t_den[sl]])
        c.op("dve", lambda e: e.reciprocal(den[:, sl, 4:8], den[:, sl, 0:4]), reads=[t_den[sl]], writes=[t_den[sl]])
        c.op("dve", lambda e: e.tensor_tensor(ytmp[:, sl, :, :], oe[:, :, 0:64], den[:, sl, 4:8].unsqueeze(2).to_broadcast([128, 4, 64]), ALU.mult),
             reads=[self.pst[6], t_den[sl]], writes=[t_ytmp[sl]])
        c.op("dve", lambda e: e.tensor_tensor(self.ytok[:, j, 256:512], ytmp[:, sl, :, :].rearrange("p h d -> p (h d)"), ssg[:, j, :], ALU.mult),
             reads=[t_ytmp[sl], t_ssg[j]], writes=[self.t_ytok])

    def swa_sample(self, j, sl, QT, t_QT, PT, t_PT, Kc, Vc, KcTz, PTc, oTs, t_s, vext, tkv, ssg, t_ssg, ytmp, t_ytmp, den, t_den):
        c = self.c
        for b4 in range(4):
            tp = self.psb(3, BF16)[:, 0:512].rearrange("p (a b) -> p a b", a=4)
            for bb in range(4):
                b = b4 * 4 + bb
                c.op("pe", lambda e, tp=tp, bb=bb, b=b: e.transpose(tp[:, bb, :], Kc[:, b, :], self.cb("ident")), reads=[t_s, self.CT], writes=[self.pst[3]])
            c.op("dve", lambda e, tp=tp, b4=b4: e.tensor_tensor(KcTz[:, b4 * 4:b4 * 4 + 4, :, :], tp.unsqueeze(2).to_broadcast([128, 4, 2, 128]),
                                                                self.cb("bd2").unsqueeze(1).unsqueeze(3).to_broadcast([128, 4, 2, 128]), ALU.mult),
                 reads=[self.pst[3], self.CT], writes=[t_s])
        scc = self.psb(7).rearrange("p (b g n) -> p b g n", b=16, g=2)
        sccf = self.psb(7)
        c.op("pe", lambda e: e.matmul(sccf[:, :], lhsT=self.cb("ident"), rhs=self.biasC[0][:, :, :, :, :].rearrange("p b g r i -> p (b g r i)"), start=True, stop=False),
             reads=[self.BT, self.CT], writes=[self.pst[7]])
        c.op("pe", lambda e: e.matmul(sccf[:, :], lhsT=self.cb("ident"), rhs=self.biasC[1][:, :, :, :, :].rearrange("p b g r i -> p (b g r i)"), start=False, stop=False),
             reads=[self.BT, self.CT], writes=[self.pst[7]])
        for b in range(16):
            for g in range(2):
                c.op("pe", lambda e, b=b, g=g: e.matmul(scc[:, b, g, :], lhsT=KcTz[:, b, g, :], rhs=QT[:, sl, :, 8 * b:8 * b + 8], start=False, stop=(b == 15 and g == 1)),
                     reads=[t_s, t_QT[sl]], writes=[self.pst[7]])
        c.op("act", lambda e: e.activation(PTc[:, :, :, :].rearrange("p b g n -> p (b g n)"), sccf[:, :], AF.Exp, scale=0.125), reads=[self.pst[7]], writes=[t_s])
        oT = self.psb(6).rearrange("p (h n) -> p h n", h=4)
        for h in range(4):
            g, r = h // 2, h % 2
            c.op("pe", lambda e, h=h, g=g, r=r: e.matmul(oT[0:65, h, :], lhsT=vext[:, g, :], rhs=PT[:, sl, g, 1, r * 128:(r + 1) * 128], start=True, stop=False),
                 reads=[tkv, t_PT[sl]], writes=[self.pst[6]])
            for b in range(16):
                c.op("pe", lambda e, h=h, g=g, r=r, b=b: e.matmul(oT[0:65, h, 8 * b:8 * b + 8], lhsT=Vc[:, b, g, :], rhs=PTc[:, b, g, r * 8:(r + 1) * 8], start=False, stop=True),
                     reads=[t_s], writes=[self.pst[6]])
        c.op("act", lambda e: e.copy(oTs[0:65, :, :], oT[0:65, :, :]), reads=[self.pst[6]], writes=[t_s])
        oe = self.psb(5).rearrange("p (h n) -> p h n", h=4)
        for h in range(4):
            c.op("pe", lambda e, h=h: e.transpose(oe[:, h, 0:65], oTs[0:65, h, :], self.cf("identf")[0:65, 0:65]), reads=[t_s, self.CT], writes=[self.pst[5]])
        c.op("dve", lambda e: e.tensor_tensor(den[:, sl, 0:4], oe[:, :, 64], self.esink[:, :], ALU.add), reads=[self.pst[5], self.PT_], writes=[t_den[sl]])
        c.op("dve", lambda e: e.reciprocal(den[:, sl, 4:8], den[:, sl, 0:4]), reads=[t_den[sl]], writes=[t_den[sl]])
        c.op("dve", lambda e: e.tensor_tensor(ytmp[:, sl, :, :], oe[:, :, 0:64], den[:, sl, 4:8].unsqueeze(2).to_broadcast([128, 4, 64]), ALU.mult),
             reads=[self.pst[5], t_den[sl]], writes=[t_ytmp[sl]])
        c.op("dve", lambda e: e.tensor_tensor(self.ytok[:, j, 256:512], ytmp[:, sl, :, :].rearrange("p h d -> p (h d)"), ssg[:, j, :], ALU.mult),
             reads=[t_ytmp[sl], t_ssg[j]], writes=[self.t_ytok])

    def gla(self, MOFF):
        c, d, l, T, NT = self.c, self.d, self.l, self.T, self.NT
        st = self.S[l]
        o = MOFF
        gqT = self.av(o, [128, 1152], BF16); o += 2304
        gkT = self.av(o, [128, 1152], BF16); o += 2304
        glrT = self.av(o, [128, 1152], BF16); o += 2304
        gtok = self.av(o, [128, 9, 384], BF16); o += 6912
        sgg = self.av(o, [128, 9, 256], BF16); o += 4608
        sp = self.av(o, [128, 2, 128], F32); o += 1024
        ebT = self.av(o, [128, 2, 2, 128], F32); o += 2048
        erc = self.av(o, [128, 2, 128], F32); o += 1024
        qeT = self.av(o, [128, 2, 128], BF16); o += 512
        keT = self.av(o, [128, 2, 128], BF16); o += 512
        kd = self.av(o, [128, 2, 128], BF16); o += 512
        qebd = self.av(o, [128, 2, 4, 128], BF16); o += 2048
        attm = self.av(o, [128, 2, 4, 128], BF16); o += 2048
        oss = self.av(o, [128, 2, 8], F32); o += 64
        otmp = self.av(o, [128, 2, 4, 64], F32); o += 2048
        um = self.av(o, [128, 4, 64], F32); o += 1024
        qeTm = self.av(o, [128, 16, 128], BF16); o += 4096
        kdm = self.av(o, [128, 16, 128], BF16); o += 4096
        Sg0 = self.av(o, [128, 16, 256], F32); o += 16384
        Sg0b = self.av(o, [128, 16, 256], BF16); o += 8192
        t_gq = [c.new_tok() for _ in self.groups]
        t_gk = [c.new_tok() for _ in self.groups]
        t_glr = [c.new_tok() for _ in self.groups]
        t_gtok = [c.new_tok() for _ in range(T)]
        t_sgg = c.new_tok()
        t_sp = [c.new_tok(), c.new_tok()]
        t_eb = [c.new_tok(), c.new_tok()]
        t_q = [c.new_tok(), c.new_tok()]
        t_att = [c.new_tok(), c.new_tok()]
        t_o = [c.new_tok(), c.new_tok()]
        t_um = c.new_tok()
        t_s = c.new_tok()
        alltoks = t_gq + t_gk + t_glr + t_gtok + [t_sgg] + t_sp + t_eb + t_q + t_att + t_o + [t_um, t_s]
        W, wt = self.getw("gla", l)
        c.op("pool", lambda e: e.memset(glrT[0:32, :], 1.0), writes=t_glr)

        def ev_fm(dst, toks):
            def f(ps, gi, c0, c1, bt):
                M = ps.shape[0]
                c.op("act", lambda e: e.copy(dst[0:M, c0:c1], ps), reads=[bt], writes=[toks[gi]])
            return f

        def xt(gi):
            return [self.t_uTg[gi]]

        self.proj_fm(W, wt, 0, 128, self.uT, xt, ev_fm(gqT, t_gq))
        self.proj_fm(W, wt, 128, 128, self.uT, xt, ev_fm(gkT, t_gk))
        self.proj_fm(W, wt, 256, 16, self.uT, xt, ev_fm(glrT, t_glr))

        def ev_kv(ps, j, bt):
            c.op("dve", lambda e: e.tensor_copy(gtok[:, j, :], ps), reads=[bt], writes=[t_gtok[j]])

        def ev_gg(ps, j, bt):
            c.op("act", lambda e: e.activation(sgg[:, j, :], ps, AF.Silu), reads=[bt], writes=[t_sgg])

        self.proj_tm(W, wt, 272, 384, ev_kv)
        self.proj_tm(W, wt, 656, 256, ev_gg)
        c.op("dve", lambda e: e.tensor_tensor(sgg[:, 0:T, :].rearrange("p j (h d) -> p j h d", d=64), sgg[:, 0:T, :].rearrange("p j (h d) -> p j h d", d=64),
                                              self.hdw[:, 6:7, :].unsqueeze(1).to_broadcast([128, T, 4, 64]), ALU.mult), reads=[t_sgg, self.PT_], writes=[t_sgg])
        if self.has_s:
            c.op("pool", lambda e: e.memset(Sg0[:, :, :], 0.0), writes=[t_s])
            for h in range(4):
                c.dma("sp", Sg0[32 * h:32 * h + 32, :, 64 * h:64 * h + 64], d["st_gla"].ap()[l][:, h, :, :].rearrange("b d v -> d b v"), writes=[t_s], key="gls")
            c.op("act", lambda e: e.copy(Sg0b[:, :, :], Sg0[:, :, :]), reads=[t_s], writes=[t_s])
        self.prefetch("ssd_x", l)
        self.prefetch("ssd_z", l)
        def body(j):
            is_s = self.has_s and j == 8
            sl = j % 2
            gi = j // 4
            cs = slice(j * 128, (j + 1) * 128)
            tri = self.cf("triS") if is_s else self.cf("triP")
            sup = self.cf("supS") if is_s else self.cf("supP")
            m01 = self.cb("triS") if is_s else self.cb("triP")
            ps2 = self.psb(2)
            c.op("pe", lambda e, cs=cs: e.matmul(ps2[:, 0:128], lhsT=glrT[0:17, cs], rhs=self.wgk[0:17, :], start=True, stop=True),
                 reads=[t_glr[gi], self.PT_], writes=[self.pst[2]])
            yield
            c.op("act", lambda e, sl=sl: e.activation(sp[:, sl, :], ps2[:, 0:128], AF.Exp, scale=-1.0), reads=[self.pst[2]], writes=[t_sp[sl]])
            yield
            c.op("act", lambda e, sl=sl: e.activation(sp[:, sl, :], sp[:, sl, :], AF.Ln, bias=1.0), reads=[t_sp[sl]], writes=[t_sp[sl]])
            yield
            ps3 = self.psb(3)
            c.op("pe", lambda e, sl=sl, sup=sup: e.matmul(ps3[:, 128:256], lhsT=sup, rhs=sp[:, sl, :], start=True, stop=True), reads=[t_sp[sl], self.CT], writes=[self.pst[3]])
            yield
            c.op("pe", lambda e, sl=sl, tri=tri: e.matmul(ps3[:, 256:384], lhsT=sp[:, sl, :], rhs=tri, start=True, stop=True), reads=[t_sp[sl], self.CT], writes=[self.pst[3]])
            yield
            c.op("act", lambda e, sl=sl: e.activation(ebT[:, sl, 0, :], ps3[:, 256:384], AF.Exp, scale=-1.0 / 16), reads=[self.pst[3]], writes=[t_eb[sl]])
            yield
            c.op("act", lambda e, sl=sl: e.activation(ebT[:, sl, 1, :], ps3[:, 256:384], AF.Exp, scale=1.0 / 16), reads=[self.pst[3]], writes=[t_eb[sl]])
            yield
            c.op("act", lambda e, sl=sl: e.activation(erc[:, sl, :], ps3[:, 128:256], AF.Exp, scale=-1.0 / 16), reads=[self.pst[3]], writes=[t_eb[sl]])
            yield
            c.op("dve", lambda e, sl=sl, cs=cs: e.scalar_tensor_tensor(qeT[:, sl, :], gqT[:, cs], 32.0 ** -0.5, ebT[:, sl, 0, :], ALU.mult, ALU.mult),
                 reads=[t_gq[gi], t_eb[sl]], writes=[t_q[sl]])
            yield
            c.op("dve", lambda e, sl=sl, cs=cs: e.tensor_tensor(keT[:, sl, :], gkT[:, cs], ebT[:, sl, 1, :], ALU.mult), reads=[t_gk[gi], t_eb[sl]], writes=[t_q[sl]])
            yield
            c.op("pool", lambda e, sl=sl, j=j: e.tensor_tensor(kd[:, sl, :], gtok[:, j, 0:128], erc[:, sl, :], ALU.mult), reads=[t_gtok[j], t_eb[sl]], writes=[t_q[sl]])
            yield
            c.op("pool", lambda e, sl=sl: e.tensor_tensor(qebd[:, sl, :, :], qeT[:, sl, :].unsqueeze(1).to_broadcast([128, 4, 128]),
                                                         self.cb("bd4").unsqueeze(2).to_broadcast([128, 4, 128]), ALU.mult), reads=[t_q[sl], self.CT], writes=[t_q[sl]])
            yield
            yield "S"
            if not is_s:
                self.keepwarm(1, 3)
            ps4 = self.psb(4)
            c.op("pe", lambda e, sl=sl: e.matmul(ps4[:, :], lhsT=keT[:, sl, :], rhs=qebd[:, sl, :, :].rearrange("p h t -> p (h t)"), start=True, stop=True),
                 reads=[t_q[sl]], writes=[self.pst[4]])
            yield
            c.op("dve", lambda e, sl=sl, m01=m01: e.tensor_tensor(attm[:, sl, :, :], ps4.rearrange("p (h t) -> p h t", h=4), m01.unsqueeze(1).to_broadcast([128, 4, 128]), ALU.mult),
                 reads=[self.pst[4], self.CT], writes=[t_att[sl]])
            yield
            ob = 5 if (j % 2 == 0) else 7
            ps5 = self.psb(ob)
            if not is_s:
                c.op("pe", lambda e, sl=sl: e.matmul(ps5[:, 0:256], lhsT=qeT[:, sl, :], rhs=st["gla_b"][:, :], start=True, stop=False),
                     reads=[t_q[sl], st["t_gla"]], writes=[self.pst[ob]])
            else:
                c.op("dve", lambda e, sl=sl: e.tensor_tensor(qeTm[:, :, :], qeT[:, sl, :].unsqueeze(1).to_broadcast([128, 16, 128]),
                                                             self.cb("smbt").rearrange("p (b t) -> p b t", b=16), ALU.mult), reads=[t_q[sl], self.CT], writes=[t_s])
                for b in range(16):
                    c.op("pe", lambda e, b=b: e.matmul(ps5[:, 0:256], lhsT=qeTm[:, b, :], rhs=Sg0b[:, b, :], start=(b == 0), stop=False), reads=[t_s], writes=[self.pst[ob]])
            for h in range(4):
                c.op("pe", lambda e, sl=sl, h=h, j=j: e.matmul(ps5[:, h * 64:(h + 1) * 64], lhsT=attm[:, sl, h, :], rhs=gtok[:, j, 128 + h * 64:128 + (h + 1) * 64],
                                                               start=False, stop=(h == 3)), reads=[t_att[sl], t_gtok[j]], writes=[self.pst[ob]])
            if not is_s:
                ps6 = self.psb(6)
                c.op("pe", lambda e, sl=sl, j=j: e.matmul(ps6[:, 0:256], lhsT=kd[:, sl, :], rhs=gtok[:, j, 128:384], start=True, stop=True),
                     reads=[t_q[sl], t_gtok[j]], writes=[self.pst[6]])
                c.op("dve", lambda e: e.tensor_tensor(um[:, :, :], ps6[:, 0:256].rearrange("p (h d) -> p h d", h=4), self.cf("bd4").unsqueeze(2).to_broadcast([128, 4, 64]), ALU.mult),
                     reads=[self.pst[6], self.CT], writes=[t_um])
                c.op("dve", lambda e, sl=sl: e.scalar_tensor_tensor(st["gla"][:, :], st["gla"][:, :], ebT[:, sl, 0, 127:128], um[:, :, :].rearrange("p h d -> p (h d)"), ALU.mult, ALU.add),
                     reads=[t_um, t_eb[sl], st["t_gla"]], writes=[st["t_gla"]])
                c.op("act", lambda e: e.copy(st["gla_b"][:, :], st["gla"][:, :]), reads=[st["t_gla"]], writes=[st["t_gla"]])
                if self.hf == 1 and j == 7:
                    for h in range(4):
                        c.dma("sp", self.o["gla_p"].ap()[l][h], st["gla"][32 * h:32 * h + 32, 64 * h:64 * h + 64], reads=[st["t_gla"]], key="og")
            else:
                c.op("dve", lambda e, sl=sl: e.tensor_tensor(kdm[:, :, :], kd[:, sl, :].unsqueeze(1).to_broadcast([128, 16, 128]),
                                                             self.cb("smtok").unsqueeze(2).to_broadcast([128, 16, 128]), ALU.mult), reads=[t_q[sl], self.CT], writes=[t_s])
                c.op("dve", lambda e, sl=sl: e.tensor_tensor(Sg0[:, :, :], Sg0[:, :, :], ebT[:, sl, 0, 7:128:8].unsqueeze(2).to_broadcast([128, 16, 256]), ALU.mult),
                     reads=[t_s, t_eb[sl]], writes=[t_s])
                for b2 in range(8):
                    bank = 6 if (b2 % 2 == 0) else 0
                    psu = self.psb(bank)
                    for bb in range(2):
                        b = b2 * 2 + bb
                        c.op("pe", lambda e, b=b, bb=bb, psu=psu, j=j: e.matmul(psu[:, bb * 256:(bb + 1) * 256], lhsT=kdm[:, b, :], rhs=gtok[:, j, 128:384], start=True, stop=True),
                             reads=[t_s, t_gtok[j]], writes=[self.pst[bank]])
                    c.op("dve", lambda e, b2=b2, psu=psu: e.tensor_tensor(Sg0[:, 2 * b2:2 * b2 + 2, :], Sg0[:, 2 * b2:2 * b2 + 2, :], psu.rearrange("p (b n) -> p b n", b=2), ALU.add),
                         reads=[self.pst[bank], t_s], writes=[t_s])
                for h in range(4):
                    c.dma("sp", self.o["gla_s"].ap()[l][:, h, :, :].rearrange("b d v -> d b v"), Sg0[32 * h:32 * h + 32, :, 64 * h:64 * h + 64], reads=[t_s], key="og")
            yield "S"
            o4 = ps5[:, 0:256].rearrange("p (h d) -> p h d", h=4)
            c.op("act", lambda e, sl=sl, o4=o4: e.activation(otmp[:, sl, :, :], o4, AF.Square), reads=[self.pst[ob]], writes=[t_o[sl]])
            yield
            c.op("dve", lambda e, sl=sl: e.tensor_reduce(oss[:, sl, 0:4], otmp[:, sl, :, :], AX.X, ALU.add), reads=[t_o[sl]], writes=[t_o[sl]])
            yield
            c.op("act", lambda e, sl=sl: e.activation(oss[:, sl, 4:8], oss[:, sl, 0:4], AF.Ln, bias=EPS, scale=1.0 / 64), reads=[t_o[sl]], writes=[t_o[sl]])
            yield
            c.op("act", lambda e, sl=sl: e.activation(oss[:, sl, 4:8], oss[:, sl, 4:8], AF.Exp, scale=-0.5), reads=[t_o[sl]], writes=[t_o[sl]])
            yield
            c.op("dve", lambda e, sl=sl, o4=o4: e.tensor_tensor(otmp[:, sl, :, :], o4, oss[:, sl, 4:8].unsqueeze(2).to_broadcast([128, 4, 64]), ALU.mult),
                 reads=[self.pst[ob], t_o[sl]], writes=[t_o[sl]])
            yield
            c.op("pool", lambda e, sl=sl, j=j: e.tensor_tensor(self.ytok[:, j, 0:256], otmp[:, sl, :, :].rearrange("p h d -> p (h d)"), sgg[:, j, :], ALU.mult),
                 reads=[t_o[sl], t_sgg], writes=[self.t_ytok])
            yield
        self.pipeline_rr([body(j) for j in range(T)], 3)
        c.release(alltoks)

    def ssd(self, MOFF):
        c, d, l, T, NT = self.c, self.d, self.l, self.T, self.NT
        st = self.S[l]
        Tp = 8
        NP = 1024
        o = MOFF
        o_xp = o
        XP = self.av(o, [128, 6, 1027], BF16); o += 12324
        XS = self.av(o, [128, 6, 16, 11], BF16); o += 2112
        xcT = self.av(o, [128, 5, 1152], BF16); o += 11520
        o_btz = o
        BTz = self.av(o, [128, 2, 1152], BF16); o += 4608
        sz = self.av(o, [128, 9, 512], BF16); o += 9216
        dtraw = self.av(o, [128, 9, 8], F32); o += 288
        dtv = self.av(o, [128, 9, 8], F32); o += 288
        adt = self.av(o, [128, 9, 8], F32); o += 288
        acum = self.av(o, [128, 9, 8], F32); o += 288
        eacum = self.av(o, [128, 9, 8], F32); o += 288
        tail = self.av(o, [128, 9, 8], F32); o += 288
        cdrep = self.av(o, [128, 9, 8], F32); o += 288
        adth = self.av(o, [128, 9, 8], BF16); o += 144
        adtl = self.av(o, [128, 9, 8], BF16); o += 144
        cdx = self.av(o, [128, 4, 16], F32); o += 256
        yss = self.av(o, [128, 2, 4], F32); o += 32
        o_tile = o
        cvt = self.av(o_tile + 8192, [128, 768], F32)
        cvo = self.av(o_tile + 8192 + 3072, [128, 6, 48], F32)
        xtok = self.av(o, [128, 2, 768], BF16); o += 3072
        Lm = self.av(o, [128, 8, 128], BF16); o += 2048
        MT = self.av(o, [128, 2, 8, 128], BF16); o += 4096
        xdt = self.av(o, [128, 2, 512], BF16); o += 2048
        xdtt = self.av(o, [128, 2, 512], BF16); o += 2048
        xD = self.av(o, [128, 2, 512], BF16); o += 2048
        ytmp = self.av(o, [128, 2, 512], F32); o += 4096
        ytk = self.av(o, [128, 2, 512], BF16); o += 2048
        ysq = self.av(o, [128, 512], F32); o += 2048
        o_ctm = o
        CTm = self.av(o, [128, 16, 128], BF16); o += 4096
        ctmp = [self.av(o_tile + i * 4096, [128, 1024], F32) for i in range(2)]
        oo = o_xp
        St32 = [self.av(oo + i * 2048, [128, 512], F32) for i in range(2)]; oo += 4096
        Stb = [self.av(oo + i * 2048, [128, 2, 4, 128], BF16) for i in range(2)]; oo += 4096
        SbdT = [self.av(oo + i * 2048, [128, 2, 512], BF16) for i in range(2)]; oo += 4096
        assert oo <= o_xp + 12324 + 2112
        Bm = self.av(o_btz, [128, 2, 16, 64], BF16)

        t_XP = [c.new_tok() for _ in range(6)]
        t_XS = [c.new_tok() for _ in range(6)]
        t_xc = [[c.new_tok() for _ in self.groups] for _ in range(6)]
        t_sz = [c.new_tok() for _ in range(T)]
        t_dt = c.new_tok()
        t_cv = c.new_tok()
        t_ct = [c.new_tok(), c.new_tok()]
        alltoks = t_XS + sum(t_xc, []) + t_sz + [t_dt, t_cv] + t_ct
        W, wt = self.getw("ssd_x", l)
        W2, wt2 = self.getw("ssd_z", l)
        for ch in range(6):
            def ev(ps, gi, c0, c1, bt, ch=ch):
                if c0 < NP:
                    c.op("act", lambda e: e.copy(XP[:, ch, 3 + c0:3 + c1], ps), reads=[bt], writes=[t_XP[ch]])
                else:
                    c.op("act", lambda e: e.copy(XS[:, ch, :, 3:11], ps.rearrange("p (b i) -> p b i", i=8)), reads=[bt], writes=[t_XS[ch]])
            self.proj_fm(W, wt, ch * 128, 128, self.uT, lambda gi: [self.t_uTg[gi]], ev)

        self.prefetch("w_out", l)

        def ev_z(ps, j, bt):
            c.op("act", lambda e: e.activation(sz[:, j, :], ps, AF.Silu), reads=[bt], writes=[t_sz[j]])

        def ev_dt(ps, j, bt):
            c.op("dve", lambda e: e.tensor_copy(dtraw[:, j, :], ps), reads=[bt], writes=[t_dt])

        self.proj_tm(W2, wt2, 512, 8, ev_dt)
        dtb = self.rep8[:, 0:8].unsqueeze(1).to_broadcast([128, T, 8])
        c.op("dve", lambda e: e.tensor_tensor(dtv[:, 0:T, :], dtraw[:, 0:T, :], dtb, ALU.add), reads=[t_dt, self.PT_], writes=[t_dt])
        c.op("act", lambda e: e.activation(dtv[:, 0:T, :], dtv[:, 0:T, :], AF.Exp), reads=[t_dt], writes=[t_dt])
        c.op("act", lambda e: e.activation(dtv[:, 0:T, :], dtv[:, 0:T, :], AF.Ln, bias=1.0), reads=[t_dt], writes=[t_dt])
        c.op("dve", lambda e: e.tensor_tensor(adt[:, 0:T, :], dtv[:, 0:T, :], self.arep[:, :].unsqueeze(1).to_broadcast([128, T, 8]), ALU.mult), reads=[t_dt, self.PT_], writes=[t_dt])
        c.op("dve", lambda e: e.tensor_copy(adth[:, 0:T, :], adt[:, 0:T, :]), reads=[t_dt], writes=[t_dt])
        c.op("dve", lambda e: e.tensor_tensor(tail[:, 0:T, :], adt[:, 0:T, :], adth[:, 0:T, :], ALU.subtract), reads=[t_dt], writes=[t_dt])
        c.op("dve", lambda e: e.tensor_copy(adtl[:, 0:T, :], tail[:, 0:T, :]), reads=[t_dt], writes=[t_dt])
        ps3 = self.psb(3)
        adp = adt[:, 0:Tp, :].rearrange("p j h -> p (j h)")
        c.op("pe", lambda e: e.matmul(ps3[:, 0:64], lhsT=self.cf("triP"), rhs=adp, start=True, stop=True), reads=[t_dt, self.CT], writes=[self.pst[3]])
        c.op("pe", lambda e: e.matmul(ps3[:, 128:192], lhsT=self.cf("onesf"), rhs=adp, start=True, stop=True), reads=[t_dt, self.CT], writes=[self.pst[3]])
        if self.has_s:
            c.op("pe", lambda e: e.matmul(ps3[:, 64:72], lhsT=self.cf("triS"), rhs=adt[:, 8, :], start=True, stop=True), reads=[t_dt, self.CT], writes=[self.pst[3]])
            c.op("pe", lambda e: e.matmul(ps3[:, 192:200], lhsT=self.cf("sameS"), rhs=adt[:, 8, :], start=True, stop=True), reads=[t_dt, self.CT], writes=[self.pst[3]])
        n8 = T * 8
        fl = lambda a: a[:, 0:T, :].rearrange("p j h -> p (j h)")
        c.op("act", lambda e: e.copy(fl(acum), ps3[:, 0:n8]), reads=[self.pst[3]], writes=[t_dt])
        c.op("act", lambda e: e.activation(fl(eacum), ps3[:, 0:n8], AF.Exp), reads=[self.pst[3]], writes=[t_dt])
        c.op("act", lambda e: e.activation(fl(cdrep), ps3[:, 128:128 + n8], AF.Exp), reads=[self.pst[3]], writes=[t_dt])
        c.op("dve", lambda e: e.tensor_tensor(fl(tail), ps3[:, 128:128 + n8], fl(acum), ALU.subtract), reads=[self.pst[3], t_dt], writes=[t_dt])
        c.op("act", lambda e: e.activation(fl(tail), fl(tail), AF.Exp), reads=[t_dt], writes=[t_dt])
        rows32 = self.av(o_tile + 12416, [128, 8, 128], F32)
        rows2 = self.av(o_ctm, [128, 8, 128], BF16)
        rows_hi = rows2
        rows_lo = rows2[32:64, :, :]
        t_rows = c.new_tok()
        alltoks.append(t_rows)
        c.dma("sp", self.scr_ac.ap()[:, 0:n8], fl(acum), reads=[t_dt], writes=[self.t_scr_ac], key="sac")
        c.dma("sp", rows32[0:T, :, :], bass.AP(self.scr_ac, 0, [[8, T], [1, 8], [72, 128]]), reads=[self.t_scr_ac], writes=[t_rows], key="sac2", nc_ok=True)
        c.op("pool", lambda e: e.tensor_copy(XP[:, :, 0:3], st["ctail"][:, :, :]), reads=[st["t_ctail"]], writes=t_XP)
        c.op("pool", lambda e: e.memset(BTz[:, :, :], 0.0), writes=[t_xc[4][gi] for gi in range(len(self.groups))])
        if self.has_s:
            c.dma("sp", cvt[0:48, :], d["st_conv"].ap()[l].rearrange("b k c -> (b k) c"), writes=[t_cv], key="cvs")
            for half, (ch0, nch) in enumerate(((0, 4), (4, 2))):
                ps = self.psb(half)
                for cc in range(nch):
                    ch = ch0 + cc
                    c.op("pe", lambda e, ps=ps, cc=cc, ch=ch: e.transpose(ps[:, cc * 48:(cc + 1) * 48], cvt[0:48, ch * 128:(ch + 1) * 128], self.cf("identf")[0:48, 0:48]),
                         reads=[t_cv, self.CT], writes=[self.pst[half]])
                c.op("dve", lambda e, ps=ps, ch0=ch0, nch=nch: e.tensor_copy(XS[:, ch0:ch0 + nch, :, 0:3], ps[:, 0:nch * 48].rearrange("p (c b k) -> p c b k", c=nch, b=16)),
                     reads=[self.pst[half]], writes=[t_XS[ch] for ch in range(ch0, ch0 + nch)])
        accS = [self.av(o_tile + 16512 + i * 512, [128, 16, 8], F32) for i in range(2)]

        def conv_id(ch):
            sl = ch % 2
            acc = ctmp[sl]
            c.op("act", lambda e: e.activation(acc[:, 0:NP], XP[:, ch, 0:NP], AF.Identity, bias=self.cbT[:, ch:ch + 1], scale=self.cwT[:, ch, 0:1]),
                 reads=[t_XP[ch], self.PT_], writes=[t_ct[sl]])
            if self.has_s:
                c.op("act", lambda e: e.activation(accS[sl], XS[:, ch, :, 0:8], AF.Identity, bias=self.cbT[:, ch:ch + 1], scale=self.cwT[:, ch, 0:1]),
                     reads=[t_XS[ch], self.PT_], writes=[t_ct[sl]])

        def conv_taps(ch):
            sl = ch % 2
            acc = ctmp[sl]
            for k in range(1, 4):
                c.op("dve", lambda e, k=k: e.scalar_tensor_tensor(acc[:, 0:NP], XP[:, ch, k:k + NP], self.cwT[:, ch, k:k + 1], acc[:, 0:NP], ALU.mult, ALU.add),
                     reads=[t_XP[ch], t_ct[sl], self.PT_], writes=[t_ct[sl]])
            if self.has_s:
                for k in range(1, 4):
                    c.op("dve", lambda e, k=k: e.scalar_tensor_tensor(accS[sl], XS[:, ch, :, k:k + 8], self.cwT[:, ch, k:k + 1], accS[sl], ALU.mult, ALU.add),
                         reads=[t_XS[ch], t_ct[sl], self.PT_], writes=[t_ct[sl]])

        def conv_silu(ch):
            sl = ch % 2
            acc = ctmp[sl]
            parts = [(acc[:, 0:NP], 0, NP, [t_xc[ch][0], t_xc[ch][1]])]
            if self.has_s:
                parts.append((accS[sl].rearrange("p b i -> p (b i)"), NP, NP + 128, [t_xc[ch][2]]))
            for (src, a0, a1, wr) in parts:
                if ch < 4:
                    c.op("act", lambda e, src=src, a0=a0, a1=a1: e.activation(xcT[:, ch, a0:a1], src, AF.Silu), reads=[t_ct[sl]], writes=wr)
                elif ch == 5:
                    c.op("act", lambda e, src=src, a0=a0, a1=a1: e.activation(xcT[:, 4, a0:a1], src, AF.Silu), reads=[t_ct[sl]], writes=wr)
                else:
                    for g in range(2):
                        c.op("act", lambda e, src=src, a0=a0, a1=a1, g=g: e.activation(BTz[64 * g:64 * g + 64, g, a0:a1], src[64 * g:64 * g + 64, :], AF.Silu), reads=[t_ct[sl]], writes=wr)

        for step in range(7):
            if step < 6:
                conv_id(step)
            if step >= 1:
                conv_silu(step - 1)
            if step < 6:
                conv_taps(step)
            zt = [step] if step < 6 else list(range(6, T))
            self.proj_tm(W2, wt2, 0, 512, ev_z, tiles=zt)
        self.prefetch("w_pg", l)
        c.op("pool", lambda e: e.tensor_copy(st["ctail"][:, :, :], XP[:, :, NP:NP + 3]), reads=t_XP, writes=[st["t_ctail"]])
        if self.hf == 1:
            self.conv_out(XP[:, :, NP:NP + 3], t_XP, 3, cvo, cvt, t_cv, self.o["conv_p"].ap()[l])
            self.conv_out(XS[:, :, :, 8:11], t_XS, 48, cvo, cvt, t_cv, self.o["conv_s"].ap()[l].rearrange("b k c -> (b k) c"))
        c.op("pool", lambda e: e.memset(rows2[:, :, :], 0.0), writes=[t_rows])
        c.op("dve", lambda e: e.tensor_copy(rows_hi[0:T, :, :], rows32[0:T, :, :]), reads=[t_rows], writes=[t_rows])
        c.op("dve", lambda e: e.tensor_tensor(rows32[0:T, :, :], rows32[0:T, :, :], rows_hi[0:T, :, :], ALU.subtract), reads=[t_rows], writes=[t_rows])
        c.op("dve", lambda e: e.tensor_copy(rows_lo[0:T, :, :], rows32[0:T, :, :]), reads=[t_rows], writes=[t_rows])
        c.release(t_ct + [t_cv, t_rows])
        t_xtok = [c.new_tok(), c.new_tok()]
        t_L = c.new_tok()
        t_MT = [c.new_tok(), c.new_tok()]
        t_xd = [c.new_tok(), c.new_tok()]
        t_y = [c.new_tok(), c.new_tok()]
        t_ytk = [c.new_tok(), c.new_tok()]
        t_yss = c.new_tok()
        t_ysq = c.new_tok()
        alltoks += t_xtok + [t_L] + t_MT + t_xd + t_y + t_ytk + [t_yss, t_ysq]
        dsk = self.rep8[:, 16:24]
        def body(j):
            is_s = self.has_s and j == 8
            sl = j % 2
            gi = j // 4
            cs = slice(j * 128, (j + 1) * 128)
            tri_b = self.cb("triS") if is_s else self.cb("triP")
            ntri_b = self.cb("ntriS") if is_s else self.cb("ntriP")
            neg_b = self.cb("negS") if is_s else self.cb("negP")
            if not is_s:
                self.keepwarm(2, 3)
            tp = self.psb(2, BF16)[:, 0:768]
            for a in range(6):
                src = xcT[:, a, cs] if a < 4 else BTz[:, a - 4, cs]
                rd = [t_xc[a][gi]] if a < 4 else [t_xc[4][gi]]
                c.op("pe", lambda e, a=a, src=src, tp=tp: e.transpose(tp[:, a * 128:(a + 1) * 128], src, self.cb("ident")), reads=rd + [self.CT], writes=[self.pst[2]])
            c.op("act", lambda e, sl=sl, tp=tp: e.copy(xtok[:, sl, :], tp), reads=[self.pst[2]], writes=[t_xtok[sl]])
            x3 = xtok[:, sl, 0:512].rearrange("p (h q) -> p h q", h=8)
            c.op("dve", lambda e, sl=sl, j=j, x3=x3: e.tensor_tensor(xdt[:, sl, :].rearrange("p (h q) -> p h q", h=8), x3, dtv[:, j, :].unsqueeze(2).to_broadcast([128, 8, 64]), ALU.mult),
                 reads=[t_xtok[sl], t_dt], writes=[t_xd[sl]])
            c.op("pool", lambda e, sl=sl, j=j: e.tensor_tensor(xdtt[:, sl, :].rearrange("p (h q) -> p h q", h=8), xdt[:, sl, :].rearrange("p (h q) -> p h q", h=8),
                                                               tail[:, j, :].unsqueeze(2).to_broadcast([128, 8, 64]), ALU.mult), reads=[t_xd[sl], t_dt], writes=[t_xd[sl]])
            c.op("pool", lambda e, sl=sl, x3=x3: e.tensor_tensor(xD[:, sl, :].rearrange("p (h q) -> p h q", h=8), x3, dsk.unsqueeze(2).to_broadcast([128, 8, 64]), ALU.mult),
                 reads=[t_xtok[sl], self.PT_], writes=[t_xd[sl]])
            cbp = self.psb(7)[:, 0:256]
            t_cb = self.pst[7]
            for g in range(2):
                c.op("pe", lambda e, g=g, cs=cs, cbp=cbp: e.matmul(cbp[:, g * 128:(g + 1) * 128], lhsT=BTz[:, g, cs], rhs=xcT[:, 4, cs], start=True, stop=True),
                     reads=[t_xc[4][gi], t_xc[5][gi]], writes=[t_cb])
            for b4 in range(2):
                bank = 4 + b4
                dst = self.psb(bank)
                rd = [t_dt, self.CT, t_rows]
                wr = [self.pst[bank]]
                rh = rows2[:, 4 * b4:4 * b4 + 4, :].rearrange("p h t -> p (h t)")
                ah = adth[:, j, 4 * b4:4 * b4 + 4].unsqueeze(2).to_broadcast([128, 4, 128])
                al = adtl[:, j, 4 * b4:4 * b4 + 4].unsqueeze(2).to_broadcast([128, 4, 128])
                ng = neg_b.unsqueeze(1).to_broadcast([128, 4, 128])
                es = self.Esel[:, j, :]
                c.op("pe", lambda e, dst=dst, es=es, rh=rh: e.matmul(dst, lhsT=es, rhs=rh, start=True, stop=False), reads=rd, writes=wr)
                c.op("pe", lambda e, dst=dst, ah=ah, ntri_b=ntri_b: e.matmul(dst, lhsT=ntri_b, rhs=ah, start=False, stop=False), reads=rd, writes=wr)
                c.op("pe", lambda e, dst=dst, al=al, ntri_b=ntri_b: e.matmul(dst, lhsT=ntri_b, rhs=al, start=False, stop=False), reads=rd, writes=wr)
                c.op("pe", lambda e, dst=dst, ng=ng: e.matmul(dst, lhsT=self.cb("ident"), rhs=ng, start=False, stop=True), reads=rd, writes=wr)
            yield
            for hb in range(2):
                c.op("act", lambda e, hb=hb: e.activation(Lm[:, hb * 4:(hb + 1) * 4, :].rearrange("p h t -> p (h t)"), self.psb(4 + hb), AF.Exp),
                     reads=[self.pst[4 + hb]], writes=[t_L])
            yield
            c.op("dve", lambda e, sl=sl: e.tensor_tensor(MT[:, sl, :, :].rearrange("p (g r) t -> p g r t", g=2), Lm[:, :, :].rearrange("p (g r) t -> p g r t", g=2),
                                                         cbp.rearrange("p (g t) -> p g t", g=2).unsqueeze(2).to_broadcast([128, 2, 4, 128]), ALU.mult),
                 reads=[t_L, t_cb], writes=[t_MT[sl]])
            yield
            yp = self.psb(6)
            c.op("pe", lambda e, sl=sl: e.matmul(yp[:, :], lhsT=self.cb("ident"), rhs=xD[:, sl, :], start=True, stop=False), reads=[t_xd[sl], self.CT], writes=[self.pst[6]])
            for h in range(8):
                c.op("pe", lambda e, sl=sl, h=h: e.matmul(yp[:, h * 64:(h + 1) * 64], lhsT=MT[:, sl, h, :], rhs=xdt[:, sl, h * 64:(h + 1) * 64], start=False, stop=True),
                     reads=[t_MT[sl], t_xd[sl]], writes=[self.pst[6]])
            yi = self.psb(3)
            if not is_s:
                c.op("pe", lambda e, cs=cs: e.matmul(yi[:, :], lhsT=xcT[:, 4, cs], rhs=st["ssd_b"][:, :], start=True, stop=True), reads=[t_xc[5][gi], st["t_ssd"]], writes=[self.pst[3]])
            else:
                c.release(t_XP + t_XS + [t_rows] + t_xc[4])
                t_st = c.new_tok()
                t_bm = c.new_tok()
                t_S32 = [c.new_tok(), c.new_tok()]
                t_Stb = [c.new_tok(), c.new_tok()]
                t_Sbd = [c.new_tok(), c.new_tok()]
                alltoks.extend([t_st, t_bm] + t_S32 + t_Stb + t_Sbd)

                def st_in(grp):
                    s2 = grp % 2
                    c.dma("sp", St32[s2].rearrange("p (b a n) -> p b a n", b=2, a=4), d["st_ssm"].ap()[l][2 * grp:2 * grp + 2].rearrange("b (a hh) p n -> (hh p) b a n", hh=2),
                          writes=[t_S32[s2]], key="st%d" % s2)

                st_in(0)
                st_in(1)
                c.op("dve", lambda e, cs=cs: e.tensor_tensor(CTm[:, :, :], xcT[:, 4, cs].unsqueeze(1).to_broadcast([128, 16, 128]), self.cb("smbt").rearrange("p (b t) -> p b t", b=16), ALU.mult),
                     reads=[t_xc[5][gi], self.CT], writes=[t_st])
                for i2 in range(2):
                    c.op("pool", lambda e, i2=i2: e.memset(Stb[i2][:, :, :, :], 0.0), writes=[t_Stb[i2]])
                for g in range(2):
                    btk = xtok[:, sl, 512 + g * 128 + g * 64:512 + g * 128 + g * 64 + 64]
                    c.op("dve", lambda e, g=g, btk=btk: e.tensor_tensor(Bm[:, g, :, :], btk.unsqueeze(1).to_broadcast([128, 16, 64]), self.cb("smtok").unsqueeze(2).to_broadcast([128, 16, 64]), ALU.mult),
                         reads=[t_xtok[sl], self.CT], writes=[t_bm])
                psx = self.psb(7)
                c.op("dve", lambda e: e.tensor_copy(ysq[:, :].rearrange("p (h q) -> p h q", h=8), adt[:, 8, :].unsqueeze(2).to_broadcast([128, 8, 64])), reads=[t_dt, t_ysq], writes=[t_ysq])
                for a in range(4):
                    c.op("pe", lambda e, a=a: e.matmul(psx[:, a * 16:(a + 1) * 16], lhsT=ysq[:, a * 128:(a + 1) * 128], rhs=self.cf("smtok"), start=True, stop=True),
                         reads=[t_ysq, self.CT, t_MT[sl]], writes=[self.pst[7]])
                c.op("act", lambda e: e.activation(cdx[:, :, :].rearrange("p a b -> p (a b)"), psx[:, 0:64], AF.Exp), reads=[self.pst[7]], writes=[t_dt])
                for grp in range(8):
                    s2 = grp % 2
                    s4 = St32[s2].rearrange("p (b a n) -> p b a n", b=2, a=4)
                    c.op("dve", lambda e, s4=s4, s2=s2: e.tensor_copy(Stb[s2][:, :, 0:2, 0:64], s4[:, :, 0:2, :]), reads=[t_S32[s2]], writes=[t_Stb[s2]])
                    c.op("dve", lambda e, s4=s4, s2=s2: e.tensor_copy(Stb[s2][:, :, 2:4, 64:128], s4[:, :, 2:4, :]), reads=[t_S32[s2]], writes=[t_Stb[s2]])
                    tp2 = self.psb(2, BF16)
                    for bb in range(2):
                        for a in range(4):
                            c.op("pe", lambda e, bb=bb, a=a, tp2=tp2, s2=s2: e.transpose(tp2[:, (bb * 4 + a) * 128:(bb * 4 + a + 1) * 128], Stb[s2][:, bb, a, :], self.cb("ident")),
                                 reads=[t_Stb[s2], self.CT], writes=[self.pst[2]])
                    c.op("act", lambda e, s2=s2, tp2=tp2: e.copy(SbdT[s2][:, :, :].rearrange("p b n -> p (b n)"), tp2), reads=[self.pst[2]], writes=[t_Sbd[s2]])
                    for bb in range(2):
                        bq = grp * 2 + bb
                        c.op("pe", lambda e, bq=bq, bb=bb, s2=s2: e.matmul(yi[:, :], lhsT=CTm[:, bq, :], rhs=SbdT[s2][:, bb, :], start=(bq == 0), stop=(bq == 15)),
                             reads=[t_st, t_Sbd[s2]], writes=[self.pst[3]])
                    bank = 4 + s2
                    pu = self.psb(bank)
                    for a in range(4):
                        g = a // 2
                        c.op("pe", lambda e, a=a, g=g, pu=pu, grp=grp: e.matmul(pu[:, a * 128:(a + 1) * 128], lhsT=xdtt[:, sl, a * 128:(a + 1) * 128],
                                                                                rhs=Bm[:, g, 2 * grp:2 * grp + 2, :].rearrange("p b n -> p (b n)"), start=True, stop=True),
                             reads=[t_xd[sl], t_bm, t_L], writes=[self.pst[bank]])
                    cdv = cdx[:, :, 2 * grp:2 * grp + 2].rearrange("p a b -> p b a").unsqueeze(3).to_broadcast([128, 2, 4, 64])
                    c.op("dve", lambda e, s4=s4, cdv=cdv: e.tensor_tensor(s4, s4, cdv, ALU.mult), reads=[t_S32[s2], t_dt, t_Stb[s2]], writes=[t_S32[s2]])
                    c.op("dve", lambda e, s4=s4, pu=pu: e.tensor_tensor(s4, s4, pu[:, :].rearrange("p (a b n) -> p b a n", a=4, b=2), ALU.add), reads=[t_S32[s2], self.pst[bank]], writes=[t_S32[s2]])
                    c.dma("sp", self.o["ssm_s"].ap()[l][2 * grp:2 * grp + 2].rearrange("b (a hh) p n -> (hh p) b a n", hh=2), s4, reads=[t_S32[s2]], key="so%d" % s2)
                    if grp + 2 < 8:
                        st_in(grp + 2)
            if not is_s:
                up = self.psb(0)
                for g in range(2):
                    c.op("pe", lambda e, g=g, sl=sl: e.matmul(up[:, g * 256:(g + 1) * 256], lhsT=xtok[:, sl, 512 + g * 128:512 + (g + 1) * 128], rhs=xdtt[:, sl, g * 256:(g + 1) * 256],
                                                              start=True, stop=True), reads=[t_xtok[sl], t_xd[sl], t_MT[sl]], writes=[self.pst[0]])
                S3 = st["ssd"][:, :].rearrange("p (h q) -> p h q", h=8)
                c.op("dve", lambda e, S3=S3, j=j: e.tensor_tensor(S3, S3, cdrep[:, j, :].unsqueeze(2).to_broadcast([128, 8, 64]), ALU.mult), reads=[st["t_ssd"], t_dt], writes=[st["t_ssd"]])
                c.op("dve", lambda e: e.tensor_tensor(st["ssd"][:, :], st["ssd"][:, :], up[:, :], ALU.add), reads=[st["t_ssd"], self.pst[0]], writes=[st["t_ssd"]])
                c.op("act", lambda e: e.copy(st["ssd_b"][:, :], st["ssd"][:, :]), reads=[st["t_ssd"]], writes=[st["t_ssd"]])
                if self.hf == 1 and j == 7:
                    pso = self.psb(0)
                    for a in range(4):
                        c.op("pe", lambda e, a=a: e.transpose(pso[:, a * 128:(a + 1) * 128], st["ssd"][:, a * 128:(a + 1) * 128], self.cf("identf")), reads=[st["t_ssd"], self.CT], writes=[self.pst[0]])
                    so = ysq[:, 0:256].rearrange("p (a n) -> p a n", a=4)
                    for g in range(2):
                        c.op("dve", lambda e, g=g: e.tensor_copy(so[:, 2 * g:2 * g + 2, :], pso[:, :].rearrange("p (a n) -> p a n", a=4)[:, 2 * g:2 * g + 2, 64 * g:64 * g + 64]),
                             reads=[self.pst[0], t_ysq], writes=[t_ysq])
                    c.dma("sp", self.o["ssm_p"].ap()[l].rearrange("(a hh) p n -> (hh p) a n", hh=2), so, reads=[t_ysq], key="oss")
            yield
            y3 = ytmp[:, sl, :].rearrange("p (h q) -> p h q", h=8)
            c.op("dve", lambda e, y3=y3, j=j: e.tensor_tensor(y3, yi[:, :].rearrange("p (h q) -> p h q", h=8), eacum[:, j, :].unsqueeze(2).to_broadcast([128, 8, 64]), ALU.mult),
                 reads=[self.pst[3], t_dt], writes=[t_y[sl]])
            c.op("dve", lambda e, sl=sl: e.tensor_tensor(ytmp[:, sl, :], yp[:, :], ytmp[:, sl, :], ALU.add), reads=[self.pst[6], t_y[sl]], writes=[t_y[sl]])
            yield
            c.op("pool", lambda e, sl=sl, j=j: e.tensor_tensor(ytmp[:, sl, :], ytmp[:, sl, :], sz[:, j, :], ALU.mult), reads=[t_y[sl], t_sz[j]], writes=[t_y[sl]])
            for g in range(2):
                c.op("act", lambda e, sl=sl, g=g: e.activation(ysq[:, g * 256:(g + 1) * 256], ytmp[:, sl, g * 256:(g + 1) * 256], AF.Square, accum_out=yss[:, sl, g:g + 1]),
                     reads=[t_y[sl], t_yss], writes=[t_ysq, t_yss])
            c.op("act", lambda e, sl=sl: e.activation(yss[:, sl, 2:4], yss[:, sl, 0:2], AF.Ln, bias=EPS, scale=1.0 / 256), reads=[t_yss], writes=[t_yss])
            c.op("act", lambda e, sl=sl: e.activation(yss[:, sl, 2:4], yss[:, sl, 2:4], AF.Exp, scale=-0.5), reads=[t_yss], writes=[t_yss])
            c.op("dve", lambda e, sl=sl: e.tensor_tensor(ytmp[:, sl, :].rearrange("p (g n) -> p g n", g=2), ytmp[:, sl, :].rearrange("p (g n) -> p g n", g=2),
                                                         yss[:, sl, 2:4].unsqueeze(2).to_broadcast([128, 2, 256]), ALU.mult), reads=[t_y[sl], t_yss], writes=[t_y[sl]])
            c.op("pool", lambda e, sl=sl: e.tensor_tensor(ytk[:, sl, :], ytmp[:, sl, :], self.ssdnw[:, :], ALU.mult), reads=[t_y[sl], self.PT_], writes=[t_ytk[sl]])
            yield
            tpy = self.psb(1, BF16)[:, 0:1024]
            for a in range(4):
                c.op("pe", lambda e, a=a, tpy=tpy, sl=sl: e.transpose(tpy[:, a * 128:(a + 1) * 128], ytk[:, sl, a * 128:(a + 1) * 128], self.cb("ident")), reads=[t_ytk[sl], self.CT], writes=[self.pst[1]])
            for a in range(4):
                c.op("pe", lambda e, a=a, tpy=tpy: e.transpose(tpy[:, (4 + a) * 128:(5 + a) * 128], self.ytok[:, j, a * 128:(a + 1) * 128], self.cb("ident")), reads=[self.t_ytok, self.CT], writes=[self.pst[1]])
            c.op("act", lambda e, tpy=tpy, cs=cs: e.copy(self.uT[:, 0:8, cs], tpy.rearrange("p (a t) -> p a t", a=8)), reads=[self.pst[1]], writes=[self.t_uTg[gi]])
        self.pipeline_sched([body(j) for j in range(T)], [(6, 4), (4, 2), (0, 0), (1, 0), (3, 1), (2, 0), (5, 2)])
        c.release(alltoks + t_XP + t_XS)

    def conv_out(self, src, toks, nrow, cvo, cvt, t_cv, dst):
        c = self.c
        if nrow == 3:
            c.op("dve", lambda e: e.tensor_copy(cvo[:, :, 0:3], src), reads=toks, writes=[t_cv])
        else:
            c.op("dve", lambda e: e.tensor_copy(cvo[:, :, :].rearrange("p c (b k) -> p c b k", b=16), src), reads=toks, writes=[t_cv])
        for half, (ch0, nch) in enumerate(((0, 4), (4, 2))):
            ps = self.psb(half)
            for cc in range(nch):
                ch = ch0 + cc
                c.op("pe", lambda e, ps=ps, cc=cc, ch=ch: e.transpose(ps[0:nrow, cc * 128:(cc + 1) * 128], cvo[:, ch, 0:nrow], self.cf("identf")), reads=[t_cv, self.CT], writes=[self.pst[half]])
            c.op("act", lambda e, ps=ps, ch0=ch0, nch=nch: e.copy(cvt[0:nrow, ch0 * 128:(ch0 + nch) * 128], ps[0:nrow, 0:nch * 128]), reads=[self.pst[half]], writes=[t_cv])
        c.dma("sp", dst, cvt[0:nrow, :], reads=[t_cv], key="ocv")

    def y_transposes(self):
        c = self.c
        for j in range(self.T):
            cs = slice(j * 128, (j + 1) * 128)
            tp = self.psb(2, BF16)[:, 0:512]
            for a in range(4):
                c.op("pe", lambda e, a=a, j=j, tp=tp: e.transpose(tp[:, a * 128:(a + 1) * 128], self.ytok[:, j, a * 128:(a + 1) * 128], self.cb("ident")),
                     reads=[self.t_ytok, self.CT], writes=[self.pst[2]])
            c.op("act", lambda e, tp=tp, cs=cs: e.copy(self.uT[:, 4:8, cs], tp.rearrange("p (a t) -> p a t", a=4)), reads=[self.pst[2]], writes=[self.t_uTg[j // 4]])
        c.release([self.t_ytok])

    def out_proj(self, MOFF):
        c, d, l = self.c, self.d, self.l
        self.hbT = self.av(0, [128, 8, 1152], BF16)
        self.t_hb = [c.new_tok() for _ in self.groups]
        self.ptok = self.av(18432, [128, 9, 256], BF16)
        self.t_pt = c.new_tok()
        c.dma("pool", self.ptok[:, 0:self.T, :], d["p"].ap()[l][self.row0:self.row0 + self.NT, :].rearrange("(j p) n -> p j n", p=128), writes=[self.t_pt], key="ptk")
        W, wt = self.getw("w_out", l)
        for dch in range(8):
            def ev(ps, gi, c0, c1, bt, dch=dch):
                c.op("dve", lambda e: e.tensor_tensor(self.hT[:, dch, c0:c1], self.hT[:, dch, c0:c1], ps, ALU.add), reads=[bt, self.t_hT[dch][gi]], writes=[self.t_hT[dch][gi]])
                c.op("act", lambda e: e.copy(self.hbT[:, dch, c0:c1], self.hT[:, dch, c0:c1]), reads=[self.t_hT[dch][gi]], writes=[self.t_hb[gi]])
            self.proj_fm(W, wt, dch * 128, 128, self.uT, lambda gi: [self.t_uTg[gi]], ev)
        self.prefetch("w_pe", l)

    def ple(self, MOFF):
        c, d, l, T, NT = self.c, self.d, self.l, self.T, self.NT
        ptok = self.ptok
        sgt = [self.av(18432 + 4608 + i * 2048, [128, 512], F32) for i in range(2)]
        t_pt = self.t_pt
        t_sg = [c.new_tok(), c.new_tok()]
        t_pT = [c.new_tok() for _ in self.groups]
        for j in range(T):
            cs = slice(j * 128, (j + 1) * 128)
            tp = self.psb(2, BF16)[:, 0:256]
            for a in range(2):
                c.op("pe", lambda e, a=a, j=j, tp=tp: e.transpose(tp[:, a * 128:(a + 1) * 128], ptok[:, j, a * 128:(a + 1) * 128], self.cb("ident")), reads=[t_pt, self.CT], writes=[self.pst[2]])
            c.op("act", lambda e, tp=tp, cs=cs: e.copy(self.pT[:, :, cs], tp.rearrange("p (a t) -> p a t", a=2)), reads=[self.pst[2]], writes=[t_pT[j // 4]])
        Wg, wtg = self.getw("w_pg", l)
        We, wte = self.getw("w_pe", l)
        it = 0
        for dch in range(8):
            for gi, (c0, c1) in enumerate(self.groups):
                n = c1 - c0
                s = it % 2
                it += 1
                pa, pb = self.psb(0 + 2 * s), self.psb(1 + 2 * s)
                ba, bb = 0 + 2 * s, 1 + 2 * s
                for k in range(8):
                    c.op("pe", lambda e, k=k, pa=pa, n=n, c0=c0, c1=c1, dch=dch: e.matmul(pa[:, 0:n], lhsT=Wg[:, k, dch * 128:(dch + 1) * 128], rhs=self.hbT[:, k, c0:c1], start=(k == 0), stop=(k == 7)),
                         reads=[wtg, self.t_hb[gi]], writes=[self.pst[ba]])
                for k in range(2):
                    c.op("pe", lambda e, k=k, pb=pb, n=n, c0=c0, c1=c1, dch=dch: e.matmul(pb[:, 0:n], lhsT=We[:, k, dch * 128:(dch + 1) * 128], rhs=self.pT[:, k, c0:c1], start=(k == 0), stop=(k == 1)),
                         reads=[wte, t_pT[gi]], writes=[self.pst[bb]])
                c.op("act", lambda e, s=s, pa=pa, n=n: e.activation(sgt[s][:, 0:n], pa[:, 0:n], AF.Sigmoid), reads=[self.pst[ba]], writes=[t_sg[s]])
                c.op("dve", lambda e, s=s, pb=pb, n=n: e.tensor_tensor(sgt[s][:, 0:n], sgt[s][:, 0:n], pb[:, 0:n], ALU.mult), reads=[self.pst[bb], t_sg[s]], writes=[t_sg[s]])
                c.op("pool", lambda e, s=s, n=n, dch=dch, c0=c0, c1=c1: e.tensor_tensor(self.hT[:, dch, c0:c1], self.hT[:, dch, c0:c1], sgt[s][:, 0:n], ALU.add),
                     reads=[t_sg[s], self.t_hT[dch][gi]], writes=[self.t_hT[dch][gi]])
        if self.next_unit is not None:
            self.prefetch("swa", self.next_unit[0])
        c.release([t_pt] + t_sg + t_pT + self.t_hb)

    def store_y(self, MOFF):
        c, T = self.c, self.T
        base = 18432 + 4608 + 4096
        ost = [self.av(base + i * 4096, [128, 1024], F32) for i in range(2)]
        t_o = [c.new_tok(), c.new_tok()]
        for j in range(T):
            s = j % 2
            gi = j // 4
            cs = slice(j * 128, (j + 1) * 128)
            for half in range(2):
                bank = 4 + half
                ps = self.psb(bank)
                for kk in range(4):
                    k = half * 4 + kk
                    c.op("pe", lambda e, ps=ps, kk=kk, k=k, cs=cs: e.transpose(ps[:, kk * 128:(kk + 1) * 128], self.hT[:, k, cs], self.cf("identf")),
                         reads=[self.t_hT[k][gi], self.CT], writes=[self.pst[bank]])
                if half == 0:
                    c.op("act", lambda e, ps=ps, s=s: e.copy(ost[s][:, 0:512], ps[:, :]), reads=[self.pst[bank]], writes=[t_o[s]])
                else:
                    c.op("dve", lambda e, ps=ps, s=s: e.tensor_copy(ost[s][:, 512:1024], ps[:, :]), reads=[self.pst[bank]], writes=[t_o[s]])
            c.dma("sp", self.o["y"].ap()[self.row0 + j * 128:self.row0 + (j + 1) * 128, :], ost[s], reads=[t_o[s]], key="oy%d" % s)
        c.release(t_o + [t for row in self.t_hT for t in row])


_CACHE = {}


def _get_nc():
    if "nc" not in _CACHE:
        k = K()
        _CACHE["nc"] = k.build()
        _CACHE["k"] = k
    return _CACHE["nc"]


def kernel(x_prompt, x_sample, state_ssm, state_conv, state_gla, cache_swa_k, cache_swa_v, p_prompt, p_sample,
           rel_bias, norm_w, w_in, conv_w, conv_b, dt_bias, a_log, d_skip, ssd_norm_w, gla_w_gk, gla_b_gk,
           gla_norm_w, q_norm_w, k_norm_w, attn_sinks, w_out, w_pe, w_pg):
    f = lambda a: np.ascontiguousarray(np.asarray(a, dtype=np.float32))
    x_prompt, x_sample, state_ssm, state_conv, state_gla = map(f, (x_prompt, x_sample, state_ssm, state_conv, state_gla))
    cache_swa_k, cache_swa_v, p_prompt, p_sample = map(f, (cache_swa_k, cache_swa_v, p_prompt, p_sample))
    w_in = f(w_in)
    sq0 = 2072
    perm = np.concatenate([
        sq0 + np.concatenate([np.arange(0, 64), np.arange(128, 192), np.arange(64, 128), np.arange(192, 256)]),
        np.arange(2328, 2456), np.arange(2456, 2584), np.arange(2584, 2840),
        np.arange(1288, 1416), np.arange(1416, 1544), np.arange(2056, 2072), np.arange(1416, 1544), np.arange(1544, 1800), np.arange(1800, 2056),
        np.arange(512, 1280), np.arange(0, 512), np.arange(1280, 1288)])
    assert perm.shape[0] == 2968
    w_in_p = np.ascontiguousarray(w_in[:, :, perm])
    cstf, cstb, oh = make_consts()
    rep8 = np.concatenate([f(dt_bias), f(a_log), f(d_skip)], axis=1)
    wgk = np.concatenate([f(gla_w_gk), f(gla_b_gk)[:, None, :]], axis=1)
    hdw = np.concatenate([np.tile(f(q_norm_w), (1, 4)), np.tile(f(k_norm_w), (1, 2)), f(gla_norm_w)], axis=1)
    shared = dict(rel_bias=f(rel_bias), norm_w=f(norm_w), w_in=w_in_p, conv_w=f(conv_w), conv_b=f(conv_b), rep8=rep8,
                  ssd_norm_w=f(ssd_norm_w), wgk=np.ascontiguousarray(wgk), hdw=np.ascontiguousarray(hdw), sinks=f(attn_sinks),
                  w_out=f(w_out), w_pe=f(w_pe), w_pg=f(w_pg), cstf=cstf, cstb=cstb, oh=oh)
    in_maps = []
    for core in range(8):
        sl = slice(16 * core, 16 * core + 16)
        m = dict(shared)
        m["x_tok"] = np.ascontiguousarray(np.concatenate([x_prompt[core], x_sample[sl].reshape(128, 1024)], axis=0))
        m["p_tok"] = np.ascontiguousarray(np.concatenate([p_prompt[:, core], p_sample[:, sl].reshape(2, 128, 256)], axis=1))
        m["st_ssm"] = np.ascontiguousarray(state_ssm[:, sl])
        m["st_conv"] = np.ascontiguousarray(state_conv[:, sl])
        m["st_gla"] = np.ascontiguousarray(state_gla[:, sl])
        m["ck"] = np.ascontiguousarray(cache_swa_k[:, sl].reshape(2, 16, 128, 128))
        m["cv"] = np.ascontiguousarray(cache_swa_v[:, sl].reshape(2, 16, 128, 128))
        in_maps.append(m)
    nc = _get_nc()
    res = run_bass_kernel_spmd(nc, in_maps, core_ids=list(range(8)))
    R = res.results
    cat = lambda name, ax: np.concatenate([np.asarray(r[name]) for r in R], axis=ax)
    stk = lambda name: np.stack([np.asarray(r[name]) for r in R], axis=1)
    y_all = np.stack([np.asarray(r["y_tok"]) for r in R], axis=0)
    y_prompt = np.ascontiguousarray(y_all[:, :2048])
    y_sample = np.ascontiguousarray(y_all[:, 2048:].reshape(128, 8, 1024))
    ssm_p = stk("ssm_p")
    conv_p = stk("conv_p")
    gla_p = stk("gla_p")
    swak_p = stk("swak_p").reshape(2, 8, 128, 2, 64)
    swav_p = stk("swav_p").reshape(2, 8, 128, 2, 64)
    ssm_s = cat("ssm_s", 1)
    conv_s = cat("conv_s", 1)
    gla_s = cat("gla_s", 1)
    swak_s = cat("swak_s", 1).reshape(2, 128, 128, 2, 64)
    swav_s = cat("swav_s", 1).reshape(2, 128, 128, 2, 64)
    outs = (y_prompt, y_sample, ssm_p, conv_p, gla_p, swak_p, swav_p, ssm_s, conv_s, gla_s, swak_s, swav_s)
    if DEBUG:
        _CACHE["dbg"] = {nm: [np.asarray(r[nm]) for r in R] for nm in DEBUG}
    return tuple(np.ascontiguousarray(o.astype(np.float32)) for o in outs)
```

```python
import numpy as np, math
from contextlib import ExitStack
import concourse.bass as bass
import concourse.mybir as mybir
from concourse.bass_utils import run_bass_kernel_spmd

F32 = mybir.dt.float32
BF16 = mybir.dt.bfloat16
AF = mybir.ActivationFunctionType
ALU = mybir.AluOpType
AX = mybir.AxisListType


NO_ELIDE = False


class Tok:
    __slots__ = ("w", "r", "const", "excl")

    def __init__(self, const=False, excl=False):
        self.w = None
        self.r = {}
        self.const = const
        self.excl = excl


class Op:
    __slots__ = ("eng", "fn", "deps", "sig", "val", "kind", "key", "seq")
    _n = 0

    def __init__(self, eng, fn, kind="op", key=None):
        Op._n += 1
        self.seq = Op._n
        self.eng = eng
        self.fn = fn
        self.deps = {}
        self.sig = False
        self.val = None
        self.kind = kind
        self.key = key


class Ctx:
    ENGS = ("pe", "act", "dve", "pool", "sp")

    def __init__(self, nc, es):
        self.nc = nc
        self.es = es
        self.sem = {k: es.enter_context(nc.semaphore("s_" + k)) for k in self.ENGS}
        self.ops = {k: [] for k in self.ENGS}
        self.dsem = {}
        self.dcnt = {}
        self.free_deps = {}

    def _skey(self, op):
        return op.key if op.kind == "dma" else op.eng

    def _adddep(self, op, p):
        if p is None or p is op:
            return
        k = self._skey(p)
        q = op.deps.get(k)
        if q is None or q.seq < p.seq:
            op.deps[k] = p

    def _track(self, op, reads, writes):
        k0 = self._skey(op)
        for t in reads:
            self._adddep(op, t.w)
            if t.excl:
                for kk, p in t.r.items():
                    if kk != k0:
                        self._adddep(op, p)
        for t in writes:
            self._adddep(op, t.w)
            for p in t.r.values():
                self._adddep(op, p)
        k = self._skey(op)
        for t in reads:
            if not t.const:
                t.r[k] = op
        for t in writes:
            t.w = op
            t.r = {}

    def op(self, eng, fn, reads=(), writes=()):
        o = Op(eng, fn)
        self._track(o, reads, writes)
        self.ops[eng].append(o)
        return o

    def dma(self, q, out_ap, in_ap, reads=(), writes=(), key=None, nc_ok=False):
        if key not in self.dsem:
            self.dsem[key] = self.es.enter_context(self.nc.semaphore("d_" + key))
            self.dcnt[key] = 0
        if nc_ok:
            fn = lambda e: e.dma_start(out=out_ap, in_=in_ap, allow_slow_non_contiguous=True)
        else:
            fn = lambda e: e.dma_start(out=out_ap, in_=in_ap)
        o = Op(q, fn, kind="dma", key=key)
        self._track(o, reads, writes)
        self.dcnt[key] += 16
        o.val = self.dcnt[key]
        self.ops[q].append(o)
        return o

    def new_tok(self, const=False):
        t = Tok(const)
        t.r = dict(self.free_deps)
        return t

    def release(self, toks):
        for t in toks:
            for p in list(t.r.values()) + ([t.w] if t.w is not None else []):
                k = self._skey(p)
                q = self.free_deps.get(k)
                if q is None or q.seq < p.seq:
                    self.free_deps[k] = p

    def emit(self, final_eng="sp"):
        for e in self.ENGS:
            for o in self.ops[e]:
                for p in o.deps.values():
                    if p.kind == "op" and not (p.eng == "pe" and o.eng == "pe" and o.kind == "op"):
                        p.sig = True
        for e in self.ENGS:
            n = 0
            for o in self.ops[e]:
                if o.kind == "op" and o.sig:
                    n += 1
                    o.val = n
        fin = [(self.dsem[k], self.dcnt[k]) for k in self.dsem]
        nsig = sum(1 for e in self.ENGS for o in self.ops[e] if o.sig)
        nops = sum(len(self.ops[e]) for e in self.ENGS)
        print("ops", {e: len(self.ops[e]) for e in self.ENGS}, "signals", nsig, "dma keys", len(self.dsem))

        class _PEProxy:
            def __init__(self, e):
                self._e = e

            def __getattr__(self, n):
                return getattr(self._e, n)

            def matmul(self, *a, **k):
                k.setdefault("skip_group_check", True)
                return self._e.matmul(*a, **k)

        def run(ename, e):
            if ename == "pe":
                e = _PEProxy(e)
            waited = {}
            for o in self.ops[ename]:
                for p in o.deps.values():
                    if p.kind == "op":
                        if p.eng == "pe" and ename == "pe" and o.kind == "op":
                            continue
                        sem = self.sem[p.eng]
                        sk = p.eng
                    else:
                        sem = self.dsem[p.key]
                        sk = p.key
                    if NO_ELIDE or waited.get(sk, 0) < p.val:
                        waited[sk] = max(waited.get(sk, 0), p.val)
                        e.wait_ge(sem, p.val)
                ins = o.fn(e)
                if o.kind == "dma":
                    ins.then_inc(self.dsem[o.key], 16)
                elif o.sig:
                    ins.then_inc(self.sem[ename], 1)
            if ename == final_eng:
                for (s, v) in fin:
                    e.wait_ge(s, v)

        with self.nc.Block() as block:
            @block.tensor
            def _(e):
                run("pe", e)

            @block.scalar
            def _(e):
                run("act", e)

            @block.vector
            def _(e):
                run("dve", e)

            @block.gpsimd
            def _(e):
                run("pool", e)

            @block.sync
            def _(e):
                run("sp", e)


NEGB = -240000.0
EPS = 1e-6
ENABLE = {"swa": True, "gla": True, "ssd": True}
PIPE = {"swa": True, "gla": True, "ssd": True}
NOY1 = False
DEBUG = {}

CF = dict(identf=(0, 128), triP=(128, 256), triS=(256, 384), supP=(384, 512), supS=(512, 640), onesf=(640, 768),
          sameS=(768, 896), smtok=(896, 912), bd2=(912, 914), bd4=(914, 918))
NF = 920
CB = dict(ident=(0, 128), triP=(128, 256), triS=(256, 384), ntriP=(384, 512), ntriS=(512, 640), negP=(640, 768),
          negS=(768, 896), ones=(896, 1024), smbt=(1024, 3072), smtok=(3072, 3088), bd2=(3088, 3090), bd4=(3090, 3094))
NB = 3096


def make_consts():
    k = np.arange(128)[:, None]
    t = np.arange(128)[None, :]
    same = (k // 8 == t // 8)
    f = np.zeros((128, NF), np.float32)
    b = np.zeros((128, NB), np.float32)

    def put(arr, d, name, val):
        a, e = d[name]
        arr[:, a:e] = val

    put(f, CF, "identf", (k == t))
    put(f, CF, "triP", (k <= t))
    put(f, CF, "triS", same & (k <= t))
    put(f, CF, "supP", (k > t))
    put(f, CF, "supS", same & (k > t))
    put(f, CF, "onesf", 1.0)
    put(f, CF, "sameS", same)
    put(f, CF, "smtok", (k // 8 == np.arange(16)[None, :]))
    put(f, CF, "bd2", (k // 64 == np.arange(2)[None, :]))
    put(f, CF, "bd4", (k // 32 == np.arange(4)[None, :]))
    put(b, CB, "ident", (k == t))
    put(b, CB, "triP", (k <= t))
    put(b, CB, "triS", same & (k <= t))
    put(b, CB, "ntriP", -1.0 * (k <= t))
    put(b, CB, "ntriS", -1.0 * (same & (k <= t)))
    put(b, CB, "negP", np.where(k <= t, 0.0, -30000.0))
    put(b, CB, "negS", np.where(same & (k <= t), 0.0, -30000.0))
    put(b, CB, "ones", 1.0)
    smbt = (np.arange(16)[:, None] == (np.arange(128)[None, :] // 8)).astype(np.float32).reshape(1, 2048)
    put(b, CB, "smbt", np.broadcast_to(smbt, (128, 2048)))
    put(b, CB, "smtok", (k // 8 == np.arange(16)[None, :]))
    put(b, CB, "bd2", (k // 64 == np.arange(2)[None, :]))
    put(b, CB, "bd4", (k // 32 == np.arange(4)[None, :]))
    oh = np.zeros((33, 384), np.float32)
    for i in range(384):
        dist = i - 127
        if 0 <= dist < 128:
            n = dist
            if n < 16:
                bk = n
            else:
                nf = np.float32(max(n, 1))
                v = np.log(nf / np.float32(16)) / np.float32(math.log(128 / 16)) * np.float32(16)
                bk = min(16 + int(np.int32(v)), 31)
            oh[bk, i] = 1.0
        else:
            oh[32, i] = 1.0
    return f, b, oh


class K:
    def __init__(self):
        self.nc = bass.Bass("TRN2", target_bir_lowering=False)
        self.es = ExitStack()

    def din(self, name, shape):
        return self.nc.dram_tensor(name, list(shape), F32, kind="ExternalInput")

    def dout(self, name, shape):
        return self.nc.dram_tensor(name, list(shape), F32, kind="ExternalOutput")

    def sb(self, name, shape, dt):
        return self.es.enter_context(self.nc.sbuf_tensor(name, list(shape), dt))

    def av(self, off, shape, dt):
        n = int(np.prod(shape[1:]))
        nb = n * (4 if dt == F32 else 2)
        assert off % 4 == 0 and off + nb <= self.ABYTES, (off, nb, self.ABYTES)
        ap = self.arena[:, off // 2:(off + nb) // 2]
        if dt == F32:
            ap = ap.bitcast(F32)
        if len(shape) == 3:
            ap = ap.rearrange("p (a b) -> p a b", a=shape[1])
        elif len(shape) == 4:
            ap = ap.rearrange("p (a b c) -> p a b c", a=shape[1], b=shape[2])
        elif len(shape) == 5:
            ap = ap.rearrange("p (a b c d) -> p a b c d", a=shape[1], b=shape[2], c=shape[3])
        return ap

    def cf(self, name):
        a, e = CF[name]
        return self.cstf[:, a:e]

    def cb(self, name):
        a, e = CB[name]
        return self.cstb[:, a:e]

    def psb(self, bank, dt=F32):
        t = self.psum[bank]
        ap = t[:, :]
        if dt == BF16:
            ap = ap.bitcast(BF16)
        return ap

    def build(self):
        nc, es = self.nc, self.es
        c = self.c = Ctx(nc, es)
        d = self.d = {}
        d["x"] = self.din("x_tok", [2176, 1024])
        d["p"] = self.din("p_tok", [2, 2176, 256])
        d["st_ssm"] = self.din("st_ssm", [2, 16, 8, 64, 64])
        d["st_conv"] = self.din("st_conv", [2, 16, 3, 768])
        d["st_gla"] = self.din("st_gla", [2, 16, 4, 32, 64])
        d["ck"] = self.din("ck", [2, 16, 128, 128])
        d["cv"] = self.din("cv", [2, 16, 128, 128])
        d["rel_bias"] = self.din("rel_bias", [32, 4])
        d["norm_w"] = self.din("norm_w", [2, 1024])
        d["w_in"] = self.din("w_in", [2, 1024, 2968])
        d["conv_w"] = self.din("conv_w", [2, 4, 768])
        d["conv_b"] = self.din("conv_b", [2, 768])
        d["rep8"] = self.din("rep8", [2, 24])
        d["ssd_norm_w"] = self.din("ssd_norm_w", [2, 512])
        d["wgk"] = self.din("wgk", [2, 17, 128])
        d["hdw"] = self.din("hdw", [2, 7 * 64])
        d["sinks"] = self.din("sinks", [2, 4])
        d["w_out"] = self.din("w_out", [2, 1024, 1024])
        d["w_pe"] = self.din("w_pe", [2, 256, 1024])
        d["w_pg"] = self.din("w_pg", [2, 1024, 1024])
        d["cstf"] = self.din("cstf", [128, NF])
        d["cstb"] = self.din("cstb", [128, NB])
        d["oh"] = self.din("oh", [33, 384])
        o = self.o = {}
        o["y"] = self.dout("y_tok", [2176, 1024])
        o["ssm_p"] = self.dout("ssm_p", [2, 8, 64, 64])
        o["conv_p"] = self.dout("conv_p", [2, 3, 768])
        o["gla_p"] = self.dout("gla_p", [2, 4, 32, 64])
        o["swak_p"] = self.dout("swak_p", [2, 128, 128])
        o["swav_p"] = self.dout("swav_p", [2, 128, 128])
        o["ssm_s"] = self.dout("ssm_s", [2, 16, 8, 64, 64])
        o["conv_s"] = self.dout("conv_s", [2, 16, 3, 768])
        o["gla_s"] = self.dout("gla_s", [2, 16, 4, 32, 64])
        o["swak_s"] = self.dout("swak_s", [2, 16, 128, 128])
        o["swav_s"] = self.dout("swav_s", [2, 16, 128, 128])
        self.scr = nc.dram_tensor("scr_bias", [128, 1536], F32)
        self.scr_ac = nc.dram_tensor("scr_acum", [128, 72], F32)
        self.t_scr_ac = Tok()
        for nm, shp in DEBUG.items():
            o[nm] = self.dout(nm, shp)

        self.hT = self.sb("hT", [128, 8, 1152], F32)
        self.uT = self.sb("uT", [128, 8, 1152], BF16)
        self.pT = self.sb("pT", [128, 2, 1152], BF16)
        self.W = [self.sb("W0", [128, 8, 1024], BF16), self.sb("W1", [128, 8, 1024], BF16)]
        self.Wt = [Tok(), Tok()]
        self.wi = 0
        self.cstf = self.sb("cstf_sb", [128, NF], F32)
        self.cstb = self.sb("cstb_sb", [128, NB], BF16)
        self.CT = Tok(const=True)
        self.biasP = [self.sb("biasP_hi", [128, 2, 2, 2, 128], BF16), self.sb("biasP_lo", [128, 2, 2, 2, 128], BF16)]
        self.biasS = [self.sb("biasS_hi", [128, 2, 2, 128], BF16), self.sb("biasS_lo", [128, 2, 2, 128], BF16)]
        self.biasC = [self.sb("biasC_hi", [128, 16, 2, 2, 8], BF16), self.sb("biasC_lo", [128, 16, 2, 2, 8], BF16)]
        self.nwT = self.sb("nwT", [128, 8], F32)
        self.cwT = self.sb("cwT", [128, 6, 4], F32)
        self.cbT = self.sb("cbT", [128, 6], F32)
        self.rep8 = self.sb("rep8_sb", [128, 24], F32)
        self.arep = self.sb("arep", [128, 8], F32)
        self.ssdnw = self.sb("ssdnw", [128, 512], F32)
        self.hdw = self.sb("hdw_sb", [128, 7, 64], F32)
        self.esink = self.sb("esink", [128, 4], F32)
        self.wgkf = self.sb("wgkf", [17, 128], F32)
        self.wgk = self.sb("wgkb", [17, 128], BF16)
        self.PT_ = Tok()
        self.Esel = self.sb("Esel", [128, 9, 128], BF16)
        self.S = []
        for l in range(2):
            st = dict(
                ssd=self.sb("Sssd%d" % l, [128, 512], F32), ssd_b=self.sb("Sssdb%d" % l, [128, 512], BF16),
                gla=self.sb("Sgla%d" % l, [128, 256], F32), gla_b=self.sb("Sglab%d" % l, [128, 256], BF16),
                ctail=self.sb("ctail%d" % l, [128, 6, 3], BF16),
                KTz=[self.sb("KTz%d_%d" % (l, i), [128, 2, 128], BF16) for i in range(3)],
                vext=[self.sb("vext%d_%d" % (l, i), [128, 2, 65], BF16) for i in range(4)],
                t_ssd=Tok(), t_gla=Tok(), t_ctail=Tok(), t_kv=[Tok(), Tok(), Tok()], t_vx=[Tok(), Tok(), Tok(), Tok()],
            )
            self.S.append(st)
        self.psum = [es.enter_context(nc.psum_tensor("ps%d" % i, [128, 512], F32)) for i in range(8)]
        self.pst = [Tok(excl=True) for _ in range(8)]
        self.ABYTES = 78 * 1024
        self.arena = self.sb("arena", [128, self.ABYTES // 2], BF16)

        self.marks = []
        self.wq = {}
        self.setup()
        import os
        self.trunc = os.environ.get("TRUNC")
        order = [(0, 0), (1, 0), (0, 1), (1, 1)]
        for ui, (l, hf) in enumerate(order):
            self.next_unit = order[ui + 1] if ui + 1 < len(order) else None
            self.unit(l, hf)
            if self.trunc:
                break
        c.emit()
        return nc

    def setup(self):
        c, d = self.c, self.d
        c.dma("sp", self.cstf[:, :], d["cstf"].ap(), writes=[self.CT], key="cst")
        c.dma("pool", self.cstb[:, :], d["cstb"].ap(), writes=[self.CT], key="cstb")
        A = Tok()
        for l in range(2):
            st = self.S[l]
            c.op("pool", lambda e, st=st: e.memset(st["ssd"][:, :], 0.0), writes=[st["t_ssd"]])
            c.op("pool", lambda e, st=st: e.memset(st["ssd_b"][:, :], 0.0), writes=[st["t_ssd"]])
            c.op("pool", lambda e, st=st: e.memset(st["gla"][:, :], 0.0), writes=[st["t_gla"]])
            c.op("pool", lambda e, st=st: e.memset(st["gla_b"][:, :], 0.0), writes=[st["t_gla"]])
            c.op("pool", lambda e, st=st: e.memset(st["ctail"][:, :, :], 0.0), writes=[st["t_ctail"]])
            for i in range(4):
                c.op("pool", lambda e, st=st, i=i: e.memset(st["vext"][i][:, :, :], 1.0), writes=[st["t_vx"][i]])
        c.op("dve", lambda e: e.tensor_tensor(self.Esel[:, :, :], self.cb("ident")[:, 0:9].unsqueeze(2).to_broadcast([128, 9, 128]),
                                              self.cb("ident")[:, 32:41].unsqueeze(2).to_broadcast([128, 9, 128]), ALU.add), reads=[self.CT], writes=[self.CT])
        rb = self.av(0, [128, 4], F32)
        ohs = self.av(16, [128, 384], F32)
        R = self.av(16 + 1536, [128, 384, 4], F32)
        Frep = self.av(16 + 1536 + 6144, [128, 1536], F32)
        Tt = self.av(16 + 1536 + 6144 + 6144, [128, 256, 4], F32)
        tmp = self.av(16 + 1536 + 6144 + 6144 + 4096, [128, 2048], F32)
        nseq = self.av(16 + 1536 + 6144 + 6144 + 4096 + 8192, [128, 128], F32)
        c.op("dve", lambda e: e.memset(rb[0:64, :], NEGB / 8.0), writes=[A])
        c.dma("sp", rb[0:32, :], d["rel_bias"].ap(), writes=[A], key="su1")
        c.dma("sp", ohs[0:33, :], d["oh"].ap(), writes=[A], key="su2")
        c.op("dve", lambda e: e.tensor_scalar_mul(rb[0:33, :], rb[0:33, :], 8.0), reads=[A], writes=[A])
        c.op("dve", lambda e: e.tensor_tensor(R[0:33, :, :], ohs[0:33, :].unsqueeze(2).to_broadcast([33, 384, 4]),
                                              rb[0:33, :].unsqueeze(1).to_broadcast([33, 384, 4]), ALU.mult),
             reads=[A], writes=[A])
        Rf = R[0:33, :, :].rearrange("p a b -> p (a b)")
        for i in range(3):
            ps = self.psb(i)
            c.op("pe", lambda e, i=i, ps=ps: e.matmul(ps[:, :], lhsT=self.cf("onesf")[0:33, :], rhs=Rf[:, i * 512:(i + 1) * 512],
                                                      start=True, stop=True), reads=[A, self.CT], writes=[self.pst[i]])
            c.op("act", lambda e, i=i, ps=ps: e.copy(Frep[:, i * 512:(i + 1) * 512], ps[:, :]), reads=[self.pst[i]], writes=[A])
        SC = Tok()
        c.dma("sp", self.scr.ap(), Frep, reads=[A], writes=[SC], key="su3")
        c.dma("sp", Tt.rearrange("p a b -> p (a b)"), bass.AP(self.scr, 127 * 4, [[1532, 128], [1, 1024]]), reads=[SC], writes=[A], key="su4")

        def hilo(dst, src_ap, shape_desc, tmp_ap):
            c.op("dve", lambda e, dst=dst, src_ap=src_ap: e.tensor_copy(dst[0], src_ap), reads=[A], writes=[A])
            c.op("dve", lambda e, dst=dst, src_ap=src_ap, tmp_ap=tmp_ap: e.tensor_tensor(tmp_ap, src_ap, dst[0], ALU.subtract), reads=[A], writes=[A])
            c.op("dve", lambda e, dst=dst, tmp_ap=tmp_ap: e.tensor_copy(dst[1], tmp_ap), reads=[A], writes=[A])

        for blk, j0 in ((0, 128), (1, 0)):
            src = Tt[:, j0:j0 + 128, :].rearrange("k q (g r) -> k g r q", g=2)
            t4 = tmp[:, 0:512].rearrange("k (g r q) -> k g r q", g=2, r=2)
            hilo([self.biasP[0][:, :, blk, :, :], self.biasP[1][:, :, blk, :, :]], src, None, t4)
        c.op("dve", lambda e: e.tensor_scalar(nseq, self.cf("sameS"), -NEGB, NEGB, ALU.mult, ALU.add), reads=[self.CT], writes=[A])
        t4 = tmp[:, 512:1024].rearrange("k (g r q) -> k g r q", g=2, r=2)
        src = Tt[:, 0:128, :].rearrange("k q (g r) -> k g r q", g=2)
        c.op("dve", lambda e, t4=t4, src=src: e.tensor_tensor(t4, src, nseq.unsqueeze(1).unsqueeze(1).to_broadcast([128, 2, 2, 128]), ALU.add),
             reads=[A], writes=[A])
        t4b = tmp[:, 1024:1536].rearrange("k (g r q) -> k g r q", g=2, r=2)
        hilo([self.biasS[0][:, :, :, :], self.biasS[1][:, :, :, :]], t4, None, t4b)
        src = Tt[:, 128:136, :].rearrange("k i (g r) -> k g r i", g=2)
        t5 = tmp[:, 1536:1536 + 32].rearrange("k (g r i) -> k g r i", g=2, r=2)
        c.op("dve", lambda e, t5=t5, src=src: e.tensor_copy(t5, src), reads=[A], writes=[A])
        srcb = t5.unsqueeze(1).to_broadcast([128, 16, 2, 2, 8])
        t6 = tmp[:, 0:512].rearrange("k (b g r i) -> k b g r i", b=16, g=2, r=2)
        hilo([self.biasC[0][:, :, :, :, :], self.biasC[1][:, :, :, :, :]], srcb, None, t6)
        c.release([A])
        self.BT = Tok(const=True)
        self.BT.w = A.w

    def wload(self, src_ap, ncols, nk=8):
        i = self.wi
        self.wi ^= 1
        W, t = self.W[i], self.Wt[i]
        self.c.dma("pool", W[:, 0:nk, 0:ncols], src_ap.rearrange("(k p) n -> p k n", p=128), writes=[t], key="w%d" % i)
        return W, t

    def wspec(self, key, l):
        d = self.d
        return {"swa": (d["w_in"].ap()[l][:, 0:768], 768, 8), "gla": (d["w_in"].ap()[l][:, 768:1680], 912, 8),
                "ssd_x": (d["w_in"].ap()[l][:, 1680:2448], 768, 8), "ssd_z": (d["w_in"].ap()[l][:, 2448:2968], 520, 8),
                "w_out": (d["w_out"].ap()[l], 1024, 8), "w_pg": (d["w_pg"].ap()[l], 1024, 8), "w_pe": (d["w_pe"].ap()[l], 1024, 2)}[key]

    def prefetch(self, key, l):
        src, ncols, nk = self.wspec(key, l)
        self.wq[(key, l)] = self.wload(src, ncols, nk)

    def getw(self, key, l):
        if (key, l) not in self.wq:
            self.prefetch(key, l)
        return self.wq.pop((key, l))

    def load_params(self, l):
        c, d = self.c, self.d
        P = self.PT_
        ncq = dict(nc_ok=True)
        c.dma("sp", self.nwT[:, :], d["norm_w"].ap()[l].rearrange("(k p) -> p k", p=128), writes=[P], key="pa", **ncq)
        for t in range(4):
            c.dma("sp", self.cwT[:, :, t], d["conv_w"].ap()[l][t].rearrange("(k p) -> p k", p=128), writes=[P], key="pa", **ncq)
        c.dma("sp", self.cbT[:, :], d["conv_b"].ap()[l].rearrange("(k p) -> p k", p=128), writes=[P], key="pa", **ncq)

        def bc(name, n):
            t = d[name]
            return bass.AP(t, l * n, [[0, 128], [1, n]])

        c.dma("sp", self.rep8[:, :], bc("rep8", 24), writes=[P], key="pa")
        c.dma("sp", self.ssdnw[:, :], bc("ssd_norm_w", 512), writes=[P], key="pa")
        c.dma("sp", self.hdw[:, :, :].rearrange("p a b -> p (a b)"), bc("hdw", 448), writes=[P], key="pa")
        c.dma("sp", self.esink[:, :], bc("sinks", 4), writes=[P], key="pa")
        c.dma("sp", self.wgkf[:, :], d["wgk"].ap()[l], writes=[P], key="pa")
        c.op("act", lambda e: e.activation(self.esink[:, :], self.esink[:, :], AF.Exp), reads=[P], writes=[P])
        c.op("act", lambda e: e.activation(self.arep[:, :], self.rep8[:, 8:16], AF.Exp), reads=[P], writes=[P])
        c.op("dve", lambda e: e.tensor_scalar_mul(self.arep[:, :], self.arep[:, :], -1.0), reads=[P], writes=[P])
        c.op("dve", lambda e: e.tensor_copy(self.wgk[:, :], self.wgkf[:, :]), reads=[P], writes=[P])

    def unit(self, l, hf):
        c, d = self.c, self.d
        row0, NT, T, has_s = [(0, 1024, 8, False), (1024, 1152, 9, True)][hf]
        self.l, self.hf, self.NT, self.T, self.has_s, self.row0 = l, hf, NT, T, has_s, row0
        groups = [(g * 512, min(NT, (g + 1) * 512)) for g in range((NT + 511) // 512)]
        self.groups = groups
        mk = lambda nm: self.marks.append((l, hf, nm, len(c.ops["pe"]), len(c.ops["dve"]), len(c.ops["act"])))
        self.mk = mk
        mk("start")
        if ("swa", l) not in self.wq:
            self.prefetch("swa", l)
        self.load_params(l)
        if l == 0:
            self.load_x()
        mk("rmsnorm")
        self.rmsnorm()
        self.ytok = self.av(0, [128, 9, 512], BF16)
        self.t_ytok = c.new_tok()
        MOFF = 9216
        if not (ENABLE["swa"] and ENABLE["gla"]):
            c.op("pool", lambda e: e.memset(self.ytok[:, :, :], 0.0), writes=[self.t_ytok])
        if ENABLE["swa"]:
            mk("swa")
            self.swa(MOFF)
            if self.trunc:
                return
        if ENABLE["gla"]:
            mk("gla")
            self.gla(MOFF)
        if ENABLE["ssd"]:
            mk("ssd")
            self.ssd(MOFF)
        else:
            c.op("pool", lambda e: e.memset(self.uT[:, 0:4, 0:NT], 0.0), writes=self.t_uTg)
        mk("ytr")
        if ENABLE["ssd"]:
            c.release([self.t_ytok])
        else:
            self.y_transposes()
        mk("out_proj")
        self.out_proj(MOFF)
        mk("ple")
        self.ple(MOFF)
        if l == 1:
            mk("store_y")
            self.store_y(MOFF)
        mk("end")

    def load_x(self):
        c, d = self.c, self.d
        self.t_hT = [[c.new_tok() for _ in range(3)] for _ in range(8)]
        xs = [self.av(i * 4096, [128, 1024], F32) for i in range(2)]
        xt = [c.new_tok() for _ in range(2)]
        for j in range(self.T):
            s = j % 2
            g = j // 4
            c.dma("sp", xs[s], d["x"].ap()[self.row0 + j * 128:self.row0 + (j + 1) * 128, :], writes=[xt[s]], key="xs%d" % s)
            for half in range(2):
                bank = half
                ps = self.psb(bank)
                for kk in range(4):
                    k = half * 4 + kk
                    c.op("pe", lambda e, ps=ps, kk=kk, k=k, s=s: e.transpose(ps[:, kk * 128:(kk + 1) * 128], xs[s][:, k * 128:(k + 1) * 128],
                                                                              self.cf("identf")), reads=[xt[s], self.CT], writes=[self.pst[bank]])
                dst = self.hT[:, half * 4:(half + 1) * 4, j * 128:(j + 1) * 128]
                src = ps[:, :].rearrange("p (a b) -> p a b", a=4)
                eng = "act" if half == 0 else "dve"
                if eng == "act":
                    c.op("act", lambda e, dst=dst, src=src: e.copy(dst, src), reads=[self.pst[bank]], writes=[self.t_hT[k][g] for k in range(half * 4, half * 4 + 4)])
                else:
                    c.op("dve", lambda e, dst=dst, src=src: e.tensor_copy(dst, src), reads=[self.pst[bank]], writes=[self.t_hT[k][g] for k in range(half * 4, half * 4 + 4)])
        c.release(xt)

    def rmsnorm(self):
        c = self.c
        if hasattr(self, "t_uTg"):
            c.release(self.t_uTg)
        self.t_uTg = [c.new_tok() for _ in self.groups]
        sq = [self.av(i * 8192, [128, 8, 512], BF16) for i in range(2)]
        rs = [self.av(16384 + i * 2048, [128, 512], F32) for i in range(2)]
        tq = [c.new_tok() for _ in range(2)]
        tr = [c.new_tok() for _ in range(2)]
        for gi, (c0, c1) in enumerate(self.groups):
            n = c1 - c0
            s = gi % 2
            c.op("act", lambda e, s=s, c0=c0, c1=c1, n=n: e.activation(sq[s][:, :, 0:n], self.hT[:, :, c0:c1], AF.Square),
                 reads=[self.t_hT[k][gi] for k in range(8)], writes=[tq[s]])
            ps = self.psb(2)
            for k in range(8):
                c.op("pe", lambda e, k=k, s=s, n=n, ps=ps: e.matmul(ps[:, 0:n], lhsT=self.cb("ones"), rhs=sq[s][:, k, 0:n], start=(k == 0), stop=(k == 7)),
                     reads=[tq[s], self.CT], writes=[self.pst[2]])
            c.op("act", lambda e, s=s, n=n, ps=ps: e.activation(rs[s][:, 0:n], ps[:, 0:n], AF.Ln, bias=EPS, scale=1.0 / 1024), reads=[self.pst[2]], writes=[tr[s]])
            c.op("act", lambda e, s=s, n=n: e.activation(rs[s][:, 0:n], rs[s][:, 0:n], AF.Exp, scale=-0.5), reads=[tr[s]], writes=[tr[s]])
            for k in range(8):
                eng = "dve"
                c.op(eng, lambda e, k=k, s=s, n=n, c0=c0, c1=c1: e.scalar_tensor_tensor(self.uT[:, k, c0:c1], self.hT[:, k, c0:c1], self.nwT[:, k:k + 1],
                                                                                         rs[s][:, 0:n], ALU.mult, ALU.mult),
                     reads=[self.t_hT[k][gi], tr[s], self.PT_], writes=[self.t_uTg[gi]])
        c.release(tq + tr)

    def proj_fm(self, W, wt, j0, M, xin, xin_toks, evac, nk=8, banks=(0, 1)):
        c = self.c
        for gi, (c0, c1) in enumerate(self.groups):
            n = c1 - c0
            bank = banks[self._rr % len(banks)]
            self._rr += 1
            ps = self.psb(bank)
            for k in range(nk):
                c.op("pe", lambda e, k=k, ps=ps, n=n, c0=c0, c1=c1: e.matmul(ps[0:M, 0:n], lhsT=W[:, k, j0:j0 + M], rhs=xin[:, k, c0:c1],
                                                                             start=(k == 0), stop=(k == nk - 1)),
                     reads=[wt] + xin_toks(gi), writes=[self.pst[bank]])
            evac(ps[0:M, 0:n], gi, c0, c1, self.pst[bank])

    def proj_tm(self, W, wt, j0, N, evac, tiles=None, banks=(0, 1)):
        c = self.c
        for j in (tiles if tiles is not None else range(self.T)):
            bank = banks[self._rr % len(banks)]
            self._rr += 1
            ps = self.psb(bank)
            for k in range(8):
                c.op("pe", lambda e, k=k, ps=ps, j=j: e.matmul(ps[:, 0:N], lhsT=self.uT[:, k, j * 128:(j + 1) * 128], rhs=W[:, k, j0:j0 + N],
                                                               start=(k == 0), stop=(k == 7)),
                     reads=[wt, self.t_uTg[j // 4]], writes=[self.pst[bank]])
            evac(ps[:, 0:N], j, self.pst[bank])

    _rr = 0

    def pipeline_fine(self, gens):
        import os
        mode = os.environ.get("PMODE", "fine")
        prev = None
        for g in list(gens) + [None]:
            a_done = g is None
            b_done = prev is None
            if mode == "newest":
                kmax = int(os.environ.get("KMAX", "99"))
                kk_ = 0
                while not a_done:
                    try:
                        if next(g) == "S":
                            a_done = True
                    except StopIteration:
                        a_done = True
                        g = None
                    kk_ += 1
                    if prev is not None and kk_ >= kmax:
                        a_done = True
                        g = None
            while not (a_done and b_done):
                if not a_done:
                    try:
                        if next(g) == "S":
                            a_done = True
                    except StopIteration:
                        a_done = True
                        g = None
                if not b_done:
                    try:
                        next(prev)
                    except StopIteration:
                        b_done = True
            prev = g

    def keepwarm(self, bank, n):
        c = self.c
        for _w in range(n):
            c.op("pe", lambda e: e.matmul(self.psb(bank)[:, :], lhsT=self.cb("ident"), rhs=self.biasP[0][:, 0, :, :, :].rearrange("p b r q -> p (b r q)"), start=True, stop=True),
                 reads=[self.BT, self.CT], writes=[self.pst[bank]])

    def pipeline_sched(self, gens, sched):
        gens = list(gens)
        n = len(gens)
        maxlag = max(l for _, l in sched)
        pos = [0] * n
        for r in range(n + maxlag):
            for seg, lag in sched:
                t = r - lag
                if 0 <= t < n:
                    assert pos[t] == seg, (t, pos[t], seg)
                    try:
                        next(gens[t])
                    except StopIteration:
                        pass
                    pos[t] += 1

    def pipeline_rr(self, gens, nstage):
        gens = list(gens)
        n = len(gens)
        done = [False] * n
        for r in range(n + nstage - 1):
            act = [t for t in range(r - nstage + 1, r + 1) if 0 <= t < n and not done[t]]
            atb = {t: False for t in act}
            while not all(atb.values()):
                for t in act:
                    if atb[t]:
                        continue
                    try:
                        if next(gens[t]) == "S":
                            atb[t] = True
                    except StopIteration:
                        atb[t] = True
                        done[t] = True

    def pipeline(self, gens, on=True):
        if not on:
            for g in gens:
                for _ in g:
                    pass
            return
        active = []
        it = iter(gens)
        while True:
            nxt = next(it, None)
            if nxt is not None:
                active.append(nxt)
            if not active:
                break
            for g in list(active):
                try:
                    next(g)
                except StopIteration:
                    active.remove(g)

    def swa(self, MOFF):
        c, d, l, T, NT = self.c, self.d, self.l, self.T, self.NT
        st = self.S[l]
        o = MOFF
        qkv = self.av(o, [128, 9, 512], F32); o += 18432
        ssg = self.av(o, [128, 9, 256], BF16); o += 4608
        tmp = self.av(o, [128, 3, 6, 64], F32); o += 4608
        rst = self.av(o, [128, 3, 6], F32); o += 128
        qkn = self.av(o, [128, 9, 6, 64], BF16); o += 6912
        kn32 = self.av(o, [128, 2, 128], F32); o += 1024
        QT = self.av(o, [128, 2, 2, 128], BF16); o += 1024
        PT = self.av(o, [128, 2, 2, 2, 256], BF16); o += 4096
        ytmp = self.av(o, [128, 2, 4, 64], F32); o += 2048
        den = self.av(o, [128, 2, 8], F32); o += 64
        Kc = self.av(o, [128, 16, 128], BF16); o += 4096
        Vc = self.av(o, [128, 16, 2, 65], BF16); o += 4160
        KcTz = self.av(o, [128, 16, 2, 128], BF16); o += 8192
        PTc = self.av(o, [128, 16, 2, 16], BF16); o += 1024
        oTs = self.av(o, [128, 4, 128], F32); o += 2048
        t_qkv = [c.new_tok() for _ in range(T)]
        t_ssg = [c.new_tok() for _ in range(T)]
        t_tmp, t_rst, t_kn32 = c.new_tok(), c.new_tok(), c.new_tok()
        t_qkn = [c.new_tok() for _ in range(T)]
        t_QT = [c.new_tok(), c.new_tok()]
        t_PT = [c.new_tok(), c.new_tok()]
        t_ytmp = [c.new_tok(), c.new_tok()]
        t_den = [c.new_tok(), c.new_tok()]
        t_s = c.new_tok()
        alltoks = t_qkv + t_ssg + [t_tmp, t_rst, t_kn32] + t_qkn + t_QT + t_PT + t_ytmp + t_den + [t_s]
        W, wt = self.getw("swa", l)

        def ev_qkv(ps, j, bt):
            c.op("act", lambda e: e.copy(qkv[:, j, :], ps), reads=[bt], writes=[t_qkv[j]])

        def ev_sg(ps, j, bt):
            c.op("act", lambda e: e.activation(ssg[:, j, :], ps, AF.Silu), reads=[bt], writes=[t_ssg[j]])

        self.proj_tm(W, wt, 0, 512, ev_qkv, tiles=list(range(0, min(3, T))))
        if self.has_s:
            c.dma("pool", Kc, d["ck"].ap()[l].rearrange("b k c -> k b c"), writes=[t_s], key="swc")
            c.op("dve", lambda e: e.memset(Vc[:, :, :, 64:65], 1.0), writes=[t_s])
            for g in range(2):
                c.dma("pool", Vc[:, :, g, 0:64], d["cv"].ap()[l][:, :, 64 * g:64 * g + 64].rearrange("b k c -> k b c"), writes=[t_s], key="swc")
        for j0 in range(0, T, 3):
            nj = min(3, T - j0)
            if j0 + 3 < T:
                self.proj_tm(W, wt, 0, 512, ev_qkv, tiles=list(range(j0 + 3, min(j0 + 6, T))))
            else:
                self.proj_tm(W, wt, 512, 256, ev_sg)
            qk = qkv[:, j0:j0 + nj, 0:384].rearrange("p j (h d) -> p j h d", d=64)
            tm = tmp[:, 0:nj, :, :]
            rs = rst[:, 0:nj, :]
            rd = [t_qkv[j] for j in range(j0, j0 + nj)]
            c.op("dve", lambda e, tm=tm, qk=qk: e.tensor_tensor(tm, qk, qk, ALU.mult), reads=rd, writes=[t_tmp])
            c.op("dve", lambda e, tm=tm, rs=rs: e.tensor_reduce(rs, tm, AX.X, ALU.add), reads=[t_tmp], writes=[t_rst])
            c.op("act", lambda e, rs=rs: e.activation(rs, rs, AF.Ln, bias=EPS, scale=1.0 / 64), reads=[t_rst], writes=[t_rst])
            c.op("act", lambda e, rs=rs: e.activation(rs, rs, AF.Exp, scale=-0.5), reads=[t_rst], writes=[t_rst])
            c.op("dve", lambda e, tm=tm, qk=qk, rs=rs, nj=nj: e.tensor_tensor(tm, qk, rs.unsqueeze(3).to_broadcast([128, nj, 6, 64]), ALU.mult),
                 reads=rd + [t_rst], writes=[t_tmp])
            c.op("dve", lambda e, tm=tm, nj=nj, j0=j0: e.tensor_tensor(qkn[:, j0:j0 + nj, :, :], tm, self.hdw[:, 0:6, :].unsqueeze(1).to_broadcast([128, nj, 6, 64]), ALU.mult),
                 reads=[t_tmp, self.PT_], writes=[t_qkn[j] for j in range(j0, j0 + nj)])
            for j in range(j0, j0 + nj):
                slot = None
                if self.hf == 1 and j == 7:
                    slot = 0
                if self.has_s and j == 8:
                    slot = 1
                if slot is not None:
                    c.op("dve", lambda e, j=j, slot=slot, j0=j0: e.tensor_tensor(kn32[:, slot, :].rearrange("p (h d) -> p h d", d=64), tmp[:, j - j0, 4:6, :],
                                                                                   self.hdw[:, 4:6, :], ALU.mult), reads=[t_tmp, self.PT_], writes=[t_kn32])
        self.prefetch("gla", l)
        def body(j):
            is_s = self.has_s and j == 8
            gt = self.hf * 8 + j
            par = gt % 3
            par4 = gt % 4
            sl = j % 2
            first = (gt == 0) or is_s
            tp = self.psb(2, BF16)[:, 0:384].rearrange("p (a b) -> p a b", a=3)
            for a in range(3):
                src = qkn[:, j, 2 * a:2 * a + 2, :].rearrange("p h d -> p (h d)")
                c.op("pe", lambda e, a=a, src=src, tp=tp: e.transpose(tp[:, a, :], src, self.cb("ident")), reads=[t_qkn[j], self.CT], writes=[self.pst[2]])
            yield
            import os
            SKIP = os.environ.get("SKIP", "") if j >= 1 else ""
            if "qt" not in SKIP:
                c.op("act", lambda e, sl=sl, tp=tp: e.copy(QT[:, sl, :, :], tp[:, 0:2, :]), reads=[self.pst[2]], writes=[t_QT[sl]])
            KTz, vext, tkv, tvx = st["KTz"][par], st["vext"][par4], st["t_kv"][par], st["t_vx"][par4]
            if "ktz" not in SKIP:
              c.op("dve", lambda e, tp=tp, KTz=KTz: e.tensor_tensor(KTz[:, :, :], tp[:, 2:3, :].to_broadcast([128, 2, 128]),
                                                                  self.cb("bd2").unsqueeze(2).to_broadcast([128, 2, 128]), ALU.mult),
                 reads=[self.pst[2], self.CT], writes=[tkv])
            if "vext" not in SKIP:
              c.op("pool", lambda e, vext=vext, j=j: e.tensor_copy(vext[:, :, 0:64], qkv[:, j, 384:512].rearrange("p (g d) -> p g d", g=2)),
                 reads=[t_qkv[j]], writes=[tvx])
            yield "S"
            if not is_s:
                for _w in range(2):
                    c.op("pe", lambda e: e.matmul(self.psb(0)[:, :], lhsT=self.cb("ident"), rhs=self.biasP[0][:, 0, :, :, :].rearrange("p b r q -> p (b r q)"), start=True, stop=True),
                         reads=[self.BT, self.CT], writes=[self.pst[0]])
            blks = [1] if first else [0, 1]
            for g in range(2):
                yield
                bank = 4 + g
                sc = self.psb(bank).rearrange("p (b n) -> p b n", b=2)
                b0 = blks[0]
                scf = sc[:, b0:2, :].rearrange("p b n -> p (b n)")
                if is_s:
                    bh = self.biasS[0][:, g, :, :].rearrange("p r q -> p (r q)")
                    bl = self.biasS[1][:, g, :, :].rearrange("p r q -> p (r q)")
                else:
                    bh = self.biasP[0][:, g, b0:2, :, :].rearrange("p b r q -> p (b r q)")
                    bl = self.biasP[1][:, g, b0:2, :, :].rearrange("p b r q -> p (b r q)")
                c.op("pe", lambda e, scf=scf, bh=bh: e.matmul(scf, lhsT=self.cb("ident"), rhs=bh, start=True, stop=False), reads=[self.BT, self.CT], writes=[self.pst[bank]])
                c.op("pe", lambda e, scf=scf, bl=bl: e.matmul(scf, lhsT=self.cb("ident"), rhs=bl, start=False, stop=False), reads=[self.BT, self.CT], writes=[self.pst[bank]])
                for blk in blks:
                    kpar = par if blk == 1 else (par + 2) % 3
                    KT_ = st["KTz"][kpar]
                    tk_ = st["t_kv"][kpar]
                    lastb = (blk == blks[-1])
                    c.op("pe", lambda e, sc=sc, blk=blk, KT_=KT_, g=g, sl=sl, lastb=lastb: e.matmul(sc[:, blk, :], lhsT=KT_[:, g, :], rhs=QT[:, sl, :, :].rearrange("p r q -> p (r q)"),
                                                                                       start=False, stop=lastb), reads=[tk_, t_QT[sl]], writes=[self.pst[bank]])
                b0 = blks[0]
                c.op("act", lambda e, sc=sc, b0=b0, sl=sl, g=g: e.activation(PT[:, sl, g, b0:2, :], sc[:, b0:2, :], AF.Exp, scale=0.125),
                     reads=[self.pst[bank]], writes=[t_PT[sl]])
            yield "S"
            if not is_s:
                for _w in range(2):
                    c.op("pe", lambda e: e.matmul(self.psb(0)[:, :], lhsT=self.cb("ident"), rhs=self.biasP[0][:, 1, :, :, :].rearrange("p b r q -> p (b r q)"), start=True, stop=True),
                         reads=[self.BT, self.CT], writes=[self.pst[0]])
                oe = self.psb(6).rearrange("p (h n) -> p h n", h=4)
                for h in range(4):
                    yield
                    g, r = h // 2, h % 2
                    for blk in blks:
                        kpar = par4 if blk == 1 else (par4 + 3) % 4
                        f0, f1 = (blk == blks[0]), (blk == blks[-1])
                        c.op("pe", lambda e, oe=oe, h=h, g=g, r=r, blk=blk, kpar=kpar, sl=sl, f0=f0, f1=f1: e.matmul(oe[:, h, 0:65], lhsT=PT[:, sl, g, blk, r * 128:(r + 1) * 128],
                                                                                                     rhs=st["vext"][kpar][:, g, :], start=f0, stop=f1),
                             reads=[t_PT[sl], st["t_vx"][kpar]], writes=[self.pst[6]])
                self.swa_finish(oe, j, sl, ssg, t_ssg, ytmp, t_ytmp, den, t_den)
            else:
                self.swa_sample(j, sl, QT, t_QT, PT, t_PT, Kc, Vc, KcTz, PTc, oTs, t_s, vext, tvx, ssg, t_ssg, ytmp, t_ytmp, den, t_den)
            if self.hf == 1 and j == 7:
                c.dma("sp", self.o["swak_p"].ap()[l], kn32[:, 0, :], reads=[t_kn32], key="ok")
                c.dma("sp", self.o["swav_p"].ap()[l], qkv[:, 7, 384:512], reads=[t_qkv[7]], key="ok")
            if is_s:
                c.dma("sp", self.o["swak_s"].ap()[l][:, 0:120, :], d["ck"].ap()[l][:, 8:128, :], key="ok")
                c.dma("sp", self.o["swav_s"].ap()[l][:, 0:120, :], d["cv"].ap()[l][:, 8:128, :], key="ok")
                for b in range(16):
                    c.dma("sp", self.o["swak_s"].ap()[l][b, 120:128, :], kn32[8 * b:8 * b + 8, 1, :], reads=[t_kn32], key="ok")
                    c.dma("sp", self.o["swav_s"].ap()[l][b, 120:128, :], qkv[8 * b:8 * b + 8, 8, 384:512], reads=[t_qkv[8]], key="ok")
        self.pipeline_rr([body(j) for j in range(T if not self.trunc else int(self.trunc))], 3)
        c.release(alltoks)

    def swa_finish(self, oe, j, sl, ssg, t_ssg, ytmp, t_ytmp, den, t_den):
        c = self.c
        c.op("dve", lambda e: e.tensor_tensor(den[:, sl, 0:4], oe[:, :, 64], self.esink[:, :], ALU.add), reads=[self.pst[6], self.PT_], writes=[t_den[sl]])
        c.op("dve", lambda e: e.reciprocal(den[:, sl, 4:8], den[:, sl, 0:4]), reads=[t_den[sl]], writes=[t_den[sl]])
        c.op("dve", lambda e: e.tensor_tensor(ytmp[:, sl, :, :], oe[:, :, 0:64], den[:, sl, 4:8].unsqueeze(2).to_broadcast([128, 4, 64]), ALU.mult),
             reads=[self.pst[6], t_den[sl]], writes=[t_ytmp[sl]])
        c.op("dve", lambda e: e.tensor_tensor(self.ytok[:, j, 256:512], ytmp[:, sl, :, :].rearrange("p h d -> p (h d)"), ssg[:, j, :], ALU.mult),
             reads=[t_ytmp[sl], t_ssg[j]], writes=[self.t_ytok])

    def swa_sample(self, j, sl, QT, t_QT, PT, t_PT, Kc, Vc, KcTz, PTc, oTs, t_s, vext, tkv, ssg, t_ssg, ytmp, t_ytmp, den, t_den):
        c = self.c
        for b4 in range(4):
            tp = self.psb(3, BF16)[:, 0:512].rearrange("p (a b) -> p a b", a=4)
            for bb in range(4):
                b = b4 * 4 + bb
                c.op("pe", lambda e, tp=tp, bb=bb, b=b: e.transpose(tp[:, bb, :], Kc[:, b, :], self.cb("ident")), reads=[t_s, self.CT], writes=[self.pst[3]])
            c.op("dve", lambda e, tp=tp, b4=b4: e.tensor_tensor(KcTz[:, b4 * 4:b4 * 4 + 4, :, :], tp.unsqueeze(2).to_broadcast([128, 4, 2, 128]),
                                                                self.cb("bd2").unsqueeze(1).unsqueeze(3).to_broadcast([128, 4, 2, 128]), ALU.mult),
                 reads=[self.pst[3], self.CT], writes=[t_s])
        scc = self.psb(7).rearrange("p (b g n) -> p b g n", b=16, g=2)
        sccf = self.psb(7)
        c.op("pe", lambda e: e.matmul(sccf[:, :], lhsT=self.cb("ident"), rhs=self.biasC[0][:, :, :, :, :].rearrange("p b g r i -> p (b g r i)"), start=True, stop=False),
             reads=[self.BT, self.CT], writes=[self.pst[7]])
        c.op("pe", lambda e: e.matmul(sccf[:, :], lhsT=self.cb("ident"), rhs=self.biasC[1][:, :, :, :, :].rearrange("p b g r i -> p (b g r i)"), start=False, stop=False),
             reads=[self.BT, self.CT], writes=[self.pst[7]])
        for b in range(16):
            for g in range(2):
                c.op("pe", lambda e, b=b, g=g: e.matmul(scc[:, b, g, :], lhsT=KcTz[:, b, g, :], rhs=QT[:, sl, :, 8 * b:8 * b + 8], start=False, stop=(b == 15 and g == 1)),
                     reads=[t_s, t_QT[sl]], writes=[self.pst[7]])
        c.op("act", lambda e: e.activation(PTc[:, :, :, :].rearrange("p b g n -> p (b g n)"), sccf[:, :], AF.Exp, scale=0.125), reads=[self.pst[7]], writes=[t_s])
        oT = self.psb(6).rearrange("p (h n) -> p h n", h=4)
        for h in range(4):
            g, r = h // 2, h % 2
            c.op("pe", lambda e, h=h, g=g, r=r: e.matmul(oT[0:65, h, :], lhsT=vext[:, g, :], rhs=PT[:, sl, g, 1, r * 128:(r + 1) * 128], start=True, stop=False),
                 reads=[tkv, t_PT[sl]], writes=[self.pst[6]])
            for b in range(16):
                c.op("pe", lambda e, h=h, g=g, r=r, b=b: e.matmul(oT[0:65, h, 8 * b:8 * b + 8], lhsT=Vc[:, b, g, :], rhs=PTc[:, b, g, r * 8:(r + 1) * 8], start=False, stop=True),
                     reads=[t_s], writes=[self.pst[6]])
        c.op("act", lambda e: e.copy(oTs[0:65, :, :], oT[0:65, :, :]), reads=[self.pst[6]], writes=[t_s])
        oe = self.psb(5).rearrange("p (h n) -> p h n", h=4)
        for h in range(4):
            c.op("pe", lambda e, h=h: e.transpose(oe[:, h, 0:65], oTs[0:65, h, :], self.cf("identf")[0:65, 0:65]), reads=[t_s, self.CT], writes=[self.pst[5]])
        c.op("dve", lambda e: e.tensor_tensor(den[:, sl, 0:4], oe[:, :, 64], self.esink[:, :], ALU.add), reads=[self.pst[5], self.PT_], writes=[t_den[sl]])
        c.op("dve", lambda e: e.reciprocal(den[:, sl, 4:8], den[:, sl, 0:4]), reads=[t_den[sl]], writes=[t_den[sl]])
        c.op("dve", lambda e: e.tensor_tensor(ytmp[:, sl, :, :], oe[:, :, 0:64], den[:, sl, 4:8].unsqueeze(2).to_broadcast([128, 4, 64]), ALU.mult),
             reads=[self.pst[5], t_den[sl]], writes=[t_ytmp[sl]])
        c.op("dve", lambda e: e.tensor_tensor(self.ytok[:, j, 256:512], ytmp[:, sl, :, :].rearrange("p h d -> p (h d)"), ssg[:, j, :], ALU.mult),
             reads=[t_ytmp[sl], t_ssg[j]], writes=[self.t_ytok])

    def gla(self, MOFF):
        c, d, l, T, NT = self.c, self.d, self.l, self.T, self.NT
        st = self.S[l]
        o = MOFF
        gqT = self.av(o, [128, 1152], BF16); o += 2304
        gkT = self.av(o, [128, 1152], BF16); o += 2304
        glrT = self.av(o, [128, 1152], BF16); o += 2304
        gtok = self.av(o, [128, 9, 384], BF16); o += 6912
        sgg = self.av(o, [128, 9, 256], BF16); o += 4608
        sp = self.av(o, [128, 2, 128], F32); o += 1024
        ebT = self.av(o, [128, 2, 2, 128], F32); o += 2048
        erc = self.av(o, [128, 2, 128], F32); o += 1024
        qeT = self.av(o, [128, 2, 128], BF16); o += 512
        keT = self.av(o, [128, 2, 128], BF16); o += 512
        kd = self.av(o, [128, 2, 128], BF16); o += 512
        qebd = self.av(o, [128, 2, 4, 128], BF16); o += 2048
        attm = self.av(o, [128, 2, 4, 128], BF16); o += 2048
        oss = self.av(o, [128, 2, 8], F32); o += 64
        otmp = self.av(o, [128, 2, 4, 64], F32); o += 2048
        um = self.av(o, [128, 4, 64], F32); o += 1024
        qeTm = self.av(o, [128, 16, 128], BF16); o += 4096
        kdm = self.av(o, [128, 16, 128], BF16); o += 4096
        Sg0 = self.av(o, [128, 16, 256], F32); o += 16384
        Sg0b = self.av(o, [128, 16, 256], BF16); o += 8192
        t_gq = [c.new_tok() for _ in self.groups]
        t_gk = [c.new_tok() for _ in self.groups]
        t_glr = [c.new_tok() for _ in self.groups]
        t_gtok = [c.new_tok() for _ in range(T)]
        t_sgg = c.new_tok()
        t_sp = [c.new_tok(), c.new_tok()]
        t_eb = [c.new_tok(), c.new_tok()]
        t_q = [c.new_tok(), c.new_tok()]
        t_att = [c.new_tok(), c.new_tok()]
        t_o = [c.new_tok(), c.new_tok()]
        t_um = c.new_tok()
        t_s = c.new_tok()
        alltoks = t_gq + t_gk + t_glr + t_gtok + [t_sgg] + t_sp + t_eb + t_q + t_att + t_o + [t_um, t_s]
        W, wt = self.getw("gla", l)
        c.op("pool", lambda e: e.memset(glrT[0:32, :], 1.0), writes=t_glr)

        def ev_fm(dst, toks):
            def f(ps, gi, c0, c1, bt):
                M = ps.shape[0]
                c.op("act", lambda e: e.copy(dst[0:M, c0:c1], ps), reads=[bt], writes=[toks[gi]])
            return f

        def xt(gi):
            return [self.t_uTg[gi]]

        self.proj_fm(W, wt, 0, 128, self.uT, xt, ev_fm(gqT, t_gq))
        self.proj_fm(W, wt, 128, 128, self.uT, xt, ev_fm(gkT, t_gk))
        self.proj_fm(W, wt, 256, 16, self.uT, xt, ev_fm(glrT, t_glr))

        def ev_kv(ps, j, bt):
            c.op("dve", lambda e: e.tensor_copy(gtok[:, j, :], ps), reads=[bt], writes=[t_gtok[j]])

        def ev_gg(ps, j, bt):
            c.op("act", lambda e: e.activation(sgg[:, j, :], ps, AF.Silu), reads=[bt], writes=[t_sgg])

        self.proj_tm(W, wt, 272, 384, ev_kv)
        self.proj_tm(W, wt, 656, 256, ev_gg)
        c.op("dve", lambda e: e.tensor_tensor(sgg[:, 0:T, :].rearrange("p j (h d) -> p j h d", d=64), sgg[:, 0:T, :].rearrange("p j (h d) -> p j h d", d=64),
                                              self.hdw[:, 6:7, :].unsqueeze(1).to_broadcast([128, T, 4, 64]), ALU.mult), reads=[t_sgg, self.PT_], writes=[t_sgg])
        if self.has_s:
            c.op("pool", lambda e: e.memset(Sg0[:, :, :], 0.0), writes=[t_s])
            for h in range(4):
                c.dma("sp", Sg0[32 * h:32 * h + 32, :, 64 * h:64 * h + 64], d["st_gla"].ap()[l][:, h, :, :].rearrange("b d v -> d b v"), writes=[t_s], key="gls")
            c.op("act", lambda e: e.copy(Sg0b[:, :, :], Sg0[:, :, :]), reads=[t_s], writes=[t_s])
        self.prefetch("ssd_x", l)
        self.prefetch("ssd_z", l)
        def body(j):
            is_s = self.has_s and j == 8
            sl = j % 2
            gi = j // 4
            cs = slice(j * 128, (j + 1) * 128)
            tri = self.cf("triS") if is_s else self.cf("triP")
            sup = self.cf("supS") if is_s else self.cf("supP")
            m01 = self.cb("triS") if is_s else self.cb("triP")
            ps2 = self.psb(2)
            c.op("pe", lambda e, cs=cs: e.matmul(ps2[:, 0:128], lhsT=glrT[0:17, cs], rhs=self.wgk[0:17, :], start=True, stop=True),
                 reads=[t_glr[gi], self.PT_], writes=[self.pst[2]])
            yield
            c.op("act", lambda e, sl=sl: e.activation(sp[:, sl, :], ps2[:, 0:128], AF.Exp, scale=-1.0), reads=[self.pst[2]], writes=[t_sp[sl]])
            yield
            c.op("act", lambda e, sl=sl: e.activation(sp[:, sl, :], sp[:, sl, :], AF.Ln, bias=1.0), reads=[t_sp[sl]], writes=[t_sp[sl]])
            yield
            ps3 = self.psb(3)
            c.op("pe", lambda e, sl=sl, sup=sup: e.matmul(ps3[:, 128:256], lhsT=sup, rhs=sp[:, sl, :], start=True, stop=True), reads=[t_sp[sl], self.CT], writes=[self.pst[3]])
            yield
            c.op("pe", lambda e, sl=sl, tri=tri: e.matmul(ps3[:, 256:384], lhsT=sp[:, sl, :], rhs=tri, start=True, stop=True), reads=[t_sp[sl], self.CT], writes=[self.pst[3]])
            yield
            c.op("act", lambda e, sl=sl: e.activation(ebT[:, sl, 0, :], ps3[:, 256:384], AF.Exp, scale=-1.0 / 16), reads=[self.pst[3]], writes=[t_eb[sl]])
            yield
            c.op("act", lambda e, sl=sl: e.activation(ebT[:, sl, 1, :], ps3[:, 256:384], AF.Exp, scale=1.0 / 16), reads=[self.pst[3]], writes=[t_eb[sl]])
            yield
            c.op("act", lambda e, sl=sl: e.activation(erc[:, sl, :], ps3[:, 128:256], AF.Exp, scale=-1.0 / 16), reads=[self.pst[3]], writes=[t_eb[sl]])
            yield
            c.op("dve", lambda e, sl=sl, cs=cs: e.scalar_tensor_tensor(qeT[:, sl, :], gqT[:, cs], 32.0 ** -0.5, ebT[:, sl, 0, :], ALU.mult, ALU.mult),
                 reads=[t_gq[gi], t_eb[sl]], writes=[t_q[sl]])
            yield
            c.op("dve", lambda e, sl=sl, cs=cs: e.tensor_tensor(keT[:, sl, :], gkT[:, cs], ebT[:, sl, 1, :], ALU.mult), reads=[t_gk[gi], t_eb[sl]], writes=[t_q[sl]])
            yield
            c.op("dve", lambda e, sl=sl, j=j: e.tensor_tensor(kd[:, sl, :], gtok[:, j, 0:128], erc[:, sl, :], ALU.mult), reads=[t_gtok[j], t_eb[sl]], writes=[t_q[sl]])
            yield
            c.op("dve", lambda e, sl=sl: e.tensor_tensor(qebd[:, sl, :, :], qeT[:, sl, :].unsqueeze(1).to_broadcast([128, 4, 128]),
                                                         self.cb("bd4").unsqueeze(2).to_broadcast([128, 4, 128]), ALU.mult), reads=[t_q[sl], self.CT], writes=[t_q[sl]])
            yield
            yield "S"
            if not is_s:
                self.keepwarm(1, 3)
            ps4 = self.psb(4)
            c.op("pe", lambda e, sl=sl: e.matmul(ps4[:, :], lhsT=keT[:, sl, :], rhs=qebd[:, sl, :, :].rearrange("p h t -> p (h t)"), start=True, stop=True),
                 reads=[t_q[sl]], writes=[self.pst[4]])
            yield
            c.op("dve", lambda e, sl=sl, m01=m01: e.tensor_tensor(attm[:, sl, :, :], ps4.rearrange("p (h t) -> p h t", h=4), m01.unsqueeze(1).to_broadcast([128, 4, 128]), ALU.mult),
                 reads=[self.pst[4], self.CT], writes=[t_att[sl]])
            yield
            ob = 5 if (j % 2 == 0) else 7
            ps5 = self.psb(ob)
            if not is_s:
                c.op("pe", lambda e, sl=sl: e.matmul(ps5[:, 0:256], lhsT=qeT[:, sl, :], rhs=st["gla_b"][:, :], start=True, stop=False),
                     reads=[t_q[sl], st["t_gla"]], writes=[self.pst[ob]])
            else:
                c.op("dve", lambda e, sl=sl: e.tensor_tensor(qeTm[:, :, :], qeT[:, sl, :].unsqueeze(1).to_broadcast([128, 16, 128]),
                                                             self.cb("smbt").rearrange("p (b t) -> p b t", b=16), ALU.mult), reads=[t_q[sl], self.CT], writes=[t_s])
                for b in range(16):
                    c.op("pe", lambda e, b=b: e.matmul(ps5[:, 0:256], lhsT=qeTm[:, b, :], rhs=Sg0b[:, b, :], start=(b == 0), stop=False), reads=[t_s], writes=[self.pst[ob]])
            for h in range(4):
                c.op("pe", lambda e, sl=sl, h=h, j=j: e.matmul(ps5[:, h * 64:(h + 1) * 64], lhsT=attm[:, sl, h, :], rhs=gtok[:, j, 128 + h * 64:128 + (h + 1) * 64],
                                                               start=False, stop=(h == 3)), reads=[t_att[sl], t_gtok[j]], writes=[self.pst[ob]])
            if not is_s:
                ps6 = self.psb(6)
                c.op("pe", lambda e, sl=sl, j=j: e.matmul(ps6[:, 0:256], lhsT=kd[:, sl, :], rhs=gtok[:, j, 128:384], start=True, stop=True),
                     reads=[t_q[sl], t_gtok[j]], writes=[self.pst[6]])
                c.op("dve", lambda e: e.tensor_tensor(um[:, :, :], ps6[:, 0:256].rearrange("p (h d) -> p h d", h=4), self.cf("bd4").unsqueeze(2).to_broadcast([128, 4, 64]), ALU.mult),
                     reads=[self.pst[6], self.CT], writes=[t_um])
                c.op("dve", lambda e, sl=sl: e.scalar_tensor_tensor(st["gla"][:, :], st["gla"][:, :], ebT[:, sl, 0, 127:128], um[:, :, :].rearrange("p h d -> p (h d)"), ALU.mult, ALU.add),
                     reads=[t_um, t_eb[sl], st["t_gla"]], writes=[st["t_gla"]])
                c.op("act", lambda e: e.copy(st["gla_b"][:, :], st["gla"][:, :]), reads=[st["t_gla"]], writes=[st["t_gla"]])
                if self.hf == 1 and j == 7:
                    for h in range(4):
                        c.dma("sp", self.o["gla_p"].ap()[l][h], st["gla"][32 * h:32 * h + 32, 64 * h:64 * h + 64], reads=[st["t_gla"]], key="og")
            else:
                c.op("dve", lambda e, sl=sl: e.tensor_tensor(kdm[:, :, :], kd[:, sl, :].unsqueeze(1).to_broadcast([128, 16, 128]),
                                                             self.cb("smtok").unsqueeze(2).to_broadcast([128, 16, 128]), ALU.mult), reads=[t_q[sl], self.CT], writes=[t_s])
                c.op("dve", lambda e, sl=sl: e.tensor_tensor(Sg0[:, :, :], Sg0[:, :, :], ebT[:, sl, 0, 7:128:8].unsqueeze(2).to_broadcast([128, 16, 256]), ALU.mult),
                     reads=[t_s, t_eb[sl]], writes=[t_s])
                for b2 in range(8):
                    bank = 6 if (b2 % 2 == 0) else 0
                    psu = self.psb(bank)
                    for bb in range(2):
                        b = b2 * 2 + bb
                        c.op("pe", lambda e, b=b, bb=bb, psu=psu, j=j: e.matmul(psu[:, bb * 256:(bb + 1) * 256], lhsT=kdm[:, b, :], rhs=gtok[:, j, 128:384], start=True, stop=True),
                             reads=[t_s, t_gtok[j]], writes=[self.pst[bank]])
                    c.op("dve", lambda e, b2=b2, psu=psu: e.tensor_tensor(Sg0[:, 2 * b2:2 * b2 + 2, :], Sg0[:, 2 * b2:2 * b2 + 2, :], psu.rearrange("p (b n) -> p b n", b=2), ALU.add),
                         reads=[self.pst[bank], t_s], writes=[t_s])
                for h in range(4):
                    c.dma("sp", self.o["gla_s"].ap()[l][:, h, :, :].rearrange("b d v -> d b v"), Sg0[32 * h:32 * h + 32, :, 64 * h:64 * h + 64], reads=[t_s], key="og")
            yield "S"
            o4 = ps5[:, 0:256].rearrange("p (h d) -> p h d", h=4)
            for hh in range(4):
                c.op("act", lambda e, sl=sl, o4=o4, hh=hh: e.activation(otmp[:, sl, hh, :], o4[:, hh, :], AF.Square, accum_out=oss[:, sl, hh:hh + 1]), reads=[self.pst[ob]], writes=[t_o[sl]])
            yield
            c.op("act", lambda e, sl=sl: e.activation(oss[:, sl, 4:8], oss[:, sl, 0:4], AF.Ln, bias=EPS, scale=1.0 / 64), reads=[t_o[sl]], writes=[t_o[sl]])
            yield
            c.op("act", lambda e, sl=sl: e.activation(oss[:, sl, 4:8], oss[:, sl, 4:8], AF.Exp, scale=-0.5), reads=[t_o[sl]], writes=[t_o[sl]])
            yield
            c.op("dve", lambda e, sl=sl, o4=o4: e.tensor_tensor(otmp[:, sl, :, :], o4, oss[:, sl, 4:8].unsqueeze(2).to_broadcast([128, 4, 64]), ALU.mult),
                 reads=[self.pst[ob], t_o[sl]], writes=[t_o[sl]])
            yield
            c.op("dve", lambda e, sl=sl, j=j: e.tensor_tensor(self.ytok[:, j, 0:256], otmp[:, sl, :, :].rearrange("p h d -> p (h d)"), sgg[:, j, :], ALU.mult),
                 reads=[t_o[sl], t_sgg], writes=[self.t_ytok])
            yield
        self.pipeline_rr([body(j) for j in range(T)], 3)
        c.release(alltoks)

    def ssd(self, MOFF):
        c, d, l, T, NT = self.c, self.d, self.l, self.T, self.NT
        st = self.S[l]
        Tp = 8
        NP = 1024
        o = MOFF
        o_xp = o
        XP = self.av(o, [128, 6, 1027], BF16); o += 12324
        XS = self.av(o, [128, 6, 16, 11], BF16); o += 2112
        xcT = self.av(o, [128, 5, 1152], BF16); o += 11520
        o_btz = o
        BTz = self.av(o, [128, 2, 1152], BF16); o += 4608
        sz = self.av(o, [128, 9, 512], BF16); o += 9216
        dtraw = self.av(o, [128, 9, 8], F32); o += 288
        dtv = self.av(o, [128, 9, 8], F32); o += 288
        adt = self.av(o, [128, 9, 8], F32); o += 288
        acum = self.av(o, [128, 9, 8], F32); o += 288
        eacum = self.av(o, [128, 9, 8], F32); o += 288
        tail = self.av(o, [128, 9, 8], F32); o += 288
        cdrep = self.av(o, [128, 9, 8], F32); o += 288
        adth = self.av(o, [128, 9, 8], BF16); o += 144
        adtl = self.av(o, [128, 9, 8], BF16); o += 144
        cdx = self.av(o, [128, 4, 16], F32); o += 256
        yss = self.av(o, [128, 2, 4], F32); o += 32
        o_tile = o
        cvt = self.av(o_tile + 8192, [128, 768], F32)
        cvo = self.av(o_tile + 8192 + 3072, [128, 6, 48], F32)
        xtok = self.av(o, [128, 2, 768], BF16); o += 3072
        Lm = self.av(o, [128, 8, 128], BF16); o += 2048
        MT = self.av(o, [128, 2, 8, 128], BF16); o += 4096
        xdt = self.av(o, [128, 2, 512], BF16); o += 2048
        xdtt = self.av(o, [128, 2, 512], BF16); o += 2048
        xD = self.av(o, [128, 2, 512], BF16); o += 2048
        ytmp = self.av(o, [128, 2, 512], F32); o += 4096
        ytk = self.av(o, [128, 2, 512], BF16); o += 2048
        ysq = self.av(o, [128, 512], F32); o += 2048
        o_ctm = o
        CTm = self.av(o, [128, 16, 128], BF16); o += 4096
        ctmp = [self.av(o_tile + i * 4096, [128, 1024], F32) for i in range(2)]
        oo = o_xp
        St32 = [self.av(oo + i * 2048, [128, 512], F32) for i in range(2)]; oo += 4096
        Stb = [self.av(oo + i * 2048, [128, 2, 4, 128], BF16) for i in range(2)]; oo += 4096
        SbdT = [self.av(oo + i * 2048, [128, 2, 512], BF16) for i in range(2)]; oo += 4096
        assert oo <= o_xp + 12324 + 2112
        Bm = self.av(o_btz, [128, 2, 16, 64], BF16)

        t_XP = [c.new_tok() for _ in range(6)]
        t_XS = [c.new_tok() for _ in range(6)]
        t_xc = [[c.new_tok() for _ in self.groups] for _ in range(6)]
        t_sz = [c.new_tok() for _ in range(T)]
        t_dt = c.new_tok()
        t_cv = c.new_tok()
        t_ct = [c.new_tok(), c.new_tok()]
        alltoks = t_XS + sum(t_xc, []) + t_sz + [t_dt, t_cv] + t_ct
        W, wt = self.getw("ssd_x", l)
        W2, wt2 = self.getw("ssd_z", l)
        for ch in range(6):
            def ev(ps, gi, c0, c1, bt, ch=ch):
                if c0 < NP:
                    c.op("act", lambda e: e.copy(XP[:, ch, 3 + c0:3 + c1], ps), reads=[bt], writes=[t_XP[ch]])
                else:
                    c.op("act", lambda e: e.copy(XS[:, ch, :, 3:11], ps.rearrange("p (b i) -> p b i", i=8)), reads=[bt], writes=[t_XS[ch]])
            self.proj_fm(W, wt, ch * 128, 128, self.uT, lambda gi: [self.t_uTg[gi]], ev)

        self.prefetch("w_out", l)

        def ev_z(ps, j, bt):
            c.op("act", lambda e: e.activation(sz[:, j, :], ps, AF.Silu), reads=[bt], writes=[t_sz[j]])

        def ev_dt(ps, j, bt):
            c.op("dve", lambda e: e.tensor_copy(dtraw[:, j, :], ps), reads=[bt], writes=[t_dt])

        self.proj_tm(W2, wt2, 512, 8, ev_dt)
        dtb = self.rep8[:, 0:8].unsqueeze(1).to_broadcast([128, T, 8])
        c.op("dve", lambda e: e.tensor_tensor(dtv[:, 0:T, :], dtraw[:, 0:T, :], dtb, ALU.add), reads=[t_dt, self.PT_], writes=[t_dt])
        c.op("act", lambda e: e.activation(dtv[:, 0:T, :], dtv[:, 0:T, :], AF.Exp), reads=[t_dt], writes=[t_dt])
        c.op("act", lambda e: e.activation(dtv[:, 0:T, :], dtv[:, 0:T, :], AF.Ln, bias=1.0), reads=[t_dt], writes=[t_dt])
        c.op("dve", lambda e: e.tensor_tensor(adt[:, 0:T, :], dtv[:, 0:T, :], self.arep[:, :].unsqueeze(1).to_broadcast([128, T, 8]), ALU.mult), reads=[t_dt, self.PT_], writes=[t_dt])
        c.op("dve", lambda e: e.tensor_copy(adth[:, 0:T, :], adt[:, 0:T, :]), reads=[t_dt], writes=[t_dt])
        c.op("dve", lambda e: e.tensor_tensor(tail[:, 0:T, :], adt[:, 0:T, :], adth[:, 0:T, :], ALU.subtract), reads=[t_dt], writes=[t_dt])
        c.op("dve", lambda e: e.tensor_copy(adtl[:, 0:T, :], tail[:, 0:T, :]), reads=[t_dt], writes=[t_dt])
        ps3 = self.psb(3)
        adp = adt[:, 0:Tp, :].rearrange("p j h -> p (j h)")
        c.op("pe", lambda e: e.matmul(ps3[:, 0:64], lhsT=self.cf("triP"), rhs=adp, start=True, stop=True), reads=[t_dt, self.CT], writes=[self.pst[3]])
        c.op("pe", lambda e: e.matmul(ps3[:, 128:192], lhsT=self.cf("onesf"), rhs=adp, start=True, stop=True), reads=[t_dt, self.CT], writes=[self.pst[3]])
        if self.has_s:
            c.op("pe", lambda e: e.matmul(ps3[:, 64:72], lhsT=self.cf("triS"), rhs=adt[:, 8, :], start=True, stop=True), reads=[t_dt, self.CT], writes=[self.pst[3]])
            c.op("pe", lambda e: e.matmul(ps3[:, 192:200], lhsT=self.cf("sameS"), rhs=adt[:, 8, :], start=True, stop=True), reads=[t_dt, self.CT], writes=[self.pst[3]])
        n8 = T * 8
        fl = lambda a: a[:, 0:T, :].rearrange("p j h -> p (j h)")
        c.op("act", lambda e: e.copy(fl(acum), ps3[:, 0:n8]), reads=[self.pst[3]], writes=[t_dt])
        c.op("act", lambda e: e.activation(fl(eacum), ps3[:, 0:n8], AF.Exp), reads=[self.pst[3]], writes=[t_dt])
        c.op("act", lambda e: e.activation(fl(cdrep), ps3[:, 128:128 + n8], AF.Exp), reads=[self.pst[3]], writes=[t_dt])
        c.op("dve", lambda e: e.tensor_tensor(fl(tail), ps3[:, 128:128 + n8], fl(acum), ALU.subtract), reads=[self.pst[3], t_dt], writes=[t_dt])
        c.op("act", lambda e: e.activation(fl(tail), fl(tail), AF.Exp), reads=[t_dt], writes=[t_dt])
        rows32 = self.av(o_tile + 12416, [128, 8, 128], F32)
        rows2 = self.av(o_ctm, [128, 8, 128], BF16)
        rows_hi = rows2
        rows_lo = rows2[32:64, :, :]
        t_rows = c.new_tok()
        alltoks.append(t_rows)
        c.dma("sp", self.scr_ac.ap()[:, 0:n8], fl(acum), reads=[t_dt], writes=[self.t_scr_ac], key="sac")
        c.dma("sp", rows32[0:T, :, :], bass.AP(self.scr_ac, 0, [[8, T], [1, 8], [72, 128]]), reads=[self.t_scr_ac], writes=[t_rows], key="sac2", nc_ok=True)
        c.op("pool", lambda e: e.tensor_copy(XP[:, :, 0:3], st["ctail"][:, :, :]), reads=[st["t_ctail"]], writes=t_XP)
        c.op("pool", lambda e: e.memset(BTz[:, :, :], 0.0), writes=[t_xc[4][gi] for gi in range(len(self.groups))])
        if self.has_s:
            c.dma("sp", cvt[0:48, :], d["st_conv"].ap()[l].rearrange("b k c -> (b k) c"), writes=[t_cv], key="cvs")
            for half, (ch0, nch) in enumerate(((0, 4), (4, 2))):
                ps = self.psb(half)
                for cc in range(nch):
                    ch = ch0 + cc
                    c.op("pe", lambda e, ps=ps, cc=cc, ch=ch: e.transpose(ps[:, cc * 48:(cc + 1) * 48], cvt[0:48, ch * 128:(ch + 1) * 128], self.cf("identf")[0:48, 0:48]),
                         reads=[t_cv, self.CT], writes=[self.pst[half]])
                c.op("dve", lambda e, ps=ps, ch0=ch0, nch=nch: e.tensor_copy(XS[:, ch0:ch0 + nch, :, 0:3], ps[:, 0:nch * 48].rearrange("p (c b k) -> p c b k", c=nch, b=16)),
                     reads=[self.pst[half]], writes=[t_XS[ch] for ch in range(ch0, ch0 + nch)])
        accS = [self.av(o_tile + 16512 + i * 512, [128, 16, 8], F32) for i in range(2)]

        def conv_id(ch):
            sl = ch % 2
            acc = ctmp[sl]
            c.op("act", lambda e: e.activation(acc[:, 0:NP], XP[:, ch, 0:NP], AF.Identity, bias=self.cbT[:, ch:ch + 1], scale=self.cwT[:, ch, 0:1]),
                 reads=[t_XP[ch], self.PT_], writes=[t_ct[sl]])
            if self.has_s:
                c.op("act", lambda e: e.activation(accS[sl], XS[:, ch, :, 0:8], AF.Identity, bias=self.cbT[:, ch:ch + 1], scale=self.cwT[:, ch, 0:1]),
                     reads=[t_XS[ch], self.PT_], writes=[t_ct[sl]])

        def conv_taps(ch):
            sl = ch % 2
            acc = ctmp[sl]
            for k in range(1, 4):
                c.op("dve", lambda e, k=k: e.scalar_tensor_tensor(acc[:, 0:NP], XP[:, ch, k:k + NP], self.cwT[:, ch, k:k + 1], acc[:, 0:NP], ALU.mult, ALU.add),
                     reads=[t_XP[ch], t_ct[sl], self.PT_], writes=[t_ct[sl]])
            if self.has_s:
                for k in range(1, 4):
                    c.op("dve", lambda e, k=k: e.scalar_tensor_tensor(accS[sl], XS[:, ch, :, k:k + 8], self.cwT[:, ch, k:k + 1], accS[sl], ALU.mult, ALU.add),
                         reads=[t_XS[ch], t_ct[sl], self.PT_], writes=[t_ct[sl]])

        def conv_silu(ch):
            sl = ch % 2
            acc = ctmp[sl]
            parts = [(acc[:, 0:NP], 0, NP, [t_xc[ch][0], t_xc[ch][1]])]
            if self.has_s:
                parts.append((accS[sl].rearrange("p b i -> p (b i)"), NP, NP + 128, [t_xc[ch][2]]))
            for (src, a0, a1, wr) in parts:
                if ch < 4:
                    c.op("act", lambda e, src=src, a0=a0, a1=a1: e.activation(xcT[:, ch, a0:a1], src, AF.Silu), reads=[t_ct[sl]], writes=wr)
                elif ch == 5:
                    c.op("act", lambda e, src=src, a0=a0, a1=a1: e.activation(xcT[:, 4, a0:a1], src, AF.Silu), reads=[t_ct[sl]], writes=wr)
                else:
                    for g in range(2):
                        c.op("act", lambda e, src=src, a0=a0, a1=a1, g=g: e.activation(BTz[64 * g:64 * g + 64, g, a0:a1], src[64 * g:64 * g + 64, :], AF.Silu), reads=[t_ct[sl]], writes=wr)

        for step in range(7):
            if step < 6:
                conv_id(step)
            if step >= 1:
                conv_silu(step - 1)
            if step < 6:
                conv_taps(step)
            zt = [step] if step < 6 else list(range(6, T))
            self.proj_tm(W2, wt2, 0, 512, ev_z, tiles=zt)
        self.prefetch("w_pg", l)
        c.op("pool", lambda e: e.tensor_copy(st["ctail"][:, :, :], XP[:, :, NP:NP + 3]), reads=t_XP, writes=[st["t_ctail"]])
        if self.hf == 1:
            self.conv_out(XP[:, :, NP:NP + 3], t_XP, 3, cvo, cvt, t_cv, self.o["conv_p"].ap()[l])
            self.conv_out(XS[:, :, :, 8:11], t_XS, 48, cvo, cvt, t_cv, self.o["conv_s"].ap()[l].rearrange("b k c -> (b k) c"))
        c.op("pool", lambda e: e.memset(rows2[:, :, :], 0.0), writes=[t_rows])
        c.op("dve", lambda e: e.tensor_copy(rows_hi[0:T, :, :], rows32[0:T, :, :]), reads=[t_rows], writes=[t_rows])
        c.op("dve", lambda e: e.tensor_tensor(rows32[0:T, :, :], rows32[0:T, :, :], rows_hi[0:T, :, :], ALU.subtract), reads=[t_rows], writes=[t_rows])
        c.op("dve", lambda e: e.tensor_copy(rows_lo[0:T, :, :], rows32[0:T, :, :]), reads=[t_rows], writes=[t_rows])
        c.release(t_ct + [t_cv, t_rows])
        t_xtok = [c.new_tok(), c.new_tok()]
        t_L = c.new_tok()
        t_MT = [c.new_tok(), c.new_tok()]
        t_xd = [c.new_tok(), c.new_tok()]
        t_y = [c.new_tok(), c.new_tok()]
        t_ytk = [c.new_tok(), c.new_tok()]
        t_yss = c.new_tok()
        t_ysq = c.new_tok()
        alltoks += t_xtok + [t_L] + t_MT + t_xd + t_y + t_ytk + [t_yss, t_ysq]
        dsk = self.rep8[:, 16:24]
        def body(j):
            is_s = self.has_s and j == 8
            sl = j % 2
            gi = j // 4
            cs = slice(j * 128, (j + 1) * 128)
            tri_b = self.cb("triS") if is_s else self.cb("triP")
            ntri_b = self.cb("ntriS") if is_s else self.cb("ntriP")
            neg_b = self.cb("negS") if is_s else self.cb("negP")
            if not is_s:
                self.keepwarm(2, 3)
            tp = self.psb(2, BF16)[:, 0:768]
            for a in range(6):
                src = xcT[:, a, cs] if a < 4 else BTz[:, a - 4, cs]
                rd = [t_xc[a][gi]] if a < 4 else [t_xc[4][gi]]
                c.op("pe", lambda e, a=a, src=src, tp=tp: e.transpose(tp[:, a * 128:(a + 1) * 128], src, self.cb("ident")), reads=rd + [self.CT], writes=[self.pst[2]])
            c.op("act", lambda e, sl=sl, tp=tp: e.copy(xtok[:, sl, :], tp), reads=[self.pst[2]], writes=[t_xtok[sl]])
            x3 = xtok[:, sl, 0:512].rearrange("p (h q) -> p h q", h=8)
            c.op("dve", lambda e, sl=sl, j=j, x3=x3: e.tensor_tensor(xdt[:, sl, :].rearrange("p (h q) -> p h q", h=8), x3, dtv[:, j, :].unsqueeze(2).to_broadcast([128, 8, 64]), ALU.mult),
                 reads=[t_xtok[sl], t_dt], writes=[t_xd[sl]])
            c.op("pool", lambda e, sl=sl, j=j: e.tensor_tensor(xdtt[:, sl, :].rearrange("p (h q) -> p h q", h=8), xdt[:, sl, :].rearrange("p (h q) -> p h q", h=8),
                                                               tail[:, j, :].unsqueeze(2).to_broadcast([128, 8, 64]), ALU.mult), reads=[t_xd[sl], t_dt], writes=[t_xd[sl]])
            c.op("pool", lambda e, sl=sl, x3=x3: e.tensor_tensor(xD[:, sl, :].rearrange("p (h q) -> p h q", h=8), x3, dsk.unsqueeze(2).to_broadcast([128, 8, 64]), ALU.mult),
                 reads=[t_xtok[sl], self.PT_], writes=[t_xd[sl]])
            cbp = self.psb(7)[:, 0:256]
            t_cb = self.pst[7]
            for g in range(2):
                c.op("pe", lambda e, g=g, cs=cs, cbp=cbp: e.matmul(cbp[:, g * 128:(g + 1) * 128], lhsT=BTz[:, g, cs], rhs=xcT[:, 4, cs], start=True, stop=True),
                     reads=[t_xc[4][gi], t_xc[5][gi]], writes=[t_cb])
            for b4 in range(2):
                bank = 4 + b4
                dst = self.psb(bank)
                rd = [t_dt, self.CT, t_rows]
                wr = [self.pst[bank]]
                rh = rows2[:, 4 * b4:4 * b4 + 4, :].rearrange("p h t -> p (h t)")
                ah = adth[:, j, 4 * b4:4 * b4 + 4].unsqueeze(2).to_broadcast([128, 4, 128])
                al = adtl[:, j, 4 * b4:4 * b4 + 4].unsqueeze(2).to_broadcast([128, 4, 128])
                ng = neg_b.unsqueeze(1).to_broadcast([128, 4, 128])
                es = self.Esel[:, j, :]
                c.op("pe", lambda e, dst=dst, es=es, rh=rh: e.matmul(dst, lhsT=es, rhs=rh, start=True, stop=False), reads=rd, writes=wr)
                c.op("pe", lambda e, dst=dst, ah=ah, ntri_b=ntri_b: e.matmul(dst, lhsT=ntri_b, rhs=ah, start=False, stop=False), reads=rd, writes=wr)
                c.op("pe", lambda e, dst=dst, al=al, ntri_b=ntri_b: e.matmul(dst, lhsT=ntri_b, rhs=al, start=False, stop=False), reads=rd, writes=wr)
                c.op("pe", lambda e, dst=dst, ng=ng: e.matmul(dst, lhsT=self.cb("ident"), rhs=ng, start=False, stop=True), reads=rd, writes=wr)
            yield
            for hb in range(2):
                c.op("act", lambda e, hb=hb: e.activation(Lm[:, hb * 4:(hb + 1) * 4, :].rearrange("p h t -> p (h t)"), self.psb(4 + hb), AF.Exp),
                     reads=[self.pst[4 + hb]], writes=[t_L])
            yield
            c.op("dve", lambda e, sl=sl: e.tensor_tensor(MT[:, sl, :, :].rearrange("p (g r) t -> p g r t", g=2), Lm[:, :, :].rearrange("p (g r) t -> p g r t", g=2),
                                                         cbp.rearrange("p (g t) -> p g t", g=2).unsqueeze(2).to_broadcast([128, 2, 4, 128]), ALU.mult),
                 reads=[t_L, t_cb], writes=[t_MT[sl]])
            yield
            yp = self.psb(6)
            c.op("pe", lambda e, sl=sl: e.matmul(yp[:, :], lhsT=self.cb("ident"), rhs=xD[:, sl, :], start=True, stop=False), reads=[t_xd[sl], self.CT], writes=[self.pst[6]])
            for h in range(8):
                c.op("pe", lambda e, sl=sl, h=h: e.matmul(yp[:, h * 64:(h + 1) * 64], lhsT=MT[:, sl, h, :], rhs=xdt[:, sl, h * 64:(h + 1) * 64], start=False, stop=True),
                     reads=[t_MT[sl], t_xd[sl]], writes=[self.pst[6]])
            yi = self.psb(3)
            if not is_s:
                c.op("pe", lambda e, cs=cs: e.matmul(yi[:, :], lhsT=xcT[:, 4, cs], rhs=st["ssd_b"][:, :], start=True, stop=True), reads=[t_xc[5][gi], st["t_ssd"]], writes=[self.pst[3]])
            else:
                c.release(t_XP + t_XS + [t_rows] + t_xc[4])
                t_st = c.new_tok()
                t_bm = c.new_tok()
                t_S32 = [c.new_tok(), c.new_tok()]
                t_Stb = [c.new_tok(), c.new_tok()]
                t_Sbd = [c.new_tok(), c.new_tok()]
                alltoks.extend([t_st, t_bm] + t_S32 + t_Stb + t_Sbd)

                def st_in(grp):
                    s2 = grp % 2
                    c.dma("sp", St32[s2].rearrange("p (b a n) -> p b a n", b=2, a=4), d["st_ssm"].ap()[l][2 * grp:2 * grp + 2].rearrange("b (a hh) p n -> (hh p) b a n", hh=2),
                          writes=[t_S32[s2]], key="st%d" % s2)

                st_in(0)
                st_in(1)
                c.op("dve", lambda e, cs=cs: e.tensor_tensor(CTm[:, :, :], xcT[:, 4, cs].unsqueeze(1).to_broadcast([128, 16, 128]), self.cb("smbt").rearrange("p (b t) -> p b t", b=16), ALU.mult),
                     reads=[t_xc[5][gi], self.CT], writes=[t_st])
                for i2 in range(2):
                    c.op("pool", lambda e, i2=i2: e.memset(Stb[i2][:, :, :, :], 0.0), writes=[t_Stb[i2]])
                for g in range(2):
                    btk = xtok[:, sl, 512 + g * 128 + g * 64:512 + g * 128 + g * 64 + 64]
                    c.op("dve", lambda e, g=g, btk=btk: e.tensor_tensor(Bm[:, g, :, :], btk.unsqueeze(1).to_broadcast([128, 16, 64]), self.cb("smtok").unsqueeze(2).to_broadcast([128, 16, 64]), ALU.mult),
                         reads=[t_xtok[sl], self.CT], writes=[t_bm])
                psx = self.psb(7)
                c.op("dve", lambda e: e.tensor_copy(ysq[:, :].rearrange("p (h q) -> p h q", h=8), adt[:, 8, :].unsqueeze(2).to_broadcast([128, 8, 64])), reads=[t_dt, t_ysq], writes=[t_ysq])
                for a in range(4):
                    c.op("pe", lambda e, a=a: e.matmul(psx[:, a * 16:(a + 1) * 16], lhsT=ysq[:, a * 128:(a + 1) * 128], rhs=self.cf("smtok"), start=True, stop=True),
                         reads=[t_ysq, self.CT, t_MT[sl]], writes=[self.pst[7]])
                c.op("act", lambda e: e.activation(cdx[:, :, :].rearrange("p a b -> p (a b)"), psx[:, 0:64], AF.Exp), reads=[self.pst[7]], writes=[t_dt])
                for grp in range(8):
                    s2 = grp % 2
                    s4 = St32[s2].rearrange("p (b a n) -> p b a n", b=2, a=4)
                    c.op("dve", lambda e, s4=s4, s2=s2: e.tensor_copy(Stb[s2][:, :, 0:2, 0:64], s4[:, :, 0:2, :]), reads=[t_S32[s2]], writes=[t_Stb[s2]])
                    c.op("dve", lambda e, s4=s4, s2=s2: e.tensor_copy(Stb[s2][:, :, 2:4, 64:128], s4[:, :, 2:4, :]), reads=[t_S32[s2]], writes=[t_Stb[s2]])
                    tp2 = self.psb(2, BF16)
                    for bb in range(2):
                        for a in range(4):
                            c.op("pe", lambda e, bb=bb, a=a, tp2=tp2, s2=s2: e.transpose(tp2[:, (bb * 4 + a) * 128:(bb * 4 + a + 1) * 128], Stb[s2][:, bb, a, :], self.cb("ident")),
                                 reads=[t_Stb[s2], self.CT], writes=[self.pst[2]])
                    c.op("act", lambda e, s2=s2, tp2=tp2: e.copy(SbdT[s2][:, :, :].rearrange("p b n -> p (b n)"), tp2), reads=[self.pst[2]], writes=[t_Sbd[s2]])
                    for bb in range(2):
                        bq = grp * 2 + bb
                        c.op("pe", lambda e, bq=bq, bb=bb, s2=s2: e.matmul(yi[:, :], lhsT=CTm[:, bq, :], rhs=SbdT[s2][:, bb, :], start=(bq == 0), stop=(bq == 15)),
                             reads=[t_st, t_Sbd[s2]], writes=[self.pst[3]])
                    bank = 4 + s2
                    pu = self.psb(bank)
                    for a in range(4):
                        g = a // 2
                        c.op("pe", lambda e, a=a, g=g, pu=pu, grp=grp: e.matmul(pu[:, a * 128:(a + 1) * 128], lhsT=xdtt[:, sl, a * 128:(a + 1) * 128],
                                                                                rhs=Bm[:, g, 2 * grp:2 * grp + 2, :].rearrange("p b n -> p (b n)"), start=True, stop=True),
                             reads=[t_xd[sl], t_bm, t_L], writes=[self.pst[bank]])
                    cdv = cdx[:, :, 2 * grp:2 * grp + 2].rearrange("p a b -> p b a").unsqueeze(3).to_broadcast([128, 2, 4, 64])
                    c.op("dve", lambda e, s4=s4, cdv=cdv: e.tensor_tensor(s4, s4, cdv, ALU.mult), reads=[t_S32[s2], t_dt, t_Stb[s2]], writes=[t_S32[s2]])
                    c.op("dve", lambda e, s4=s4, pu=pu: e.tensor_tensor(s4, s4, pu[:, :].rearrange("p (a b n) -> p b a n", a=4, b=2), ALU.add), reads=[t_S32[s2], self.pst[bank]], writes=[t_S32[s2]])
                    c.dma("sp", self.o["ssm_s"].ap()[l][2 * grp:2 * grp + 2].rearrange("b (a hh) p n -> (hh p) b a n", hh=2), s4, reads=[t_S32[s2]], key="so%d" % s2)
                    if grp + 2 < 8:
                        st_in(grp + 2)
            if not is_s:
                up = self.psb(0)
                for g in range(2):
                    c.op("pe", lambda e, g=g, sl=sl: e.matmul(up[:, g * 256:(g + 1) * 256], lhsT=xtok[:, sl, 512 + g * 128:512 + (g + 1) * 128], rhs=xdtt[:, sl, g * 256:(g + 1) * 256],
                                                              start=True, stop=True), reads=[t_xtok[sl], t_xd[sl], t_MT[sl]], writes=[self.pst[0]])
                S3 = st["ssd"][:, :].rearrange("p (h q) -> p h q", h=8)
                c.op("dve", lambda e, S3=S3, j=j: e.tensor_tensor(S3, S3, cdrep[:, j, :].unsqueeze(2).to_broadcast([128, 8, 64]), ALU.mult), reads=[st["t_ssd"], t_dt], writes=[st["t_ssd"]])
                c.op("dve", lambda e: e.tensor_tensor(st["ssd"][:, :], st["ssd"][:, :], up[:, :], ALU.add), reads=[st["t_ssd"], self.pst[0]], writes=[st["t_ssd"]])
                c.op("act", lambda e: e.copy(st["ssd_b"][:, :], st["ssd"][:, :]), reads=[st["t_ssd"]], writes=[st["t_ssd"]])
                if self.hf == 1 and j == 7:
                    pso = self.psb(0)
                    for a in range(4):
                        c.op("pe", lambda e, a=a: e.transpose(pso[:, a * 128:(a + 1) * 128], st["ssd"][:, a * 128:(a + 1) * 128], self.cf("identf")), reads=[st["t_ssd"], self.CT], writes=[self.pst[0]])
                    so = ysq[:, 0:256].rearrange("p (a n) -> p a n", a=4)
                    for g in range(2):
                        c.op("dve", lambda e, g=g: e.tensor_copy(so[:, 2 * g:2 * g + 2, :], pso[:, :].rearrange("p (a n) -> p a n", a=4)[:, 2 * g:2 * g + 2, 64 * g:64 * g + 64]),
                             reads=[self.pst[0], t_ysq], writes=[t_ysq])
                    c.dma("sp", self.o["ssm_p"].ap()[l].rearrange("(a hh) p n -> (hh p) a n", hh=2), so, reads=[t_ysq], key="oss")
            yield
            y3 = ytmp[:, sl, :].rearrange("p (h q) -> p h q", h=8)
            c.op("dve", lambda e, y3=y3, j=j: e.tensor_tensor(y3, yi[:, :].rearrange("p (h q) -> p h q", h=8), eacum[:, j, :].unsqueeze(2).to_broadcast([128, 8, 64]), ALU.mult),
                 reads=[self.pst[3], t_dt], writes=[t_y[sl]])
            c.op("dve", lambda e, sl=sl: e.tensor_tensor(ytmp[:, sl, :], yp[:, :], ytmp[:, sl, :], ALU.add), reads=[self.pst[6], t_y[sl]], writes=[t_y[sl]])
            yield
            c.op("pool", lambda e, sl=sl, j=j: e.tensor_tensor(ytmp[:, sl, :], ytmp[:, sl, :], sz[:, j, :], ALU.mult), reads=[t_y[sl], t_sz[j]], writes=[t_y[sl]])
            for g in range(2):
                c.op("act", lambda e, sl=sl, g=g: e.activation(ysq[:, g * 256:(g + 1) * 256], ytmp[:, sl, g * 256:(g + 1) * 256], AF.Square, accum_out=yss[:, sl, g:g + 1]),
                     reads=[t_y[sl], t_yss], writes=[t_ysq, t_yss])
            c.op("act", lambda e, sl=sl: e.activation(yss[:, sl, 2:4], yss[:, sl, 0:2], AF.Ln, bias=EPS, scale=1.0 / 256), reads=[t_yss], writes=[t_yss])
            c.op("act", lambda e, sl=sl: e.activation(yss[:, sl, 2:4], yss[:, sl, 2:4], AF.Exp, scale=-0.5), reads=[t_yss], writes=[t_yss])
            c.op("dve", lambda e, sl=sl: e.tensor_tensor(ytmp[:, sl, :].rearrange("p (g n) -> p g n", g=2), ytmp[:, sl, :].rearrange("p (g n) -> p g n", g=2),
                                                         yss[:, sl, 2:4].unsqueeze(2).to_broadcast([128, 2, 256]), ALU.mult), reads=[t_y[sl], t_yss], writes=[t_y[sl]])
            c.op("pool", lambda e, sl=sl: e.tensor_tensor(ytk[:, sl, :], ytmp[:, sl, :], self.ssdnw[:, :], ALU.mult), reads=[t_y[sl], self.PT_], writes=[t_ytk[sl]])
            yield
            tpy = self.psb(1, BF16)[:, 0:1024]
            for a in range(4):
                c.op("pe", lambda e, a=a, tpy=tpy, sl=sl: e.transpose(tpy[:, a * 128:(a + 1) * 128], ytk[:, sl, a * 128:(a + 1) * 128], self.cb("ident")), reads=[t_ytk[sl], self.CT], writes=[self.pst[1]])
            for a in range(4):
                c.op("pe", lambda e, a=a, tpy=tpy: e.transpose(tpy[:, (4 + a) * 128:(5 + a) * 128], self.ytok[:, j, a * 128:(a + 1) * 128], self.cb("ident")), reads=[self.t_ytok, self.CT], writes=[self.pst[1]])
            c.op("act", lambda e, tpy=tpy, cs=cs: e.copy(self.uT[:, 0:8, cs], tpy.rearrange("p (a t) -> p a t", a=8)), reads=[self.pst[1]], writes=[self.t_uTg[gi]])
        self.pipeline_sched([body(j) for j in range(T)], [(6, 4), (4, 2), (0, 0), (1, 0), (3, 1), (2, 0), (5, 2)])
        c.release(alltoks + t_XP + t_XS)

    def conv_out(self, src, toks, nrow, cvo, cvt, t_cv, dst):
        c = self.c
        if nrow == 3:
            c.op("dve", lambda e: e.tensor_copy(cvo[:, :, 0:3], src), reads=toks, writes=[t_cv])
        else:
            c.op("dve", lambda e: e.tensor_copy(cvo[:, :, :].rearrange("p c (b k) -> p c b k", b=16), src), reads=toks, writes=[t_cv])
        for half, (ch0, nch) in enumerate(((0, 4), (4, 2))):
            ps = self.psb(half)
            for cc in range(nch):
                ch = ch0 + cc
                c.op("pe", lambda e, ps=ps, cc=cc, ch=ch: e.transpose(ps[0:nrow, cc * 128:(cc + 1) * 128], cvo[:, ch, 0:nrow], self.cf("identf")), reads=[t_cv, self.CT], writes=[self.pst[half]])
            c.op("act", lambda e, ps=ps, ch0=ch0, nch=nch: e.copy(cvt[0:nrow, ch0 * 128:(ch0 + nch) * 128], ps[0:nrow, 0:nch * 128]), reads=[self.pst[half]], writes=[t_cv])
        c.dma("sp", dst, cvt[0:nrow, :], reads=[t_cv], key="ocv")

    def y_transposes(self):
        c = self.c
        for j in range(self.T):
            cs = slice(j * 128, (j + 1) * 128)
            tp = self.psb(2, BF16)[:, 0:512]
            for a in range(4):
                c.op("pe", lambda e, a=a, j=j, tp=tp: e.transpose(tp[:, a * 128:(a + 1) * 128], self.ytok[:, j, a * 128:(a + 1) * 128], self.cb("ident")),
                     reads=[self.t_ytok, self.CT], writes=[self.pst[2]])
            c.op("act", lambda e, tp=tp, cs=cs: e.copy(self.uT[:, 4:8, cs], tp.rearrange("p (a t) -> p a t", a=4)), reads=[self.pst[2]], writes=[self.t_uTg[j // 4]])
        c.release([self.t_ytok])

    def out_proj(self, MOFF):
        c, d, l = self.c, self.d, self.l
        self.hbT = self.av(0, [128, 8, 1152], BF16)
        self.t_hb = [c.new_tok() for _ in self.groups]
        self.ptok = self.av(18432, [128, 9, 256], BF16)
        self.t_pt = c.new_tok()
        c.dma("pool", self.ptok[:, 0:self.T, :], d["p"].ap()[l][self.row0:self.row0 + self.NT, :].rearrange("(j p) n -> p j n", p=128), writes=[self.t_pt], key="ptk")
        W, wt = self.getw("w_out", l)
        for dch in range(8):
            def ev(ps, gi, c0, c1, bt, dch=dch):
                c.op("dve", lambda e: e.tensor_tensor(self.hT[:, dch, c0:c1], self.hT[:, dch, c0:c1], ps, ALU.add), reads=[bt, self.t_hT[dch][gi]], writes=[self.t_hT[dch][gi]])
                c.op("act", lambda e: e.copy(self.hbT[:, dch, c0:c1], self.hT[:, dch, c0:c1]), reads=[self.t_hT[dch][gi]], writes=[self.t_hb[gi]])
            self.proj_fm(W, wt, dch * 128, 128, self.uT, lambda gi: [self.t_uTg[gi]], ev)
        self.prefetch("w_pe", l)

    def ple(self, MOFF):
        c, d, l, T, NT = self.c, self.d, self.l, self.T, self.NT
        ptok = self.ptok
        sgt = [self.av(18432 + 4608 + i * 2048, [128, 512], F32) for i in range(2)]
        t_pt = self.t_pt
        t_sg = [c.new_tok(), c.new_tok()]
        t_pT = [c.new_tok() for _ in self.groups]
        for j in range(T):
            cs = slice(j * 128, (j + 1) * 128)
            tp = self.psb(2, BF16)[:, 0:256]
            for a in range(2):
                c.op("pe", lambda e, a=a, j=j, tp=tp: e.transpose(tp[:, a * 128:(a + 1) * 128], ptok[:, j, a * 128:(a + 1) * 128], self.cb("ident")), reads=[t_pt, self.CT], writes=[self.pst[2]])
            c.op("act", lambda e, tp=tp, cs=cs: e.copy(self.pT[:, :, cs], tp.rearrange("p (a t) -> p a t", a=2)), reads=[self.pst[2]], writes=[t_pT[j // 4]])
        Wg, wtg = self.getw("w_pg", l)
        We, wte = self.getw("w_pe", l)
        it = 0
        for dch in range(8):
            for gi, (c0, c1) in enumerate(self.groups):
                n = c1 - c0
                s = it % 2
                it += 1
                pa, pb = self.psb(0 + 2 * s), self.psb(1 + 2 * s)
                ba, bb = 0 + 2 * s, 1 + 2 * s
                for k in range(8):
                    c.op("pe", lambda e, k=k, pa=pa, n=n, c0=c0, c1=c1, dch=dch: e.matmul(pa[:, 0:n], lhsT=Wg[:, k, dch * 128:(dch + 1) * 128], rhs=self.hbT[:, k, c0:c1], start=(k == 0), stop=(k == 7)),
                         reads=[wtg, self.t_hb[gi]], writes=[self.pst[ba]])
                for k in range(2):
                    c.op("pe", lambda e, k=k, pb=pb, n=n, c0=c0, c1=c1, dch=dch: e.matmul(pb[:, 0:n], lhsT=We[:, k, dch * 128:(dch + 1) * 128], rhs=self.pT[:, k, c0:c1], start=(k == 0), stop=(k == 1)),
                         reads=[wte, t_pT[gi]], writes=[self.pst[bb]])
                c.op("act", lambda e, s=s, pa=pa, n=n: e.activation(sgt[s][:, 0:n], pa[:, 0:n], AF.Sigmoid), reads=[self.pst[ba]], writes=[t_sg[s]])
                c.op("dve", lambda e, s=s, pb=pb, n=n: e.tensor_tensor(sgt[s][:, 0:n], sgt[s][:, 0:n], pb[:, 0:n], ALU.mult), reads=[self.pst[bb], t_sg[s]], writes=[t_sg[s]])
                c.op("pool", lambda e, s=s, n=n, dch=dch, c0=c0, c1=c1: e.tensor_tensor(self.hT[:, dch, c0:c1], self.hT[:, dch, c0:c1], sgt[s][:, 0:n], ALU.add),
                     reads=[t_sg[s], self.t_hT[dch][gi]], writes=[self.t_hT[dch][gi]])
        if self.next_unit is not None:
            self.prefetch("swa", self.next_unit[0])
        c.release([t_pt] + t_sg + t_pT + self.t_hb)

    def store_y(self, MOFF):
        c, T = self.c, self.T
        base = 18432 + 4608 + 4096
        ost = [self.av(base + i * 4096, [128, 1024], F32) for i in range(2)]
        t_o = [c.new_tok(), c.new_tok()]
        for j in range(T):
            s = j % 2
            gi = j // 4
            cs = slice(j * 128, (j + 1) * 128)
            for half in range(2):
                bank = 4 + half
                ps = self.psb(bank)
                for kk in range(4):
                    k = half * 4 + kk
                    c.op("pe", lambda e, ps=ps, kk=kk, k=k, cs=cs: e.transpose(ps[:, kk * 128:(kk + 1) * 128], self.hT[:, k, cs], self.cf("identf")),
                         reads=[self.t_hT[k][gi], self.CT], writes=[self.pst[bank]])
                if half == 0:
                    c.op("act", lambda e, ps=ps, s=s: e.copy(ost[s][:, 0:512], ps[:, :]), reads=[self.pst[bank]], writes=[t_o[s]])
                else:
                    c.op("dve", lambda e, ps=ps, s=s: e.tensor_copy(ost[s][:, 512:1024], ps[:, :]), reads=[self.pst[bank]], writes=[t_o[s]])
            c.dma("sp", self.o["y"].ap()[self.row0 + j * 128:self.row0 + (j + 1) * 128, :], ost[s], reads=[t_o[s]], key="oy%d" % s)
        c.release(t_o + [t for row in self.t_hT for t in row])


_CACHE = {}


def _get_nc():
    if "nc" not in _CACHE:
        k = K()
        _CACHE["nc"] = k.build()
        _CACHE["k"] = k
    return _CACHE["nc"]


def kernel(x_prompt, x_sample, state_ssm, state_conv, state_gla, cache_swa_k, cache_swa_v, p_prompt, p_sample,
           rel_bias, norm_w, w_in, conv_w, conv_b, dt_bias, a_log, d_skip, ssd_norm_w, gla_w_gk, gla_b_gk,
           gla_norm_w, q_norm_w, k_norm_w, attn_sinks, w_out, w_pe, w_pg):
    f = lambda a: np.ascontiguousarray(np.asarray(a, dtype=np.float32))
    x_prompt, x_sample, state_ssm, state_conv, state_gla = map(f, (x_prompt, x_sample, state_ssm, state_conv, state_gla))
    cache_swa_k, cache_swa_v, p_prompt, p_sample = map(f, (cache_swa_k, cache_swa_v, p_prompt, p_sample))
    w_in = f(w_in)
    sq0 = 2072
    perm = np.concatenate([
        sq0 + np.concatenate([np.arange(0, 64), np.arange(128, 192), np.arange(64, 128), np.arange(192, 256)]),
        np.arange(2328, 2456), np.arange(2456, 2584), np.arange(2584, 2840),
        np.arange(1288, 1416), np.arange(1416, 1544), np.arange(2056, 2072), np.arange(1416, 1544), np.arange(1544, 1800), np.arange(1800, 2056),
        np.arange(512, 1280), np.arange(0, 512), np.arange(1280, 1288)])
    assert perm.shape[0] == 2968
    w_in_p = np.ascontiguousarray(w_in[:, :, perm])
    cstf, cstb, oh = make_consts()
    rep8 = np.concatenate([f(dt_bias), f(a_log), f(d_skip)], axis=1)
    wgk = np.concatenate([f(gla_w_gk), f(gla_b_gk)[:, None, :]], axis=1)
    hdw = np.concatenate([np.tile(f(q_norm_w), (1, 4)), np.tile(f(k_norm_w), (1, 2)), f(gla_norm_w)], axis=1)
    shared = dict(rel_bias=f(rel_bias), norm_w=f(norm_w), w_in=w_in_p, conv_w=f(conv_w), conv_b=f(conv_b), rep8=rep8,
                  ssd_norm_w=f(ssd_norm_w), wgk=np.ascontiguousarray(wgk), hdw=np.ascontiguousarray(hdw), sinks=f(attn_sinks),
                  w_out=f(w_out), w_pe=f(w_pe), w_pg=f(w_pg), cstf=cstf, cstb=cstb, oh=oh)
    in_maps = []
    for core in range(8):
        sl = slice(16 * core, 16 * core + 16)
        m = dict(shared)
        m["x_tok"] = np.ascontiguousarray(np.concatenate([x_prompt[core], x_sample[sl].reshape(128, 1024)], axis=0))
        m["p_tok"] = np.ascontiguousarray(np.concatenate([p_prompt[:, core], p_sample[:, sl].reshape(2, 128, 256)], axis=1))
        m["st_ssm"] = np.ascontiguousarray(state_ssm[:, sl])
        m["st_conv"] = np.ascontiguousarray(state_conv[:, sl])
        m["st_gla"] = np.ascontiguousarray(state_gla[:, sl])
        m["ck"] = np.ascontiguousarray(cache_swa_k[:, sl].reshape(2, 16, 128, 128))
        m["cv"] = np.ascontiguousarray(cache_swa_v[:, sl].reshape(2, 16, 128, 128))
        in_maps.append(m)
    nc = _get_nc()
    res = run_bass_kernel_spmd(nc, in_maps, core_ids=list(range(8)))
    R = res.results
    cat = lambda name, ax: np.concatenate([np.asarray(r[name]) for r in R], axis=ax)
    stk = lambda name: np.stack([np.asarray(r[name]) for r in R], axis=1)
    y_all = np.stack([np.asarray(r["y_tok"]) for r in R], axis=0)
    y_prompt = np.ascontiguousarray(y_all[:, :2048])
    y_sample = np.ascontiguousarray(y_all[:, 2048:].reshape(128, 8, 1024))
    ssm_p = stk("ssm_p")
    conv_p = stk("conv_p")
    gla_p = stk("gla_p")
    swak_p = stk("swak_p").reshape(2, 8, 128, 2, 64)
    swav_p = stk("swav_p").reshape(2, 8, 128, 2, 64)
    ssm_s = cat("ssm_s", 1)
    conv_s = cat("conv_s", 1)
    gla_s = cat("gla_s", 1)
    swak_s = cat("swak_s", 1).reshape(2, 128, 128, 2, 64)
    swav_s = cat("swav_s", 1).reshape(2, 128, 128, 2, 64)
    outs = (y_prompt, y_sample, ssm_p, conv_p, gla_p, swak_p, swav_p, ssm_s, conv_s, gla_s, swak_s, swav_s)
    if DEBUG:
        _CACHE["dbg"] = {nm: [np.asarray(r[nm]) for r in R] for nm in DEBUG}
    return tuple(np.ascontiguousarray(o.astype(np.float32)) for o in outs)
```

```python
import numpy as np, math
from contextlib import ExitStack
import concourse.bass as bass
import concourse.mybir as mybir
from concourse.bass_utils import run_bass_kernel_spmd

F32 = mybir.dt.float32
BF16 = mybir.dt.bfloat16
AF = mybir.ActivationFunctionType
ALU = mybir.AluOpType
AX = mybir.AxisListType


NO_ELIDE = False


class Tok:
    __slots__ = ("w", "r", "const", "excl")

    def __init__(self, const=False, excl=False):
        self.w = None
        self.r = {}
        self.const = const
        self.excl = excl


class Op:
    __slots__ = ("eng", "fn", "deps", "sig", "val", "kind", "key", "seq")
    _n = 0

    def __init__(self, eng, fn, kind="op", key=None):
        Op._n += 1
        self.seq = Op._n
        self.eng = eng
        self.fn = fn
        self.deps = {}
        self.sig = False
        self.val = None
        self.kind = kind
        self.key = key


class Ctx:
    ENGS = ("pe", "act", "dve", "pool", "sp")

    def __init__(self, nc, es):
        self.nc = nc
        self.es = es
        self.sem = {k: es.enter_context(nc.semaphore("s_" + k)) for k in self.ENGS}
        self.ops = {k: [] for k in self.ENGS}
        self.dsem = {}
        self.dcnt = {}
        self.free_deps = {}

    def _skey(self, op):
        return op.key if op.kind == "dma" else op.eng

    def _adddep(self, op, p):
        if p is None or p is op:
            return
        k = self._skey(p)
        q = op.deps.get(k)
        if q is None or q.seq < p.seq:
            op.deps[k] = p

    def _track(self, op, reads, writes):
        k0 = self._skey(op)
        for t in reads:
            self._adddep(op, t.w)
            if t.excl:
                for kk, p in t.r.items():
                    if kk != k0:
                        self._adddep(op, p)
        for t in writes:
            self._adddep(op, t.w)
            for p in t.r.values():
                self._adddep(op, p)
        k = self._skey(op)
        for t in reads:
            if not t.const:
                t.r[k] = op
        for t in writes:
            t.w = op
            t.r = {}

    def op(self, eng, fn, reads=(), writes=()):
        o = Op(eng, fn)
        self._track(o, reads, writes)
        self.ops[eng].append(o)
        return o

    def dma(self, q, out_ap, in_ap, reads=(), writes=(), key=None, nc_ok=False):
        if key not in self.dsem:
            self.dsem[key] = self.es.enter_context(self.nc.semaphore("d_" + key))
            self.dcnt[key] = 0
        if nc_ok:
            fn = lambda e: e.dma_start(out=out_ap, in_=in_ap, allow_slow_non_contiguous=True)
        else:
            fn = lambda e: e.dma_start(out=out_ap, in_=in_ap)
        o = Op(q, fn, kind="dma", key=key)
        self._track(o, reads, writes)
        self.dcnt[key] += 16
        o.val = self.dcnt[key]
        self.ops[q].append(o)
        return o

    def new_tok(self, const=False):
        t = Tok(const)
        t.r = dict(self.free_deps)
        return t

    def release(self, toks):
        for t in toks:
            for p in list(t.r.values()) + ([t.w] if t.w is not None else []):
                k = self._skey(p)
                q = self.free_deps.get(k)
                if q is None or q.seq < p.seq:
                    self.free_deps[k] = p

    def emit(self, final_eng="sp"):
        for e in self.ENGS:
            for o in self.ops[e]:
                for p in o.deps.values():
                    if p.kind == "op" and not (p.eng == "pe" and o.eng == "pe" and o.kind == "op"):
                        p.sig = True
        for e in self.ENGS:
            n = 0
            for o in self.ops[e]:
                if o.kind == "op" and o.sig:
                    n += 1
                    o.val = n
        fin = [(self.dsem[k], self.dcnt[k]) for k in self.dsem]
        nsig = sum(1 for e in self.ENGS for o in self.ops[e] if o.sig)
        nops = sum(len(self.ops[e]) for e in self.ENGS)
        print("ops", {e: len(self.ops[e]) for e in self.ENGS}, "signals", nsig, "dma keys", len(self.dsem))

        class _PEProxy:
            def __init__(self, e):
                self._e = e

            def __getattr__(self, n):
                return getattr(self._e, n)

            def matmul(self, *a, **k):
                k.setdefault("skip_group_check", True)
                return self._e.matmul(*a, **k)

        def run(ename, e):
            if ename == "pe":
                e = _PEProxy(e)
            waited = {}
            for o in self.ops[ename]:
                for p in o.deps.values():
                    if p.kind == "op":
                        if p.eng == "pe" and ename == "pe" and o.kind == "op":
                            continue
                        sem = self.sem[p.eng]
                        sk = p.eng
                    else:
                        sem = self.dsem[p.key]
                        sk = p.key
                    if NO_ELIDE or waited.get(sk, 0) < p.val:
                        waited[sk] = max(waited.get(sk, 0), p.val)
                        e.wait_ge(sem, p.val)
                ins = o.fn(e)
                if o.kind == "dma":
                    ins.then_inc(self.dsem[o.key], 16)
                elif o.sig:
                    ins.then_inc(self.sem[ename], 1)
            if ename == final_eng:
                for (s, v) in fin:
                    e.wait_ge(s, v)

        with self.nc.Block() as block:
            @block.tensor
            def _(e):
                run("pe", e)

            @block.scalar
            def _(e):
                run("act", e)

            @block.vector
            def _(e):
                run("dve", e)

            @block.gpsimd
            def _(e):
                run("pool", e)

            @block.sync
            def _(e):
                run("sp", e)


NEGB = -240000.0
EPS = 1e-6
ENABLE = {"swa": True, "gla": True, "ssd": True}
PIPE = {"swa": True, "gla": True, "ssd": True}
NOY1 = False
DEBUG = {}

CF = dict(identf=(0, 128), triP=(128, 256), triS=(256, 384), supP=(384, 512), supS=(512, 640), onesf=(640, 768),
          sameS=(768, 896), smtok=(896, 912), bd2=(912, 914), bd4=(914, 918))
NF = 920
CB = dict(ident=(0, 128), triP=(128, 256), triS=(256, 384), ntriP=(384, 512), ntriS=(512, 640), negP=(640, 768),
          negS=(768, 896), ones=(896, 1024), smbt=(1024, 3072), smtok=(3072, 3088), bd2=(3088, 3090), bd4=(3090, 3094))
NB = 3096


def make_consts():
    k = np.arange(128)[:, None]
    t = np.arange(128)[None, :]
    same = (k // 8 == t // 8)
    f = np.zeros((128, NF), np.float32)
    b = np.zeros((128, NB), np.float32)

    def put(arr, d, name, val):
        a, e = d[name]
        arr[:, a:e] = val

    put(f, CF, "identf", (k == t))
    put(f, CF, "triP", (k <= t))
    put(f, CF, "triS", same & (k <= t))
    put(f, CF, "supP", (k > t))
    put(f, CF, "supS", same & (k > t))
    put(f, CF, "onesf", 1.0)
    put(f, CF, "sameS", same)
    put(f, CF, "smtok", (k // 8 == np.arange(16)[None, :]))
    put(f, CF, "bd2", (k // 64 == np.arange(2)[None, :]))
    put(f, CF, "bd4", (k // 32 == np.arange(4)[None, :]))
    put(b, CB, "ident", (k == t))
    put(b, CB, "triP", (k <= t))
    put(b, CB, "triS", same & (k <= t))
    put(b, CB, "ntriP", -1.0 * (k <= t))
    put(b, CB, "ntriS", -1.0 * (same & (k <= t)))
    put(b, CB, "negP", np.where(k <= t, 0.0, -30000.0))
    put(b, CB, "negS", np.where(same & (k <= t), 0.0, -30000.0))
    put(b, CB, "ones", 1.0)
    smbt = (np.arange(16)[:, None] == (np.arange(128)[None, :] // 8)).astype(np.float32).reshape(1, 2048)
    put(b, CB, "smbt", np.broadcast_to(smbt, (128, 2048)))
    put(b, CB, "smtok", (k // 8 == np.arange(16)[None, :]))
    put(b, CB, "bd2", (k // 64 == np.arange(2)[None, :]))
    put(b, CB, "bd4", (k // 32 == np.arange(4)[None, :]))
    oh = np.zeros((33, 384), np.float32)
    for i in range(384):
        dist = i - 127
        if 0 <= dist < 128:
            n = dist
            if n < 16:
                bk = n
            else:
                nf = np.float32(max(n, 1))
                v = np.log(nf / np.float32(16)) / np.float32(math.log(128 / 16)) * np.float32(16)
                bk = min(16 + int(np.int32(v)), 31)
            oh[bk, i] = 1.0
        else:
            oh[32, i] = 1.0
    return f, b, oh


class K:
    def __init__(self):
        self.nc = bass.Bass("TRN2", target_bir_lowering=False)
        self.es = ExitStack()

    def din(self, name, shape):
        return self.nc.dram_tensor(name, list(shape), F32, kind="ExternalInput")

    def dout(self, name, shape):
        return self.nc.dram_tensor(name, list(shape), F32, kind="ExternalOutput")

    def sb(self, name, shape, dt):
        return self.es.enter_context(self.nc.sbuf_tensor(name, list(shape), dt))

    def av(self, off, shape, dt):
        n = int(np.prod(shape[1:]))
        nb = n * (4 if dt == F32 else 2)
        assert off % 4 == 0 and off + nb <= self.ABYTES, (off, nb, self.ABYTES)
        ap = self.arena[:, off // 2:(off + nb) // 2]
        if dt == F32:
            ap = ap.bitcast(F32)
        if len(shape) == 3:
            ap = ap.rearrange("p (a b) -> p a b", a=shape[1])
        elif len(shape) == 4:
            ap = ap.rearrange("p (a b c) -> p a b c", a=shape[1], b=shape[2])
        elif len(shape) == 5:
            ap = ap.rearrange("p (a b c d) -> p a b c d", a=shape[1], b=shape[2], c=shape[3])
        return ap

    def cf(self, name):
        a, e = CF[name]
        return self.cstf[:, a:e]

    def cb(self, name):
        a, e = CB[name]
        return self.cstb[:, a:e]

    def psb(self, bank, dt=F32):
        t = self.psum[bank]
        ap = t[:, :]
        if dt == BF16:
            ap = ap.bitcast(BF16)
        return ap

    def build(self):
        nc, es = self.nc, self.es
        c = self.c = Ctx(nc, es)
        d = self.d = {}
        d["x"] = self.din("x_tok", [2176, 1024])
        d["p"] = self.din("p_tok", [2, 2176, 256])
        d["st_ssm"] = self.din("st_ssm", [2, 16, 8, 64, 64])
        d["st_conv"] = self.din("st_conv", [2, 16, 3, 768])
        d["st_gla"] = self.din("st_gla", [2, 16, 4, 32, 64])
        d["ck"] = self.din("ck", [2, 16, 128, 128])
        d["cv"] = self.din("cv", [2, 16, 128, 128])
        d["rel_bias"] = self.din("rel_bias", [32, 4])
        d["norm_w"] = self.din("norm_w", [2, 1024])
        d["w_in"] = self.din("w_in", [2, 1024, 2968])
        d["conv_w"] = self.din("conv_w", [2, 4, 768])
        d["conv_b"] = self.din("conv_b", [2, 768])
        d["rep8"] = self.din("rep8", [2, 24])
        d["ssd_norm_w"] = self.din("ssd_norm_w", [2, 512])
        d["wgk"] = self.din("wgk", [2, 17, 128])
        d["hdw"] = self.din("hdw", [2, 7 * 64])
        d["sinks"] = self.din("sinks", [2, 4])
        d["w_out"] = self.din("w_out", [2, 1024, 1024])
        d["w_pe"] = self.din("w_pe", [2, 256, 1024])
        d["w_pg"] = self.din("w_pg", [2, 1024, 1024])
        d["cstf"] = self.din("cstf", [128, NF])
        d["cstb"] = self.din("cstb", [128, NB])
        d["oh"] = self.din("oh", [33, 384])
        o = self.o = {}
        o["y"] = self.dout("y_tok", [2176, 1024])
        o["ssm_p"] = self.dout("ssm_p", [2, 8, 64, 64])
        o["conv_p"] = self.dout("conv_p", [2, 3, 768])
        o["gla_p"] = self.dout("gla_p", [2, 4, 32, 64])
        o["swak_p"] = self.dout("swak_p", [2, 128, 128])
        o["swav_p"] = self.dout("swav_p", [2, 128, 128])
        o["ssm_s"] = self.dout("ssm_s", [2, 16, 8, 64, 64])
        o["conv_s"] = self.dout("conv_s", [2, 16, 3, 768])
        o["gla_s"] = self.dout("gla_s", [2, 16, 4, 32, 64])
        o["swak_s"] = self.dout("swak_s", [2, 16, 128, 128])
        o["swav_s"] = self.dout("swav_s", [2, 16, 128, 128])
        self.scr = nc.dram_tensor("scr_bias", [128, 1536], F32)
        self.scr_ac = nc.dram_tensor("scr_acum", [128, 72], F32)
        self.t_scr_ac = Tok()
        for nm, shp in DEBUG.items():
            o[nm] = self.dout(nm, shp)

        self.hT = self.sb("hT", [128, 8, 1152], F32)
        self.uT = self.sb("uT", [128, 8, 1152], BF16)
        self.pT = self.sb("pT", [128, 2, 1152], BF16)
        self.W = [self.sb("W0", [128, 8, 1024], BF16), self.sb("W1", [128, 8, 1024], BF16)]
        self.Wt = [Tok(), Tok()]
        self.wi = 0
        self.cstf = self.sb("cstf_sb", [128, NF], F32)
        self.cstb = self.sb("cstb_sb", [128, NB], BF16)
        self.CT = Tok(const=True)
        self.biasP = [self.sb("biasP_hi", [128, 2, 2, 2, 128], BF16), self.sb("biasP_lo", [128, 2, 2, 2, 128], BF16)]
        self.biasS = [self.sb("biasS_hi", [128, 2, 2, 128], BF16), self.sb("biasS_lo", [128, 2, 2, 128], BF16)]
        self.biasC = [self.sb("biasC_hi", [128, 16, 2, 2, 8], BF16), self.sb("biasC_lo", [128, 16, 2, 2, 8], BF16)]
        self.nwT = self.sb("nwT", [128, 8], F32)
        self.cwT = self.sb("cwT", [128, 6, 4], F32)
        self.cbT = self.sb("cbT", [128, 6], F32)
        self.rep8 = self.sb("rep8_sb", [128, 24], F32)
        self.arep = self.sb("arep", [128, 8], F32)
        self.ssdnw = self.sb("ssdnw", [128, 512], F32)
        self.hdw = self.sb("hdw_sb", [128, 7, 64], F32)
        self.esink = self.sb("esink", [128, 4], F32)
        self.wgkf = self.sb("wgkf", [17, 128], F32)
        self.wgk = self.sb("wgkb", [17, 128], BF16)
        self.PT_ = Tok()
        self.Esel = self.sb("Esel", [128, 9, 128], BF16)
        self.S = []
        for l in range(2):
            st = dict(
                ssd=self.sb("Sssd%d" % l, [128, 512], F32), ssd_b=self.sb("Sssdb%d" % l, [128, 512], BF16),
                gla=self.sb("Sgla%d" % l, [128, 256], F32), gla_b=self.sb("Sglab%d" % l, [128, 256], BF16),
                ctail=self.sb("ctail%d" % l, [128, 6, 3], BF16),
                KTz=[self.sb("KTz%d_%d" % (l, i), [128, 2, 128], BF16) for i in range(3)],
                vext=[self.sb("vext%d_%d" % (l, i), [128, 2, 65], BF16) for i in range(4)],
                t_ssd=Tok(), t_gla=Tok(), t_ctail=Tok(), t_kv=[Tok(), Tok(), Tok()], t_vx=[Tok(), Tok(), Tok(), Tok()],
            )
            self.S.append(st)
        self.psum = [es.enter_context(nc.psum_tensor("ps%d" % i, [128, 512], F32)) for i in range(8)]
        self.pst = [Tok(excl=True) for _ in range(8)]
        self.ABYTES = 78 * 1024
        self.arena = self.sb("arena", [128, self.ABYTES // 2], BF16)

        self.marks = []
        self.wq = {}
        self.setup()
        import os
        self.trunc = os.environ.get("TRUNC")
        order = [(0, 0), (1, 0), (0, 1), (1, 1)]
        for ui, (l, hf) in enumerate(order):
            self.next_unit = order[ui + 1] if ui + 1 < len(order) else None
            self.unit(l, hf)
            if self.trunc:
                break
        c.emit()
        return nc

    def setup(self):
        c, d = self.c, self.d
        c.dma("sp", self.cstf[:, :], d["cstf"].ap(), writes=[self.CT], key="cst")
        c.dma("pool", self.cstb[:, :], d["cstb"].ap(), writes=[self.CT], key="cstb")
        A = Tok()
        for l in range(2):
            st = self.S[l]
            c.op("pool", lambda e, st=st: e.memset(st["ssd"][:, :], 0.0), writes=[st["t_ssd"]])
            c.op("pool", lambda e, st=st: e.memset(st["ssd_b"][:, :], 0.0), writes=[st["t_ssd"]])
            c.op("pool", lambda e, st=st: e.memset(st["gla"][:, :], 0.0), writes=[st["t_gla"]])
            c.op("pool", lambda e, st=st: e.memset(st["gla_b"][:, :], 0.0), writes=[st["t_gla"]])
            c.op("pool", lambda e, st=st: e.memset(st["ctail"][:, :, :], 0.0), writes=[st["t_ctail"]])
            for i in range(4):
                c.op("pool", lambda e, st=st, i=i: e.memset(st["vext"][i][:, :, :], 1.0), writes=[st["t_vx"][i]])
        c.op("dve", lambda e: e.tensor_tensor(self.Esel[:, :, :], self.cb("ident")[:, 0:9].unsqueeze(2).to_broadcast([128, 9, 128]),
                                              self.cb("ident")[:, 32:41].unsqueeze(2).to_broadcast([128, 9, 128]), ALU.add), reads=[self.CT], writes=[self.CT])
        rb = self.av(0, [128, 4], F32)
        ohs = self.av(16, [128, 384], F32)
        R = self.av(16 + 1536, [128, 384, 4], F32)
        Frep = self.av(16 + 1536 + 6144, [128, 1536], F32)
        Tt = self.av(16 + 1536 + 6144 + 6144, [128, 256, 4], F32)
        tmp = self.av(16 + 1536 + 6144 + 6144 + 4096, [128, 2048], F32)
        nseq = self.av(16 + 1536 + 6144 + 6144 + 4096 + 8192, [128, 128], F32)
        c.op("dve", lambda e: e.memset(rb[0:64, :], NEGB / 8.0), writes=[A])
        c.dma("sp", rb[0:32, :], d["rel_bias"].ap(), writes=[A], key="su1")
        c.dma("sp", ohs[0:33, :], d["oh"].ap(), writes=[A], key="su2")
        c.op("dve", lambda e: e.tensor_scalar_mul(rb[0:33, :], rb[0:33, :], 8.0), reads=[A], writes=[A])
        c.op("dve", lambda e: e.tensor_tensor(R[0:33, :, :], ohs[0:33, :].unsqueeze(2).to_broadcast([33, 384, 4]),
                                              rb[0:33, :].unsqueeze(1).to_broadcast([33, 384, 4]), ALU.mult),
             reads=[A], writes=[A])
        Rf = R[0:33, :, :].rearrange("p a b -> p (a b)")
        for i in range(3):
            ps = self.psb(i)
            c.op("pe", lambda e, i=i, ps=ps: e.matmul(ps[:, :], lhsT=self.cf("onesf")[0:33, :], rhs=Rf[:, i * 512:(i + 1) * 512],
                                                      start=True, stop=True), reads=[A, self.CT], writes=[self.pst[i]])
            c.op("act", lambda e, i=i, ps=ps: e.copy(Frep[:, i * 512:(i + 1) * 512], ps[:, :]), reads=[self.pst[i]], writes=[A])
        SC = Tok()
        c.dma("sp", self.scr.ap(), Frep, reads=[A], writes=[SC], key="su3")
        c.dma("sp", Tt.rearrange("p a b -> p (a b)"), bass.AP(self.scr, 127 * 4, [[1532, 128], [1, 1024]]), reads=[SC], writes=[A], key="su4")

        def hilo(dst, src_ap, shape_desc, tmp_ap):
            c.op("dve", lambda e, dst=dst, src_ap=src_ap: e.tensor_copy(dst[0], src_ap), reads=[A], writes=[A])
            c.op("dve", lambda e, dst=dst, src_ap=src_ap, tmp_ap=tmp_ap: e.tensor_tensor(tmp_ap, src_ap, dst[0], ALU.subtract), reads=[A], writes=[A])
            c.op("dve", lambda e, dst=dst, tmp_ap=tmp_ap: e.tensor_copy(dst[1], tmp_ap), reads=[A], writes=[A])

        for blk, j0 in ((0, 128), (1, 0)):
            src = Tt[:, j0:j0 + 128, :].rearrange("k q (g r) -> k g r q", g=2)
            t4 = tmp[:, 0:512].rearrange("k (g r q) -> k g r q", g=2, r=2)
            hilo([self.biasP[0][:, :, blk, :, :], self.biasP[1][:, :, blk, :, :]], src, None, t4)
        c.op("dve", lambda e: e.tensor_scalar(nseq, self.cf("sameS"), -NEGB, NEGB, ALU.mult, ALU.add), reads=[self.CT], writes=[A])
        t4 = tmp[:, 512:1024].rearrange("k (g r q) -> k g r q", g=2, r=2)
        src = Tt[:, 0:128, :].rearrange("k q (g r) -> k g r q", g=2)
        c.op("dve", lambda e, t4=t4, src=src: e.tensor_tensor(t4, src, nseq.unsqueeze(1).unsqueeze(1).to_broadcast([128, 2, 2, 128]), ALU.add),
             reads=[A], writes=[A])
        t4b = tmp[:, 1024:1536].rearrange("k (g r q) -> k g r q", g=2, r=2)
        hilo([self.biasS[0][:, :, :, :], self.biasS[1][:, :, :, :]], t4, None, t4b)
        src = Tt[:, 128:136, :].rearrange("k i (g r) -> k g r i", g=2)
        t5 = tmp[:, 1536:1536 + 32].rearrange("k (g r i) -> k g r i", g=2, r=2)
        c.op("dve", lambda e, t5=t5, src=src: e.tensor_copy(t5, src), reads=[A], writes=[A])
        srcb = t5.unsqueeze(1).to_broadcast([128, 16, 2, 2, 8])
        t6 = tmp[:, 0:512].rearrange("k (b g r i) -> k b g r i", b=16, g=2, r=2)
        hilo([self.biasC[0][:, :, :, :, :], self.biasC[1][:, :, :, :, :]], srcb, None, t6)
        c.release([A])
        self.BT = Tok(const=True)
        self.BT.w = A.w

    def wload(self, src_ap, ncols, nk=8):
        i = self.wi
        self.wi ^= 1
        W, t = self.W[i], self.Wt[i]
        self.c.dma("pool", W[:, 0:nk, 0:ncols], src_ap.rearrange("(k p) n -> p k n", p=128), writes=[t], key="w%d" % i)
        return W, t

    def wspec(self, key, l):
        d = self.d
        return {"swa": (d["w_in"].ap()[l][:, 0:768], 768, 8), "gla": (d["w_in"].ap()[l][:, 768:1680], 912, 8),
                "ssd_x": (d["w_in"].ap()[l][:, 1680:2448], 768, 8), "ssd_z": (d["w_in"].ap()[l][:, 2448:2968], 520, 8),
                "w_out": (d["w_out"].ap()[l], 1024, 8), "w_pg": (d["w_pg"].ap()[l], 1024, 8), "w_pe": (d["w_pe"].ap()[l], 1024, 2)}[key]

    def prefetch(self, key, l):
        src, ncols, nk = self.wspec(key, l)
        self.wq[(key, l)] = self.wload(src, ncols, nk)

    def getw(self, key, l):
        if (key, l) not in self.wq:
            self.prefetch(key, l)
        return self.wq.pop((key, l))

    def load_params(self, l):
        c, d = self.c, self.d
        P = self.PT_
        ncq = dict(nc_ok=True)
        c.dma("sp", self.nwT[:, :], d["norm_w"].ap()[l].rearrange("(k p) -> p k", p=128), writes=[P], key="pa", **ncq)
        for t in range(4):
            c.dma("sp", self.cwT[:, :, t], d["conv_w"].ap()[l][t].rearrange("(k p) -> p k", p=128), writes=[P], key="pa", **ncq)
        c.dma("sp", self.cbT[:, :], d["conv_b"].ap()[l].rearrange("(k p) -> p k", p=128), writes=[P], key="pa", **ncq)

        def bc(name, n):
            t = d[name]
            return bass.AP(t, l * n, [[0, 128], [1, n]])

        c.dma("sp", self.rep8[:, :], bc("rep8", 24), writes=[P], key="pa")
        c.dma("sp", self.ssdnw[:, :], bc("ssd_norm_w", 512), writes=[P], key="pa")
        c.dma("sp", self.hdw[:, :, :].rearrange("p a b -> p (a b)"), bc("hdw", 448), writes=[P], key="pa")
        c.dma("sp", self.esink[:, :], bc("sinks", 4), writes=[P], key="pa")
        c.dma("sp", self.wgkf[:, :], d["wgk"].ap()[l], writes=[P], key="pa")
        c.op("act", lambda e: e.activation(self.esink[:, :], self.esink[:, :], AF.Exp), reads=[P], writes=[P])
        c.op("act", lambda e: e.activation(self.arep[:, :], self.rep8[:, 8:16], AF.Exp), reads=[P], writes=[P])
        c.op("dve", lambda e: e.tensor_scalar_mul(self.arep[:, :], self.arep[:, :], -1.0), reads=[P], writes=[P])
        c.op("dve", lambda e: e.tensor_copy(self.wgk[:, :], self.wgkf[:, :]), reads=[P], writes=[P])

    def unit(self, l, hf):
        c, d = self.c, self.d
        row0, NT, T, has_s = [(0, 1024, 8, False), (1024, 1152, 9, True)][hf]
        self.l, self.hf, self.NT, self.T, self.has_s, self.row0 = l, hf, NT, T, has_s, row0
        groups = [(g * 512, min(NT, (g + 1) * 512)) for g in range((NT + 511) // 512)]
        self.groups = groups
        mk = lambda nm: self.marks.append((l, hf, nm, len(c.ops["pe"]), len(c.ops["dve"]), len(c.ops["act"])))
        self.mk = mk
        mk("start")
        if ("swa", l) not in self.wq:
            self.prefetch("swa", l)
        self.load_params(l)
        if l == 0:
            self.load_x()
        mk("rmsnorm")
        self.rmsnorm()
        self.ytok = self.av(0, [128, 9, 512], BF16)
        self.t_ytok = c.new_tok()
        MOFF = 9216
        if not (ENABLE["swa"] and ENABLE["gla"]):
            c.op("pool", lambda e: e.memset(self.ytok[:, :, :], 0.0), writes=[self.t_ytok])
        if ENABLE["swa"]:
            mk("swa")
            self.swa(MOFF)
            if self.trunc:
                return
        if ENABLE["gla"]:
            mk("gla")
            self.gla(MOFF)
        if ENABLE["ssd"]:
            mk("ssd")
            self.ssd(MOFF)
        else:
            c.op("pool", lambda e: e.memset(self.uT[:, 0:4, 0:NT], 0.0), writes=self.t_uTg)
        mk("ytr")
        if ENABLE["ssd"]:
            c.release([self.t_ytok])
        else:
            self.y_transposes()
        mk("out_proj")
        self.out_proj(MOFF)
        mk("ple")
        self.ple(MOFF)
        if l == 1:
            mk("store_y")
            self.store_y(MOFF)
        mk("end")

    def load_x(self):
        c, d = self.c, self.d
        self.t_hT = [[c.new_tok() for _ in range(3)] for _ in range(8)]
        xs = [self.av(i * 4096, [128, 1024], F32) for i in range(2)]
        xt = [c.new_tok() for _ in range(2)]
        for j in range(self.T):
            s = j % 2
            g = j // 4
            c.dma("sp", xs[s], d["x"].ap()[self.row0 + j * 128:self.row0 + (j + 1) * 128, :], writes=[xt[s]], key="xs%d" % s)
            for half in range(2):
                bank = half
                ps = self.psb(bank)
                for kk in range(4):
                    k = half * 4 + kk
                    c.op("pe", lambda e, ps=ps, kk=kk, k=k, s=s: e.transpose(ps[:, kk * 128:(kk + 1) * 128], xs[s][:, k * 128:(k + 1) * 128],
                                                                              self.cf("identf")), reads=[xt[s], self.CT], writes=[self.pst[bank]])
                dst = self.hT[:, half * 4:(half + 1) * 4, j * 128:(j + 1) * 128]
                src = ps[:, :].rearrange("p (a b) -> p a b", a=4)
                eng = "act" if half == 0 else "dve"
                if eng == "act":
                    c.op("act", lambda e, dst=dst, src=src: e.copy(dst, src), reads=[self.pst[bank]], writes=[self.t_hT[k][g] for k in range(half * 4, half * 4 + 4)])
                else:
                    c.op("dve", lambda e, dst=dst, src=src: e.tensor_copy(dst, src), reads=[self.pst[bank]], writes=[self.t_hT[k][g] for k in range(half * 4, half * 4 + 4)])
        c.release(xt)

    def rmsnorm(self):
        c = self.c
        if hasattr(self, "t_uTg"):
            c.release(self.t_uTg)
        self.t_uTg = [c.new_tok() for _ in self.groups]
        sq = [self.av(i * 8192, [128, 8, 512], BF16) for i in range(2)]
        rs = [self.av(16384 + i * 2048, [128, 512], F32) for i in range(2)]
        tq = [c.new_tok() for _ in range(2)]
        tr = [c.new_tok() for _ in range(2)]
        def emit_sq(gi):
            c0, c1 = self.groups[gi]
            n = c1 - c0
            s = gi % 2
            c.op("act", lambda e: e.activation(sq[s][:, :, 0:n], self.hT[:, :, c0:c1], AF.Square),
                 reads=[self.t_hT[k][gi] for k in range(8)], writes=[tq[s]])

        emit_sq(0)
        for gi, (c0, c1) in enumerate(self.groups):
            n = c1 - c0
            s = gi % 2
            ps = self.psb(2)
            for k in range(8):
                c.op("pe", lambda e, k=k, s=s, n=n, ps=ps: e.matmul(ps[:, 0:n], lhsT=self.cb("ones"), rhs=sq[s][:, k, 0:n], start=(k == 0), stop=(k == 7)),
                     reads=[tq[s], self.CT], writes=[self.pst[2]])
            if gi + 1 < len(self.groups):
                emit_sq(gi + 1)
            c.op("act", lambda e, s=s, n=n, ps=ps: e.activation(rs[s][:, 0:n], ps[:, 0:n], AF.Ln, bias=EPS, scale=1.0 / 1024), reads=[self.pst[2]], writes=[tr[s]])
            c.op("act", lambda e, s=s, n=n: e.activation(rs[s][:, 0:n], rs[s][:, 0:n], AF.Exp, scale=-0.5), reads=[tr[s]], writes=[tr[s]])
            for k in range(8):
                c.op("dve", lambda e, k=k, s=s, n=n, c0=c0, c1=c1: e.scalar_tensor_tensor(self.uT[:, k, c0:c1], self.hT[:, k, c0:c1], self.nwT[:, k:k + 1],
                                                                                         rs[s][:, 0:n], ALU.mult, ALU.mult),
                     reads=[self.t_hT[k][gi], tr[s], self.PT_], writes=[self.t_uTg[gi]])
        c.release(tq + tr)

    def proj_fm(self, W, wt, j0, M, xin, xin_toks, evac, nk=8, banks=(0, 1)):
        c = self.c
        for gi, (c0, c1) in enumerate(self.groups):
            n = c1 - c0
            bank = banks[self._rr % len(banks)]
            self._rr += 1
            ps = self.psb(bank)
            for k in range(nk):
                c.op("pe", lambda e, k=k, ps=ps, n=n, c0=c0, c1=c1: e.matmul(ps[0:M, 0:n], lhsT=W[:, k, j0:j0 + M], rhs=xin[:, k, c0:c1],
                                                                             start=(k == 0), stop=(k == nk - 1)),
                     reads=[wt] + xin_toks(gi), writes=[self.pst[bank]])
            evac(ps[0:M, 0:n], gi, c0, c1, self.pst[bank])

    def proj_tm(self, W, wt, j0, N, evac, tiles=None, banks=(0, 1)):
        c = self.c
        for j in (tiles if tiles is not None else range(self.T)):
            bank = banks[self._rr % len(banks)]
            self._rr += 1
            ps = self.psb(bank)
            for k in range(8):
                c.op("pe", lambda e, k=k, ps=ps, j=j: e.matmul(ps[:, 0:N], lhsT=self.uT[:, k, j * 128:(j + 1) * 128], rhs=W[:, k, j0:j0 + N],
                                                               start=(k == 0), stop=(k == 7)),
                     reads=[wt, self.t_uTg[j // 4]], writes=[self.pst[bank]])
            evac(ps[:, 0:N], j, self.pst[bank])

    _rr = 0

    def pipeline_fine(self, gens):
        import os
        mode = os.environ.get("PMODE", "fine")
        prev = None
        for g in list(gens) + [None]:
            a_done = g is None
            b_done = prev is None
            if mode == "newest":
                kmax = int(os.environ.get("KMAX", "99"))
                kk_ = 0
                while not a_done:
                    try:
                        if next(g) == "S":
                            a_done = True
                    except StopIteration:
                        a_done = True
                        g = None
                    kk_ += 1
                    if prev is not None and kk_ >= kmax:
                        a_done = True
                        g = None
            while not (a_done and b_done):
                if not a_done:
                    try:
                        if next(g) == "S":
                            a_done = True
                    except StopIteration:
                        a_done = True
                        g = None
                if not b_done:
                    try:
                        next(prev)
                    except StopIteration:
                        b_done = True
            prev = g

    def keepwarm(self, bank, n):
        c = self.c
        for _w in range(n):
            c.op("pe", lambda e: e.matmul(self.psb(bank)[:, :], lhsT=self.cb("ident"), rhs=self.biasP[0][:, 0, :, :, :].rearrange("p b r q -> p (b r q)"), start=True, stop=True),
                 reads=[self.BT, self.CT], writes=[self.pst[bank]])

    def pipeline_sched(self, gens, sched):
        gens = list(gens)
        n = len(gens)
        maxlag = max(l for _, l in sched)
        pos = [0] * n
        for r in range(n + maxlag):
            for seg, lag in sched:
                t = r - lag
                if 0 <= t < n:
                    assert pos[t] == seg, (t, pos[t], seg)
                    try:
                        next(gens[t])
                    except StopIteration:
                        pass
                    pos[t] += 1

    def pipeline_rr(self, gens, nstage):
        gens = list(gens)
        n = len(gens)
        done = [False] * n
        for r in range(n + nstage - 1):
            act = [t for t in range(r - nstage + 1, r + 1) if 0 <= t < n and not done[t]]
            atb = {t: False for t in act}
            while not all(atb.values()):
                for t in act:
                    if atb[t]:
                        continue
                    try:
                        if next(gens[t]) == "S":
                            atb[t] = True
                    except StopIteration:
                        atb[t] = True
                        done[t] = True

    def pipeline(self, gens, on=True):
        if not on:
            for g in gens:
                for _ in g:
                    pass
            return
        active = []
        it = iter(gens)
        while True:
            nxt = next(it, None)
            if nxt is not None:
                active.append(nxt)
            if not active:
                break
            for g in list(active):
                try:
                    next(g)
                except StopIteration:
                    active.remove(g)

    def swa(self, MOFF):
        c, d, l, T, NT = self.c, self.d, self.l, self.T, self.NT
        st = self.S[l]
        o = MOFF
        qkv = self.av(o, [128, 9, 512], F32); o += 18432
        ssg = self.av(o, [128, 9, 256], BF16); o += 4608
        tmp = self.av(o, [128, 3, 6, 64], F32); o += 4608
        rst = self.av(o, [128, 3, 6], F32); o += 128
        qkn = self.av(o, [128, 9, 6, 64], BF16); o += 6912
        kn32 = self.av(o, [128, 2, 128], F32); o += 1024
        QT = self.av(o, [128, 2, 2, 128], BF16); o += 1024
        PT = self.av(o, [128, 2, 2, 2, 256], BF16); o += 4096
        ytmp = self.av(o, [128, 2, 4, 64], F32); o += 2048
        den = self.av(o, [128, 2, 8], F32); o += 64
        Kc = self.av(o, [128, 16, 128], BF16); o += 4096
        Vc = self.av(o, [128, 16, 2, 65], BF16); o += 4160
        KcTz = self.av(o, [128, 16, 2, 128], BF16); o += 8192
        PTc = self.av(o, [128, 16, 2, 16], BF16); o += 1024
        oTs = self.av(o, [128, 4, 128], F32); o += 2048
        t_qkv = [c.new_tok() for _ in range(T)]
        t_ssg = [c.new_tok() for _ in range(T)]
        t_tmp, t_rst, t_kn32 = c.new_tok(), c.new_tok(), c.new_tok()
        t_qkn = [c.new_tok() for _ in range(T)]
        t_QT = [c.new_tok(), c.new_tok()]
        t_PT = [c.new_tok(), c.new_tok()]
        t_ytmp = [c.new_tok(), c.new_tok()]
        t_den = [c.new_tok(), c.new_tok()]
        t_s = c.new_tok()
        alltoks = t_qkv + t_ssg + [t_tmp, t_rst, t_kn32] + t_qkn + t_QT + t_PT + t_ytmp + t_den + [t_s]
        W, wt = self.getw("swa", l)

        def ev_qkv(ps, j, bt):
            c.op("act", lambda e: e.copy(qkv[:, j, :], ps), reads=[bt], writes=[t_qkv[j]])

        def ev_sg(ps, j, bt):
            c.op("act", lambda e: e.activation(ssg[:, j, :], ps, AF.Silu), reads=[bt], writes=[t_ssg[j]])

        self.proj_tm(W, wt, 0, 512, ev_qkv, tiles=list(range(0, min(3, T))))
        if self.has_s:
            c.dma("pool", Kc, d["ck"].ap()[l].rearrange("b k c -> k b c"), writes=[t_s], key="swc")
            c.op("dve", lambda e: e.memset(Vc[:, :, :, 64:65], 1.0), writes=[t_s])
            for g in range(2):
                c.dma("pool", Vc[:, :, g, 0:64], d["cv"].ap()[l][:, :, 64 * g:64 * g + 64].rearrange("b k c -> k b c"), writes=[t_s], key="swc")
        for j0 in range(0, T, 3):
            nj = min(3, T - j0)
            if j0 + 3 < T:
                self.proj_tm(W, wt, 0, 512, ev_qkv, tiles=list(range(j0 + 3, min(j0 + 6, T))))
            else:
                self.proj_tm(W, wt, 512, 256, ev_sg)
            qk = qkv[:, j0:j0 + nj, 0:384].rearrange("p j (h d) -> p j h d", d=64)
            tm = tmp[:, 0:nj, :, :]
            rs = rst[:, 0:nj, :]
            rd = [t_qkv[j] for j in range(j0, j0 + nj)]
            c.op("dve", lambda e, tm=tm, qk=qk: e.tensor_tensor(tm, qk, qk, ALU.mult), reads=rd, writes=[t_tmp])
            c.op("dve", lambda e, tm=tm, rs=rs: e.tensor_reduce(rs, tm, AX.X, ALU.add), reads=[t_tmp], writes=[t_rst])
            c.op("act", lambda e, rs=rs: e.activation(rs, rs, AF.Ln, bias=EPS, scale=1.0 / 64), reads=[t_rst], writes=[t_rst])
            c.op("act", lambda e, rs=rs: e.activation(rs, rs, AF.Exp, scale=-0.5), reads=[t_rst], writes=[t_rst])
            c.op("dve", lambda e, tm=tm, qk=qk, rs=rs, nj=nj: e.tensor_tensor(tm, qk, rs.unsqueeze(3).to_broadcast([128, nj, 6, 64]), ALU.mult),
                 reads=rd + [t_rst], writes=[t_tmp])
            c.op("dve", lambda e, tm=tm, nj=nj, j0=j0: e.tensor_tensor(qkn[:, j0:j0 + nj, :, :], tm, self.hdw[:, 0:6, :].unsqueeze(1).to_broadcast([128, nj, 6, 64]), ALU.mult),
                 reads=[t_tmp, self.PT_], writes=[t_qkn[j] for j in range(j0, j0 + nj)])
            for j in range(j0, j0 + nj):
                slot = None
                if self.hf == 1 and j == 7:
                    slot = 0
                if self.has_s and j == 8:
                    slot = 1
                if slot is not None:
                    c.op("dve", lambda e, j=j, slot=slot, j0=j0: e.tensor_tensor(kn32[:, slot, :].rearrange("p (h d) -> p h d", d=64), tmp[:, j - j0, 4:6, :],
                                                                                   self.hdw[:, 4:6, :], ALU.mult), reads=[t_tmp, self.PT_], writes=[t_kn32])
        self.prefetch("gla", l)
        def body(j):
            is_s = self.has_s and j == 8
            gt = self.hf * 8 + j
            par = gt % 3
            par4 = gt % 4
            sl = j % 2
            first = (gt == 0) or is_s
            tp = self.psb(2, BF16)[:, 0:384].rearrange("p (a b) -> p a b", a=3)
            for a in range(3):
                src = qkn[:, j, 2 * a:2 * a + 2, :].rearrange("p h d -> p (h d)")
                c.op("pe", lambda e, a=a, src=src, tp=tp: e.transpose(tp[:, a, :], src, self.cb("ident")), reads=[t_qkn[j], self.CT], writes=[self.pst[2]])
            yield
            import os
            SKIP = os.environ.get("SKIP", "") if j >= 1 else ""
            if "qt" not in SKIP:
                c.op("act", lambda e, sl=sl, tp=tp: e.copy(QT[:, sl, :, :], tp[:, 0:2, :]), reads=[self.pst[2]], writes=[t_QT[sl]])
            KTz, vext, tkv, tvx = st["KTz"][par], st["vext"][par4], st["t_kv"][par], st["t_vx"][par4]
            if "ktz" not in SKIP:
              c.op("dve", lambda e, tp=tp, KTz=KTz: e.tensor_tensor(KTz[:, :, :], tp[:, 2:3, :].to_broadcast([128, 2, 128]),
                                                                  self.cb("bd2").unsqueeze(2).to_broadcast([128, 2, 128]), ALU.mult),
                 reads=[self.pst[2], self.CT], writes=[tkv])
            if "vext" not in SKIP:
              c.op("pool", lambda e, vext=vext, j=j: e.tensor_copy(vext[:, :, 0:64], qkv[:, j, 384:512].rearrange("p (g d) -> p g d", g=2)),
                 reads=[t_qkv[j]], writes=[tvx])
            yield "S"
            if not is_s:
                for _w in range(2):
                    c.op("pe", lambda e: e.matmul(self.psb(0)[:, :], lhsT=self.cb("ident"), rhs=self.biasP[0][:, 0, :, :, :].rearrange("p b r q -> p (b r q)"), start=True, stop=True),
                         reads=[self.BT, self.CT], writes=[self.pst[0]])
            blks = [1] if first else [0, 1]
            for g in range(2):
                yield
                bank = 4 + g
                sc = self.psb(bank).rearrange("p (b n) -> p b n", b=2)
                b0 = blks[0]
                scf = sc[:, b0:2, :].rearrange("p b n -> p (b n)")
                if is_s:
                    bh = self.biasS[0][:, g, :, :].rearrange("p r q -> p (r q)")
                    bl = self.biasS[1][:, g, :, :].rearrange("p r q -> p (r q)")
                else:
                    bh = self.biasP[0][:, g, b0:2, :, :].rearrange("p b r q -> p (b r q)")
                    bl = self.biasP[1][:, g, b0:2, :, :].rearrange("p b r q -> p (b r q)")
                c.op("pe", lambda e, scf=scf, bh=bh: e.matmul(scf, lhsT=self.cb("ident"), rhs=bh, start=True, stop=False), reads=[self.BT, self.CT], writes=[self.pst[bank]])
                c.op("pe", lambda e, scf=scf, bl=bl: e.matmul(scf, lhsT=self.cb("ident"), rhs=bl, start=False, stop=False), reads=[self.BT, self.CT], writes=[self.pst[bank]])
                for blk in blks:
                    kpar = par if blk == 1 else (par + 2) % 3
                    KT_ = st["KTz"][kpar]
                    tk_ = st["t_kv"][kpar]
                    lastb = (blk == blks[-1])
                    c.op("pe", lambda e, sc=sc, blk=blk, KT_=KT_, g=g, sl=sl, lastb=lastb: e.matmul(sc[:, blk, :], lhsT=KT_[:, g, :], rhs=QT[:, sl, :, :].rearrange("p r q -> p (r q)"),
                                                                                       start=False, stop=lastb), reads=[tk_, t_QT[sl]], writes=[self.pst[bank]])
                b0 = blks[0]
                c.op("act", lambda e, sc=sc, b0=b0, sl=sl, g=g: e.activation(PT[:, sl, g, b0:2, :], sc[:, b0:2, :], AF.Exp, scale=0.125),
                     reads=[self.pst[bank]], writes=[t_PT[sl]])
            yield "S"
            if not is_s:
                for _w in range(2):
                    c.op("pe", lambda e: e.matmul(self.psb(0)[:, :], lhsT=self.cb("ident"), rhs=self.biasP[0][:, 1, :, :, :].rearrange("p b r q -> p (b r q)"), start=True, stop=True),
                         reads=[self.BT, self.CT], writes=[self.pst[0]])
                oe = self.psb(6).rearrange("p (h n) -> p h n", h=4)
                for h in range(4):
                    yield
                    g, r = h // 2, h % 2
                    for blk in blks:
                        kpar = par4 if blk == 1 else (par4 + 3) % 4
                        f0, f1 = (blk == blks[0]), (blk == blks[-1])
                        c.op("pe", lambda e, oe=oe, h=h, g=g, r=r, blk=blk, kpar=kpar, sl=sl, f0=f0, f1=f1: e.matmul(oe[:, h, 0:65], lhsT=PT[:, sl, g, blk, r * 128:(r + 1) * 128],
                                                                                                     rhs=st["vext"][kpar][:, g, :], start=f0, stop=f1),
                             reads=[t_PT[sl], st["t_vx"][kpar]], writes=[self.pst[6]])
                self.swa_finish(oe, j, sl, ssg, t_ssg, ytmp, t_ytmp, den, t_den)
            else:
                self.swa_sample(j, sl, QT, t_QT, PT, t_PT, Kc, Vc, KcTz, PTc, oTs, t_s, vext, tvx, ssg, t_ssg, ytmp, t_ytmp, den, t_den)
            if self.hf == 1 and j == 7:
                c.dma("sp", self.o["swak_p"].ap()[l], kn32[:, 0, :], reads=[t_kn32], key="ok")
                c.dma("sp", self.o["swav_p"].ap()[l], qkv[:, 7, 384:512], reads=[t_qkv[7]], key="ok")
            if is_s:
                c.dma("sp", self.o["swak_s"].ap()[l][:, 0:120, :], d["ck"].ap()[l][:, 8:128, :], key="ok")
                c.dma("sp", self.o["swav_s"].ap()[l][:, 0:120, :], d["cv"].ap()[l][:, 8:128, :], key="ok")
                for b in range(16):
                    c.dma("sp", self.o["swak_s"].ap()[l][b, 120:128, :], kn32[8 * b:8 * b + 8, 1, :], reads=[t_kn32], key="ok")
                    c.dma("sp", self.o["swav_s"].ap()[l][b, 120:128, :], qkv[8 * b:8 * b + 8, 8, 384:512], reads=[t_qkv[8]], key="ok")
        self.pipeline_rr([body(j) for j in range(T if not self.trunc else int(self.trunc))], 3)
        c.release(alltoks)

    def swa_finish(self, oe, j, sl, ssg, t_ssg, ytmp, t_ytmp, den, t_den):
        c = self.c
        c.op("dve", lambda e: e.tensor_tensor(den[:, sl, 0:4], oe[:, :, 64], self.esink[:, :], ALU.add), reads=[self.pst[6], self.PT_], writes=[t_den[sl]])
        c.op("dve", lambda e: e.reciprocal(den[:, sl, 4:8], den[:, sl, 0:4]), reads=[t_den[sl]], writes=[t_den[sl]])
        c.op("dve", lambda e: e.tensor_tensor(ytmp[:, sl, :, :], oe[:, :, 0:64], den[:, sl, 4:8].unsqueeze(2).to_broadcast([128, 4, 64]), ALU.mult),
             reads=[self.pst[6], t_den[sl]], writes=[t_ytmp[sl]])
        c.op("dve", lambda e: e.tensor_tensor(self.ytok[:, j, 256:512], ytmp[:, sl, :, :].rearrange("p h d -> p (h d)"), ssg[:, j, :], ALU.mult),
             reads=[t_ytmp[sl], t_ssg[j]], writes=[self.t_ytok])

    def swa_sample(self, j, sl, QT, t_QT, PT, t_PT, Kc, Vc, KcTz, PTc, oTs, t_s, vext, tkv, ssg, t_ssg, ytmp, t_ytmp, den, t_den):
        c = self.c
        for b4 in range(4):
            tp = self.psb(3, BF16)[:, 0:512].rearrange("p (a b) -> p a b", a=4)
            for bb in range(4):
                b = b4 * 4 + bb
                c.op("pe", lambda e, tp=tp, bb=bb, b=b: e.transpose(tp[:, bb, :], Kc[:, b, :], self.cb("ident")), reads=[t_s, self.CT], writes=[self.pst[3]])
            c.op("dve", lambda e, tp=tp, b4=b4: e.tensor_tensor(KcTz[:, b4 * 4:b4 * 4 + 4, :, :], tp.unsqueeze(2).to_broadcast([128, 4, 2, 128]),
                                                                self.cb("bd2").unsqueeze(1).unsqueeze(3).to_broadcast([128, 4, 2, 128]), ALU.mult),
                 reads=[self.pst[3], self.CT], writes=[t_s])
        scc = self.psb(7).rearrange("p (b g n) -> p b g n", b=16, g=2)
        sccf = self.psb(7)
        c.op("pe", lambda e: e.matmul(sccf[:, :], lhsT=self.cb("ident"), rhs=self.biasC[0][:, :, :, :, :].rearrange("p b g r i -> p (b g r i)"), start=True, stop=False),
             reads=[self.BT, self.CT], writes=[self.pst[7]])
        c.op("pe", lambda e: e.matmul(sccf[:, :], lhsT=self.cb("ident"), rhs=self.biasC[1][:, :, :, :, :].rearrange("p b g r i -> p (b g r i)"), start=False, stop=False),
             reads=[self.BT, self.CT], writes=[self.pst[7]])
        for b in range(16):
            for g in range(2):
                c.op("pe", lambda e, b=b, g=g: e.matmul(scc[:, b, g, :], lhsT=KcTz[:, b, g, :], rhs=QT[:, sl, :, 8 * b:8 * b + 8], start=False, stop=(b == 15 and g == 1)),
                     reads=[t_s, t_QT[sl]], writes=[self.pst[7]])
        c.op("act", lambda e: e.activation(PTc[:, :, :, :].rearrange("p b g n -> p (b g n)"), sccf[:, :], AF.Exp, scale=0.125), reads=[self.pst[7]], writes=[t_s])
        oT = self.psb(6).rearrange("p (h n) -> p h n", h=4)
        for h in range(4):
            g, r = h // 2, h % 2
            c.op("pe", lambda e, h=h, g=g, r=r: e.matmul(oT[0:65, h, :], lhsT=vext[:, g, :], rhs=PT[:, sl, g, 1, r * 128:(r + 1) * 128], start=True, stop=False),
                 reads=[tkv, t_PT[sl]], writes=[self.pst[6]])
            for b in range(16):
                c.op("pe", lambda e, h=h, g=g, r=r, b=b: e.matmul(oT[0:65, h, 8 * b:8 * b + 8], lhsT=Vc[:, b, g, :], rhs=PTc[:, b, g, r * 8:(r + 1) * 8], start=False, stop=True),
                     reads=[t_s], writes=[self.pst[6]])
        c.op("act", lambda e: e.copy(oTs[0:65, :, :], oT[0:65, :, :]), reads=[self.pst[6]], writes=[t_s])
        oe = self.psb(5).rearrange("p (h n) -> p h n", h=4)
        for h in range(4):
            c.op("pe", lambda e, h=h: e.transpose(oe[:, h, 0:65], oTs[0:65, h, :], self.cf("identf")[0:65, 0:65]), reads=[t_s, self.CT], writes=[self.pst[5]])
        c.op("dve", lambda e: e.tensor_tensor(den[:, sl, 0:4], oe[:, :, 64], self.esink[:, :], ALU.add), reads=[self.pst[5], self.PT_], writes=[t_den[sl]])
        c.op("dve", lambda e: e.reciprocal(den[:, sl, 4:8], den[:, sl, 0:4]), reads=[t_den[sl]], writes=[t_den[sl]])
        c.op("dve", lambda e: e.tensor_tensor(ytmp[:, sl, :, :], oe[:, :, 0:64], den[:, sl, 4:8].unsqueeze(2).to_broadcast([128, 4, 64]), ALU.mult),
             reads=[self.pst[5], t_den[sl]], writes=[t_ytmp[sl]])
        c.op("dve", lambda e: e.tensor_tensor(self.ytok[:, j, 256:512], ytmp[:, sl, :, :].rearrange("p h d -> p (h d)"), ssg[:, j, :], ALU.mult),
             reads=[t_ytmp[sl], t_ssg[j]], writes=[self.t_ytok])

    def gla(self, MOFF):
        c, d, l, T, NT = self.c, self.d, self.l, self.T, self.NT
        st = self.S[l]
        o = MOFF
        gqT = self.av(o, [128, 1152], BF16); o += 2304
        gkT = self.av(o, [128, 1152], BF16); o += 2304
        glrT = self.av(o, [128, 1152], BF16); o += 2304
        gtok = self.av(o, [128, 9, 384], BF16); o += 6912
        sgg = self.av(o, [128, 9, 256], BF16); o += 4608
        sp = self.av(o, [128, 2, 128], F32); o += 1024
        ebT = self.av(o, [128, 2, 2, 128], F32); o += 2048
        erc = self.av(o, [128, 2, 128], F32); o += 1024
        qeT = self.av(o, [128, 2, 128], BF16); o += 512
        keT = self.av(o, [128, 2, 128], BF16); o += 512
        kd = self.av(o, [128, 2, 128], BF16); o += 512
        qebd = self.av(o, [128, 2, 4, 128], BF16); o += 2048
        attm = self.av(o, [128, 2, 4, 128], BF16); o += 2048
        oss = self.av(o, [128, 2, 8], F32); o += 64
        otmp = self.av(o, [128, 2, 4, 64], F32); o += 2048
        um = self.av(o, [128, 4, 64], F32); o += 1024
        qeTm = self.av(o, [128, 16, 128], BF16); o += 4096
        kdm = self.av(o, [128, 16, 128], BF16); o += 4096
        Sg0 = self.av(o, [128, 16, 256], F32); o += 16384
        Sg0b = self.av(o, [128, 16, 256], BF16); o += 8192
        t_gq = [c.new_tok() for _ in self.groups]
        t_gk = [c.new_tok() for _ in self.groups]
        t_glr = [c.new_tok() for _ in self.groups]
        t_gtok = [c.new_tok() for _ in range(T)]
        t_sgg = c.new_tok()
        t_sp = [c.new_tok(), c.new_tok()]
        t_eb = [c.new_tok(), c.new_tok()]
        t_q = [c.new_tok(), c.new_tok()]
        t_att = [c.new_tok(), c.new_tok()]
        t_o = [c.new_tok(), c.new_tok()]
        t_um = c.new_tok()
        t_s = c.new_tok()
        alltoks = t_gq + t_gk + t_glr + t_gtok + [t_sgg] + t_sp + t_eb + t_q + t_att + t_o + [t_um, t_s]
        W, wt = self.getw("gla", l)
        c.op("pool", lambda e: e.memset(glrT[0:32, :], 1.0), writes=t_glr)

        def ev_fm(dst, toks):
            def f(ps, gi, c0, c1, bt):
                M = ps.shape[0]
                c.op("act", lambda e: e.copy(dst[0:M, c0:c1], ps), reads=[bt], writes=[toks[gi]])
            return f

        def xt(gi):
            return [self.t_uTg[gi]]

        self.proj_fm(W, wt, 0, 128, self.uT, xt, ev_fm(gqT, t_gq))
        self.proj_fm(W, wt, 128, 128, self.uT, xt, ev_fm(gkT, t_gk))
        self.proj_fm(W, wt, 256, 16, self.uT, xt, ev_fm(glrT, t_glr))

        def ev_kv(ps, j, bt):
            c.op("dve", lambda e: e.tensor_copy(gtok[:, j, :], ps), reads=[bt], writes=[t_gtok[j]])

        def ev_gg(ps, j, bt):
            c.op("act", lambda e: e.activation(sgg[:, j, :], ps, AF.Silu), reads=[bt], writes=[t_sgg])

        self.proj_tm(W, wt, 272, 384, ev_kv)
        self.proj_tm(W, wt, 656, 256, ev_gg)
        c.op("dve", lambda e: e.tensor_tensor(sgg[:, 0:T, :].rearrange("p j (h d) -> p j h d", d=64), sgg[:, 0:T, :].rearrange("p j (h d) -> p j h d", d=64),
                                              self.hdw[:, 6:7, :].unsqueeze(1).to_broadcast([128, T, 4, 64]), ALU.mult), reads=[t_sgg, self.PT_], writes=[t_sgg])
        if self.has_s:
            c.op("pool", lambda e: e.memset(Sg0[:, :, :], 0.0), writes=[t_s])
            for h in range(4):
                c.dma("sp", Sg0[32 * h:32 * h + 32, :, 64 * h:64 * h + 64], d["st_gla"].ap()[l][:, h, :, :].rearrange("b d v -> d b v"), writes=[t_s], key="gls")
            c.op("act", lambda e: e.copy(Sg0b[:, :, :], Sg0[:, :, :]), reads=[t_s], writes=[t_s])
        self.prefetch("ssd_x", l)
        self.prefetch("ssd_z", l)
        def body(j):
            is_s = self.has_s and j == 8
            sl = j % 2
            gi = j // 4
            cs = slice(j * 128, (j + 1) * 128)
            tri = self.cf("triS") if is_s else self.cf("triP")
            sup = self.cf("supS") if is_s else self.cf("supP")
            m01 = self.cb("triS") if is_s else self.cb("triP")
            ps2 = self.psb(2)
            c.op("pe", lambda e, cs=cs: e.matmul(ps2[:, 0:128], lhsT=glrT[0:17, cs], rhs=self.wgk[0:17, :], start=True, stop=True),
                 reads=[t_glr[gi], self.PT_], writes=[self.pst[2]])
            yield
            c.op("act", lambda e, sl=sl: e.activation(sp[:, sl, :], ps2[:, 0:128], AF.Exp, scale=-1.0), reads=[self.pst[2]], writes=[t_sp[sl]])
            yield
            c.op("act", lambda e, sl=sl: e.activation(sp[:, sl, :], sp[:, sl, :], AF.Ln, bias=1.0), reads=[t_sp[sl]], writes=[t_sp[sl]])
            yield
            ps3 = self.psb(3)
            c.op("pe", lambda e, sl=sl, sup=sup: e.matmul(ps3[:, 128:256], lhsT=sup, rhs=sp[:, sl, :], start=True, stop=True), reads=[t_sp[sl], self.CT], writes=[self.pst[3]])
            yield
            c.op("pe", lambda e, sl=sl, tri=tri: e.matmul(ps3[:, 256:384], lhsT=sp[:, sl, :], rhs=tri, start=True, stop=True), reads=[t_sp[sl], self.CT], writes=[self.pst[3]])
            yield
            c.op("act", lambda e, sl=sl: e.activation(ebT[:, sl, 0, :], ps3[:, 256:384], AF.Exp, scale=-1.0 / 16), reads=[self.pst[3]], writes=[t_eb[sl]])
            yield
            c.op("act", lambda e, sl=sl: e.activation(ebT[:, sl, 1, :], ps3[:, 256:384], AF.Exp, scale=1.0 / 16), reads=[self.pst[3]], writes=[t_eb[sl]])
            yield
            c.op("act", lambda e, sl=sl: e.activation(erc[:, sl, :], ps3[:, 128:256], AF.Exp, scale=-1.0 / 16), reads=[self.pst[3]], writes=[t_eb[sl]])
            yield
            c.op("dve", lambda e, sl=sl, cs=cs: e.scalar_tensor_tensor(qeT[:, sl, :], gqT[:, cs], 32.0 ** -0.5, ebT[:, sl, 0, :], ALU.mult, ALU.mult),
                 reads=[t_gq[gi], t_eb[sl]], writes=[t_q[sl]])
            yield
            c.op("dve", lambda e, sl=sl, cs=cs: e.tensor_tensor(keT[:, sl, :], gkT[:, cs], ebT[:, sl, 1, :], ALU.mult), reads=[t_gk[gi], t_eb[sl]], writes=[t_q[sl]])
            yield
            c.op("dve", lambda e, sl=sl, j=j: e.tensor_tensor(kd[:, sl, :], gtok[:, j, 0:128], erc[:, sl, :], ALU.mult), reads=[t_gtok[j], t_eb[sl]], writes=[t_q[sl]])
            yield
            c.op("dve", lambda e, sl=sl: e.tensor_tensor(qebd[:, sl, :, :], qeT[:, sl, :].unsqueeze(1).to_broadcast([128, 4, 128]),
                                                         self.cb("bd4").unsqueeze(2).to_broadcast([128, 4, 128]), ALU.mult), reads=[t_q[sl], self.CT], writes=[t_q[sl]])
            yield
            yield "S"
            if not is_s:
                self.keepwarm(1, 3)
            ps4 = self.psb(4)
            c.op("pe", lambda e, sl=sl: e.matmul(ps4[:, :], lhsT=keT[:, sl, :], rhs=qebd[:, sl, :, :].rearrange("p h t -> p (h t)"), start=True, stop=True),
                 reads=[t_q[sl]], writes=[self.pst[4]])
            yield
            c.op("dve", lambda e, sl=sl, m01=m01: e.tensor_tensor(attm[:, sl, :, :], ps4.rearrange("p (h t) -> p h t", h=4), m01.unsqueeze(1).to_broadcast([128, 4, 128]), ALU.mult),
                 reads=[self.pst[4], self.CT], writes=[t_att[sl]])
            yield
            ob = 5 if (j % 2 == 0) else 7
            ps5 = self.psb(ob)
            if not is_s:
                c.op("pe", lambda e, sl=sl: e.matmul(ps5[:, 0:256], lhsT=qeT[:, sl, :], rhs=st["gla_b"][:, :], start=True, stop=False),
                     reads=[t_q[sl], st["t_gla"]], writes=[self.pst[ob]])
            else:
                c.op("dve", lambda e, sl=sl: e.tensor_tensor(qeTm[:, :, :], qeT[:, sl, :].unsqueeze(1).to_broadcast([128, 16, 128]),
                                                             self.cb("smbt").rearrange("p (b t) -> p b t", b=16), ALU.mult), reads=[t_q[sl], self.CT], writes=[t_s])
                for b in range(16):
                    c.op("pe", lambda e, b=b: e.matmul(ps5[:, 0:256], lhsT=qeTm[:, b, :], rhs=Sg0b[:, b, :], start=(b == 0), stop=False), reads=[t_s], writes=[self.pst[ob]])
            for h in range(4):
                c.op("pe", lambda e, sl=sl, h=h, j=j: e.matmul(ps5[:, h * 64:(h + 1) * 64], lhsT=attm[:, sl, h, :], rhs=gtok[:, j, 128 + h * 64:128 + (h + 1) * 64],
                                                               start=False, stop=(h == 3)), reads=[t_att[sl], t_gtok[j]], writes=[self.pst[ob]])
            if not is_s:
                ps6 = self.psb(6)
                c.op("pe", lambda e, sl=sl, j=j: e.matmul(ps6[:, 0:256], lhsT=kd[:, sl, :], rhs=gtok[:, j, 128:384], start=True, stop=True),
                     reads=[t_q[sl], t_gtok[j]], writes=[self.pst[6]])
                c.op("dve", lambda e: e.tensor_tensor(um[:, :, :], ps6[:, 0:256].rearrange("p (h d) -> p h d", h=4), self.cf("bd4").unsqueeze(2).to_broadcast([128, 4, 64]), ALU.mult),
                     reads=[self.pst[6], self.CT], writes=[t_um])
                c.op("dve", lambda e, sl=sl: e.scalar_tensor_tensor(st["gla"][:, :], st["gla"][:, :], ebT[:, sl, 0, 127:128], um[:, :, :].rearrange("p h d -> p (h d)"), ALU.mult, ALU.add),
                     reads=[t_um, t_eb[sl], st["t_gla"]], writes=[st["t_gla"]])
                c.op("act", lambda e: e.copy(st["gla_b"][:, :], st["gla"][:, :]), reads=[st["t_gla"]], writes=[st["t_gla"]])
                if self.hf == 1 and j == 7:
                    for h in range(4):
                        c.dma("sp", self.o["gla_p"].ap()[l][h], st["gla"][32 * h:32 * h + 32, 64 * h:64 * h + 64], reads=[st["t_gla"]], key="og")
            else:
                c.op("dve", lambda e, sl=sl: e.tensor_tensor(kdm[:, :, :], kd[:, sl, :].unsqueeze(1).to_broadcast([128, 16, 128]),
                                                             self.cb("smtok").unsqueeze(2).to_broadcast([128, 16, 128]), ALU.mult), reads=[t_q[sl], self.CT], writes=[t_s])
                c.op("dve", lambda e, sl=sl: e.tensor_tensor(Sg0[:, :, :], Sg0[:, :, :], ebT[:, sl, 0, 7:128:8].unsqueeze(2).to_broadcast([128, 16, 256]), ALU.mult),
                     reads=[t_s, t_eb[sl]], writes=[t_s])
                for b2 in range(8):
                    bank = 6 if (b2 % 2 == 0) else 0
                    psu = self.psb(bank)
                    for bb in range(2):
                        b = b2 * 2 + bb
                        c.op("pe", lambda e, b=b, bb=bb, psu=psu, j=j: e.matmul(psu[:, bb * 256:(bb + 1) * 256], lhsT=kdm[:, b, :], rhs=gtok[:, j, 128:384], start=True, stop=True),
                             reads=[t_s, t_gtok[j]], writes=[self.pst[bank]])
                    c.op("dve", lambda e, b2=b2, psu=psu: e.tensor_tensor(Sg0[:, 2 * b2:2 * b2 + 2, :], Sg0[:, 2 * b2:2 * b2 + 2, :], psu.rearrange("p (b n) -> p b n", b=2), ALU.add),
                         reads=[self.pst[bank], t_s], writes=[t_s])
                for h in range(4):
                    c.dma("sp", self.o["gla_s"].ap()[l][:, h, :, :].rearrange("b d v -> d b v"), Sg0[32 * h:32 * h + 32, :, 64 * h:64 * h + 64], reads=[t_s], key="og")
            yield "S"
            o4 = ps5[:, 0:256].rearrange("p (h d) -> p h d", h=4)
            c.op("act", lambda e, sl=sl, o4=o4: e.activation(otmp[:, sl, :, :], o4, AF.Square), reads=[self.pst[ob]], writes=[t_o[sl]])
            yield
            c.op("dve", lambda e, sl=sl: e.tensor_reduce(oss[:, sl, 0:4], otmp[:, sl, :, :], AX.X, ALU.add), reads=[t_o[sl]], writes=[t_o[sl]])
            yield
            c.op("act", lambda e, sl=sl: e.activation(oss[:, sl, 4:8], oss[:, sl, 0:4], AF.Ln, bias=EPS, scale=1.0 / 64), reads=[t_o[sl]], writes=[t_o[sl]])
            yield
            c.op("act", lambda e, sl=sl: e.activation(oss[:, sl, 4:8], oss[:, sl, 4:8], AF.Exp, scale=-0.5), reads=[t_o[sl]], writes=[t_o[sl]])
            yield
            c.op("dve", lambda e, sl=sl, o4=o4: e.tensor_tensor(otmp[:, sl, :, :], o4, oss[:, sl, 4:8].unsqueeze(2).to_broadcast([128, 4, 64]), ALU.mult),
                 reads=[self.pst[ob], t_o[sl]], writes=[t_o[sl]])
            yield
            c.op("dve", lambda e, sl=sl, j=j: e.tensor_tensor(self.ytok[:, j, 0:256], otmp[:, sl, :, :].rearrange("p h d -> p (h d)"), sgg[:, j, :], ALU.mult),
                 reads=[t_o[sl], t_sgg], writes=[self.t_ytok])
            yield
        self.pipeline_rr([body(j) for j in range(T)], 3)
        c.release(alltoks)

    def ssd(self, MOFF):
        c, d, l, T, NT = self.c, self.d, self.l, self.T, self.NT
        st = self.S[l]
        Tp = 8
        NP = 1024
        o = MOFF
        o_xp = o
        XP = self.av(o, [128, 6, 1027], BF16); o += 12324
        XS = self.av(o, [128, 6, 16, 11], BF16); o += 2112
        xcT = self.av(o, [128, 5, 1152], BF16); o += 11520
        o_btz = o
        BTz = self.av(o, [128, 2, 1152], BF16); o += 4608
        sz = self.av(o, [128, 9, 512], BF16); o += 9216
        dtraw = self.av(o, [128, 9, 8], F32); o += 288
        dtv = self.av(o, [128, 9, 8], F32); o += 288
        adt = self.av(o, [128, 9, 8], F32); o += 288
        acum = self.av(o, [128, 9, 8], F32); o += 288
        eacum = self.av(o, [128, 9, 8], F32); o += 288
        tail = self.av(o, [128, 9, 8], F32); o += 288
        cdrep = self.av(o, [128, 9, 8], F32); o += 288
        adth = self.av(o, [128, 9, 8], BF16); o += 144
        adtl = self.av(o, [128, 9, 8], BF16); o += 144
        cdx = self.av(o, [128, 4, 16], F32); o += 256
        yss = self.av(o, [128, 2, 4], F32); o += 32
        o_tile = o
        cvt = self.av(o_tile + 8192, [128, 768], F32)
        cvo = self.av(o_tile + 8192 + 3072, [128, 6, 48], F32)
        xtok = self.av(o, [128, 2, 768], BF16); o += 3072
        Lm = self.av(o, [128, 8, 128], BF16); o += 2048
        MT = self.av(o, [128, 2, 8, 128], BF16); o += 4096
        xdt = self.av(o, [128, 2, 512], BF16); o += 2048
        xdtt = self.av(o, [128, 2, 512], BF16); o += 2048
        xD = self.av(o, [128, 2, 512], BF16); o += 2048
        ytmp = self.av(o, [128, 2, 512], F32); o += 4096
        ytk = self.av(o, [128, 2, 512], BF16); o += 2048
        ysq = self.av(o, [128, 512], F32); o += 2048
        o_ctm = o
        CTm = self.av(o, [128, 16, 128], BF16); o += 4096
        ctmp = [self.av(o_tile + i * 4096, [128, 1024], F32) for i in range(2)]
        oo = o_xp
        St32 = [self.av(oo + i * 2048, [128, 512], F32) for i in range(2)]; oo += 4096
        Stb = [self.av(oo + i * 2048, [128, 2, 4, 128], BF16) for i in range(2)]; oo += 4096
        SbdT = [self.av(oo + i * 2048, [128, 2, 512], BF16) for i in range(2)]; oo += 4096
        assert oo <= o_xp + 12324 + 2112
        Bm = self.av(o_btz, [128, 2, 16, 64], BF16)

        t_XP = [c.new_tok() for _ in range(6)]
        t_XS = [c.new_tok() for _ in range(6)]
        t_xc = [[c.new_tok() for _ in self.groups] for _ in range(6)]
        t_sz = [c.new_tok() for _ in range(T)]
        t_dt = c.new_tok()
        t_cv = c.new_tok()
        t_ct = [c.new_tok(), c.new_tok()]
        alltoks = t_XS + sum(t_xc, []) + t_sz + [t_dt, t_cv] + t_ct
        W, wt = self.getw("ssd_x", l)
        W2, wt2 = self.getw("ssd_z", l)
        for ch in range(6):
            def ev(ps, gi, c0, c1, bt, ch=ch):
                if c0 < NP:
                    c.op("act", lambda e: e.copy(XP[:, ch, 3 + c0:3 + c1], ps), reads=[bt], writes=[t_XP[ch]])
                else:
                    c.op("act", lambda e: e.copy(XS[:, ch, :, 3:11], ps.rearrange("p (b i) -> p b i", i=8)), reads=[bt], writes=[t_XS[ch]])
            self.proj_fm(W, wt, ch * 128, 128, self.uT, lambda gi: [self.t_uTg[gi]], ev)

        self.prefetch("w_out", l)

        def ev_z(ps, j, bt):
            c.op("act", lambda e: e.activation(sz[:, j, :], ps, AF.Silu), reads=[bt], writes=[t_sz[j]])

        def ev_dt(ps, j, bt):
            c.op("dve", lambda e: e.tensor_copy(dtraw[:, j, :], ps), reads=[bt], writes=[t_dt])

        self.proj_tm(W2, wt2, 512, 8, ev_dt)
        dtb = self.rep8[:, 0:8].unsqueeze(1).to_broadcast([128, T, 8])
        c.op("dve", lambda e: e.tensor_tensor(dtv[:, 0:T, :], dtraw[:, 0:T, :], dtb, ALU.add), reads=[t_dt, self.PT_], writes=[t_dt])
        c.op("act", lambda e: e.activation(dtv[:, 0:T, :], dtv[:, 0:T, :], AF.Exp), reads=[t_dt], writes=[t_dt])
        c.op("act", lambda e: e.activation(dtv[:, 0:T, :], dtv[:, 0:T, :], AF.Ln, bias=1.0), reads=[t_dt], writes=[t_dt])
        c.op("dve", lambda e: e.tensor_tensor(adt[:, 0:T, :], dtv[:, 0:T, :], self.arep[:, :].unsqueeze(1).to_broadcast([128, T, 8]), ALU.mult), reads=[t_dt, self.PT_], writes=[t_dt])
        c.op("dve", lambda e: e.tensor_copy(adth[:, 0:T, :], adt[:, 0:T, :]), reads=[t_dt], writes=[t_dt])
        c.op("dve", lambda e: e.tensor_tensor(tail[:, 0:T, :], adt[:, 0:T, :], adth[:, 0:T, :], ALU.subtract), reads=[t_dt], writes=[t_dt])
        c.op("dve", lambda e: e.tensor_copy(adtl[:, 0:T, :], tail[:, 0:T, :]), reads=[t_dt], writes=[t_dt])
        ps3 = self.psb(3)
        adp = adt[:, 0:Tp, :].rearrange("p j h -> p (j h)")
        c.op("pe", lambda e: e.matmul(ps3[:, 0:64], lhsT=self.cf("triP"), rhs=adp, start=True, stop=True), reads=[t_dt, self.CT], writes=[self.pst[3]])
        c.op("pe", lambda e: e.matmul(ps3[:, 128:192], lhsT=self.cf("onesf"), rhs=adp, start=True, stop=True), reads=[t_dt, self.CT], writes=[self.pst[3]])
        if self.has_s:
            c.op("pe", lambda e: e.matmul(ps3[:, 64:72], lhsT=self.cf("triS"), rhs=adt[:, 8, :], start=True, stop=True), reads=[t_dt, self.CT], writes=[self.pst[3]])
            c.op("pe", lambda e: e.matmul(ps3[:, 192:200], lhsT=self.cf("sameS"), rhs=adt[:, 8, :], start=True, stop=True), reads=[t_dt, self.CT], writes=[self.pst[3]])
        n8 = T * 8
        fl = lambda a: a[:, 0:T, :].rearrange("p j h -> p (j h)")
        c.op("act", lambda e: e.copy(fl(acum), ps3[:, 0:n8]), reads=[self.pst[3]], writes=[t_dt])
        c.op("act", lambda e: e.activation(fl(eacum), ps3[:, 0:n8], AF.Exp), reads=[self.pst[3]], writes=[t_dt])
        c.op("act", lambda e: e.activation(fl(cdrep), ps3[:, 128:128 + n8], AF.Exp), reads=[self.pst[3]], writes=[t_dt])
        c.op("dve", lambda e: e.tensor_tensor(fl(tail), ps3[:, 128:128 + n8], fl(acum), ALU.subtract), reads=[self.pst[3], t_dt], writes=[t_dt])
        c.op("act", lambda e: e.activation(fl(tail), fl(tail), AF.Exp), reads=[t_dt], writes=[t_dt])
        rows32 = self.av(o_tile + 12416, [128, 8, 128], F32)
        rows2 = self.av(o_ctm, [128, 8, 128], BF16)
        rows_hi = rows2
        rows_lo = rows2[32:64, :, :]
        t_rows = c.new_tok()
        alltoks.append(t_rows)
        c.dma("sp", self.scr_ac.ap()[:, 0:n8], fl(acum), reads=[t_dt], writes=[self.t_scr_ac], key="sac")
        c.dma("sp", rows32[0:T, :, :], bass.AP(self.scr_ac, 0, [[8, T], [1, 8], [72, 128]]), reads=[self.t_scr_ac], writes=[t_rows], key="sac2", nc_ok=True)
        c.op("pool", lambda e: e.tensor_copy(XP[:, :, 0:3], st["ctail"][:, :, :]), reads=[st["t_ctail"]], writes=t_XP)
        c.op("pool", lambda e: e.memset(BTz[:, :, :], 0.0), writes=[t_xc[4][gi] for gi in range(len(self.groups))])
        if self.has_s:
            c.dma("sp", cvt[0:48, :], d["st_conv"].ap()[l].rearrange("b k c -> (b k) c"), writes=[t_cv], key="cvs")
            for half, (ch0, nch) in enumerate(((0, 4), (4, 2))):
                ps = self.psb(half)
                for cc in range(nch):
                    ch = ch0 + cc
                    c.op("pe", lambda e, ps=ps, cc=cc, ch=ch: e.transpose(ps[:, cc * 48:(cc + 1) * 48], cvt[0:48, ch * 128:(ch + 1) * 128], self.cf("identf")[0:48, 0:48]),
                         reads=[t_cv, self.CT], writes=[self.pst[half]])
                c.op("dve", lambda e, ps=ps, ch0=ch0, nch=nch: e.tensor_copy(XS[:, ch0:ch0 + nch, :, 0:3], ps[:, 0:nch * 48].rearrange("p (c b k) -> p c b k", c=nch, b=16)),
                     reads=[self.pst[half]], writes=[t_XS[ch] for ch in range(ch0, ch0 + nch)])
        accS = [self.av(o_tile + 16512 + i * 512, [128, 16, 8], F32) for i in range(2)]

        def conv_id(ch):
            sl = ch % 2
            acc = ctmp[sl]
            c.op("act", lambda e: e.activation(acc[:, 0:NP], XP[:, ch, 0:NP], AF.Identity, bias=self.cbT[:, ch:ch + 1], scale=self.cwT[:, ch, 0:1]),
                 reads=[t_XP[ch], self.PT_], writes=[t_ct[sl]])
            if self.has_s:
                c.op("act", lambda e: e.activation(accS[sl], XS[:, ch, :, 0:8], AF.Identity, bias=self.cbT[:, ch:ch + 1], scale=self.cwT[:, ch, 0:1]),
                     reads=[t_XS[ch], self.PT_], writes=[t_ct[sl]])

        def conv_taps(ch):
            sl = ch % 2
            acc = ctmp[sl]
            for k in range(1, 4):
                c.op("dve", lambda e, k=k: e.scalar_tensor_tensor(acc[:, 0:NP], XP[:, ch, k:k + NP], self.cwT[:, ch, k:k + 1], acc[:, 0:NP], ALU.mult, ALU.add),
                     reads=[t_XP[ch], t_ct[sl], self.PT_], writes=[t_ct[sl]])
            if self.has_s:
                for k in range(1, 4):
                    c.op("dve", lambda e, k=k: e.scalar_tensor_tensor(accS[sl], XS[:, ch, :, k:k + 8], self.cwT[:, ch, k:k + 1], accS[sl], ALU.mult, ALU.add),
                         reads=[t_XS[ch], t_ct[sl], self.PT_], writes=[t_ct[sl]])

        def conv_silu(ch):
            sl = ch % 2
            acc = ctmp[sl]
            parts = [(acc[:, 0:NP], 0, NP, [t_xc[ch][0], t_xc[ch][1]])]
            if self.has_s:
                parts.append((accS[sl].rearrange("p b i -> p (b i)"), NP, NP + 128, [t_xc[ch][2]]))
            for (src, a0, a1, wr) in parts:
                if ch < 4:
                    c.op("act", lambda e, src=src, a0=a0, a1=a1: e.activation(xcT[:, ch, a0:a1], src, AF.Silu), reads=[t_ct[sl]], writes=wr)
                elif ch == 5:
                    c.op("act", lambda e, src=src, a0=a0, a1=a1: e.activation(xcT[:, 4, a0:a1], src, AF.Silu), reads=[t_ct[sl]], writes=wr)
                else:
                    for g in range(2):
                        c.op("act", lambda e, src=src, a0=a0, a1=a1, g=g: e.activation(BTz[64 * g:64 * g + 64, g, a0:a1], src[64 * g:64 * g + 64, :], AF.Silu), reads=[t_ct[sl]], writes=wr)

        for step in range(7):
            if step < 6:
                conv_id(step)
            if step >= 1:
                conv_silu(step - 1)
            if step < 6:
                conv_taps(step)
            zt = [step] if step < 6 else list(range(6, T))
            self.proj_tm(W2, wt2, 0, 512, ev_z, tiles=zt)
        self.prefetch("w_pg", l)
        c.op("pool", lambda e: e.tensor_copy(st["ctail"][:, :, :], XP[:, :, NP:NP + 3]), reads=t_XP, writes=[st["t_ctail"]])
        if self.hf == 1:
            self.conv_out(XP[:, :, NP:NP + 3], t_XP, 3, cvo, cvt, t_cv, self.o["conv_p"].ap()[l])
            self.conv_out(XS[:, :, :, 8:11], t_XS, 48, cvo, cvt, t_cv, self.o["conv_s"].ap()[l].rearrange("b k c -> (b k) c"))
        c.op("pool", lambda e: e.memset(rows2[:, :, :], 0.0), writes=[t_rows])
        c.op("dve", lambda e: e.tensor_copy(rows_hi[0:T, :, :], rows32[0:T, :, :]), reads=[t_rows], writes=[t_rows])
        c.op("dve", lambda e: e.tensor_tensor(rows32[0:T, :, :], rows32[0:T, :, :], rows_hi[0:T, :, :], ALU.subtract), reads=[t_rows], writes=[t_rows])
        c.op("dve", lambda e: e.tensor_copy(rows_lo[0:T, :, :], rows32[0:T, :, :]), reads=[t_rows], writes=[t_rows])
        c.release(t_ct + [t_cv, t_rows])
        t_xtok = [c.new_tok(), c.new_tok()]
        t_L = c.new_tok()
        t_MT = [c.new_tok(), c.new_tok()]
        t_xd = [c.new_tok(), c.new_tok()]
        t_y = [c.new_tok(), c.new_tok()]
        t_ytk = [c.new_tok(), c.new_tok()]
        t_yss = c.new_tok()
        t_ysq = c.new_tok()
        alltoks += t_xtok + [t_L] + t_MT + t_xd + t_y + t_ytk + [t_yss, t_ysq]
        dsk = self.rep8[:, 16:24]
        def body(j):
            is_s = self.has_s and j == 8
            sl = j % 2
            gi = j // 4
            cs = slice(j * 128, (j + 1) * 128)
            tri_b = self.cb("triS") if is_s else self.cb("triP")
            ntri_b = self.cb("ntriS") if is_s else self.cb("ntriP")
            neg_b = self.cb("negS") if is_s else self.cb("negP")
            if not is_s:
                self.keepwarm(2, 3)
            tp = self.psb(2, BF16)[:, 0:768]
            for a in range(6):
                src = xcT[:, a, cs] if a < 4 else BTz[:, a - 4, cs]
                rd = [t_xc[a][gi]] if a < 4 else [t_xc[4][gi]]
                c.op("pe", lambda e, a=a, src=src, tp=tp: e.transpose(tp[:, a * 128:(a + 1) * 128], src, self.cb("ident")), reads=rd + [self.CT], writes=[self.pst[2]])
            c.op("act", lambda e, sl=sl, tp=tp: e.copy(xtok[:, sl, :], tp), reads=[self.pst[2]], writes=[t_xtok[sl]])
            x3 = xtok[:, sl, 0:512].rearrange("p (h q) -> p h q", h=8)
            c.op("dve", lambda e, sl=sl, j=j, x3=x3: e.tensor_tensor(xdt[:, sl, :].rearrange("p (h q) -> p h q", h=8), x3, dtv[:, j, :].unsqueeze(2).to_broadcast([128, 8, 64]), ALU.mult),
                 reads=[t_xtok[sl], t_dt], writes=[t_xd[sl]])
            c.op("pool", lambda e, sl=sl, j=j: e.tensor_tensor(xdtt[:, sl, :].rearrange("p (h q) -> p h q", h=8), xdt[:, sl, :].rearrange("p (h q) -> p h q", h=8),
                                                               tail[:, j, :].unsqueeze(2).to_broadcast([128, 8, 64]), ALU.mult), reads=[t_xd[sl], t_dt], writes=[t_xd[sl]])
            c.op("pool", lambda e, sl=sl, x3=x3: e.tensor_tensor(xD[:, sl, :].rearrange("p (h q) -> p h q", h=8), x3, dsk.unsqueeze(2).to_broadcast([128, 8, 64]), ALU.mult),
                 reads=[t_xtok[sl], self.PT_], writes=[t_xd[sl]])
            cbp = self.psb(7)[:, 0:256]
            t_cb = self.pst[7]
            for g in range(2):
                c.op("pe", lambda e, g=g, cs=cs, cbp=cbp: e.matmul(cbp[:, g * 128:(g + 1) * 128], lhsT=BTz[:, g, cs], rhs=xcT[:, 4, cs], start=True, stop=True),
                     reads=[t_xc[4][gi], t_xc[5][gi]], writes=[t_cb])
            for b4 in range(2):
                bank = 4 + b4
                dst = self.psb(bank)
                rd = [t_dt, self.CT, t_rows]
                wr = [self.pst[bank]]
                rh = rows2[:, 4 * b4:4 * b4 + 4, :].rearrange("p h t -> p (h t)")
                ah = adth[:, j, 4 * b4:4 * b4 + 4].unsqueeze(2).to_broadcast([128, 4, 128])
                al = adtl[:, j, 4 * b4:4 * b4 + 4].unsqueeze(2).to_broadcast([128, 4, 128])
                ng = neg_b.unsqueeze(1).to_broadcast([128, 4, 128])
                es = self.Esel[:, j, :]
                c.op("pe", lambda e, dst=dst, es=es, rh=rh: e.matmul(dst, lhsT=es, rhs=rh, start=True, stop=False), reads=rd, writes=wr)
                c.op("pe", lambda e, dst=dst, ah=ah, ntri_b=ntri_b: e.matmul(dst, lhsT=ntri_b, rhs=ah, start=False, stop=False), reads=rd, writes=wr)
                c.op("pe", lambda e, dst=dst, al=al, ntri_b=ntri_b: e.matmul(dst, lhsT=ntri_b, rhs=al, start=False, stop=False), reads=rd, writes=wr)
                c.op("pe", lambda e, dst=dst, ng=ng: e.matmul(dst, lhsT=self.cb("ident"), rhs=ng, start=False, stop=True), reads=rd, writes=wr)
            yield
            for hb in range(2):
                c.op("act", lambda e, hb=hb: e.activation(Lm[:, hb * 4:(hb + 1) * 4, :].rearrange("p h t -> p (h t)"), self.psb(4 + hb), AF.Exp),
                     reads=[self.pst[4 + hb]], writes=[t_L])
            yield
            c.op("dve", lambda e, sl=sl: e.tensor_tensor(MT[:, sl, :, :].rearrange("p (g r) t -> p g r t", g=2), Lm[:, :, :].rearrange("p (g r) t -> p g r t", g=2),
                                                         cbp.rearrange("p (g t) -> p g t", g=2).unsqueeze(2).to_broadcast([128, 2, 4, 128]), ALU.mult),
                 reads=[t_L, t_cb], writes=[t_MT[sl]])
            yield
            yp = self.psb(6)
            c.op("pe", lambda e, sl=sl: e.matmul(yp[:, :], lhsT=self.cb("ident"), rhs=xD[:, sl, :], start=True, stop=False), reads=[t_xd[sl], self.CT], writes=[self.pst[6]])
            for h in range(8):
                c.op("pe", lambda e, sl=sl, h=h: e.matmul(yp[:, h * 64:(h + 1) * 64], lhsT=MT[:, sl, h, :], rhs=xdt[:, sl, h * 64:(h + 1) * 64], start=False, stop=True),
                     reads=[t_MT[sl], t_xd[sl]], writes=[self.pst[6]])
            yi = self.psb(3)
            if not is_s:
                c.op("pe", lambda e, cs=cs: e.matmul(yi[:, :], lhsT=xcT[:, 4, cs], rhs=st["ssd_b"][:, :], start=True, stop=True), reads=[t_xc[5][gi], st["t_ssd"]], writes=[self.pst[3]])
            else:
                c.release(t_XP + t_XS + [t_rows] + t_xc[4])
                t_st = c.new_tok()
                t_bm = c.new_tok()
                t_S32 = [c.new_tok(), c.new_tok()]
                t_Stb = [c.new_tok(), c.new_tok()]
                t_Sbd = [c.new_tok(), c.new_tok()]
                alltoks.extend([t_st, t_bm] + t_S32 + t_Stb + t_Sbd)

                def st_in(grp):
                    s2 = grp % 2
                    c.dma("sp", St32[s2].rearrange("p (b a n) -> p b a n", b=2, a=4), d["st_ssm"].ap()[l][2 * grp:2 * grp + 2].rearrange("b (a hh) p n -> (hh p) b a n", hh=2),
                          writes=[t_S32[s2]], key="st%d" % s2)

                st_in(0)
                st_in(1)
                c.op("dve", lambda e, cs=cs: e.tensor_tensor(CTm[:, :, :], xcT[:, 4, cs].unsqueeze(1).to_broadcast([128, 16, 128]), self.cb("smbt").rearrange("p (b t) -> p b t", b=16), ALU.mult),
                     reads=[t_xc[5][gi], self.CT], writes=[t_st])
                for i2 in range(2):
                    c.op("pool", lambda e, i2=i2: e.memset(Stb[i2][:, :, :, :], 0.0), writes=[t_Stb[i2]])
                for g in range(2):
                    btk = xtok[:, sl, 512 + g * 128 + g * 64:512 + g * 128 + g * 64 + 64]
                    c.op("dve", lambda e, g=g, btk=btk: e.tensor_tensor(Bm[:, g, :, :], btk.unsqueeze(1).to_broadcast([128, 16, 64]), self.cb("smtok").unsqueeze(2).to_broadcast([128, 16, 64]), ALU.mult),
                         reads=[t_xtok[sl], self.CT], writes=[t_bm])
                psx = self.psb(7)
                c.op("dve", lambda e: e.tensor_copy(ysq[:, :].rearrange("p (h q) -> p h q", h=8), adt[:, 8, :].unsqueeze(2).to_broadcast([128, 8, 64])), reads=[t_dt, t_ysq], writes=[t_ysq])
                for a in range(4):
                    c.op("pe", lambda e, a=a: e.matmul(psx[:, a * 16:(a + 1) * 16], lhsT=ysq[:, a * 128:(a + 1) * 128], rhs=self.cf("smtok"), start=True, stop=True),
                         reads=[t_ysq, self.CT, t_MT[sl]], writes=[self.pst[7]])
                c.op("act", lambda e: e.activation(cdx[:, :, :].rearrange("p a b -> p (a b)"), psx[:, 0:64], AF.Exp), reads=[self.pst[7]], writes=[t_dt])
                for grp in range(8):
                    s2 = grp % 2
                    s4 = St32[s2].rearrange("p (b a n) -> p b a n", b=2, a=4)
                    c.op("dve", lambda e, s4=s4, s2=s2: e.tensor_copy(Stb[s2][:, :, 0:2, 0:64], s4[:, :, 0:2, :]), reads=[t_S32[s2]], writes=[t_Stb[s2]])
                    c.op("dve", lambda e, s4=s4, s2=s2: e.tensor_copy(Stb[s2][:, :, 2:4, 64:128], s4[:, :, 2:4, :]), reads=[t_S32[s2]], writes=[t_Stb[s2]])
                    tp2 = self.psb(2, BF16)
                    for bb in range(2):
                        for a in range(4):
                            c.op("pe", lambda e, bb=bb, a=a, tp2=tp2, s2=s2: e.transpose(tp2[:, (bb * 4 + a) * 128:(bb * 4 + a + 1) * 128], Stb[s2][:, bb, a, :], self.cb("ident")),
                                 reads=[t_Stb[s2], self.CT], writes=[self.pst[2]])
                    c.op("act", lambda e, s2=s2, tp2=tp2: e.copy(SbdT[s2][:, :, :].rearrange("p b n -> p (b n)"), tp2), reads=[self.pst[2]], writes=[t_Sbd[s2]])
                    for bb in range(2):
                        bq = grp * 2 + bb
                        c.op("pe", lambda e, bq=bq, bb=bb, s2=s2: e.matmul(yi[:, :], lhsT=CTm[:, bq, :], rhs=SbdT[s2][:, bb, :], start=(bq == 0), stop=(bq == 15)),
                             reads=[t_st, t_Sbd[s2]], writes=[self.pst[3]])
                    bank = 4 + s2
                    pu = self.psb(bank)
                    for a in range(4):
                        g = a // 2
                        c.op("pe", lambda e, a=a, g=g, pu=pu, grp=grp: e.matmul(pu[:, a * 128:(a + 1) * 128], lhsT=xdtt[:, sl, a * 128:(a + 1) * 128],
                                                                                rhs=Bm[:, g, 2 * grp:2 * grp + 2, :].rearrange("p b n -> p (b n)"), start=True, stop=True),
                             reads=[t_xd[sl], t_bm, t_L], writes=[self.pst[bank]])
                    cdv = cdx[:, :, 2 * grp:2 * grp + 2].rearrange("p a b -> p b a").unsqueeze(3).to_broadcast([128, 2, 4, 64])
                    c.op("dve", lambda e, s4=s4, cdv=cdv: e.tensor_tensor(s4, s4, cdv, ALU.mult), reads=[t_S32[s2], t_dt, t_Stb[s2]], writes=[t_S32[s2]])
                    c.op("dve", lambda e, s4=s4, pu=pu: e.tensor_tensor(s4, s4, pu[:, :].rearrange("p (a b n) -> p b a n", a=4, b=2), ALU.add), reads=[t_S32[s2], self.pst[bank]], writes=[t_S32[s2]])
                    c.dma("sp", self.o["ssm_s"].ap()[l][2 * grp:2 * grp + 2].rearrange("b (a hh) p n -> (hh p) b a n", hh=2), s4, reads=[t_S32[s2]], key="so%d" % s2)
                    if grp + 2 < 8:
                        st_in(grp + 2)
            if not is_s:
                up = self.psb(0)
                for g in range(2):
                    c.op("pe", lambda e, g=g, sl=sl: e.matmul(up[:, g * 256:(g + 1) * 256], lhsT=xtok[:, sl, 512 + g * 128:512 + (g + 1) * 128], rhs=xdtt[:, sl, g * 256:(g + 1) * 256],
                                                              start=True, stop=True), reads=[t_xtok[sl], t_xd[sl], t_MT[sl]], writes=[self.pst[0]])
                S3 = st["ssd"][:, :].rearrange("p (h q) -> p h q", h=8)
                c.op("dve", lambda e, S3=S3, j=j: e.tensor_tensor(S3, S3, cdrep[:, j, :].unsqueeze(2).to_broadcast([128, 8, 64]), ALU.mult), reads=[st["t_ssd"], t_dt], writes=[st["t_ssd"]])
                c.op("dve", lambda e: e.tensor_tensor(st["ssd"][:, :], st["ssd"][:, :], up[:, :], ALU.add), reads=[st["t_ssd"], self.pst[0]], writes=[st["t_ssd"]])
                c.op("act", lambda e: e.copy(st["ssd_b"][:, :], st["ssd"][:, :]), reads=[st["t_ssd"]], writes=[st["t_ssd"]])
                if self.hf == 1 and j == 7:
                    pso = self.psb(0)
                    for a in range(4):
                        c.op("pe", lambda e, a=a: e.transpose(pso[:, a * 128:(a + 1) * 128], st["ssd"][:, a * 128:(a + 1) * 128], self.cf("identf")), reads=[st["t_ssd"], self.CT], writes=[self.pst[0]])
                    so = ysq[:, 0:256].rearrange("p (a n) -> p a n", a=4)
                    for g in range(2):
                        c.op("dve", lambda e, g=g: e.tensor_copy(so[:, 2 * g:2 * g + 2, :], pso[:, :].rearrange("p (a n) -> p a n", a=4)[:, 2 * g:2 * g + 2, 64 * g:64 * g + 64]),
                             reads=[self.pst[0], t_ysq], writes=[t_ysq])
                    c.dma("sp", self.o["ssm_p"].ap()[l].rearrange("(a hh) p n -> (hh p) a n", hh=2), so, reads=[t_ysq], key="oss")
            yield
            y3 = ytmp[:, sl, :].rearrange("p (h q) -> p h q", h=8)
            c.op("dve", lambda e, y3=y3, j=j: e.tensor_tensor(y3, yi[:, :].rearrange("p (h q) -> p h q", h=8), eacum[:, j, :].unsqueeze(2).to_broadcast([128, 8, 64]), ALU.mult),
                 reads=[self.pst[3], t_dt], writes=[t_y[sl]])
            c.op("dve", lambda e, sl=sl: e.tensor_tensor(ytmp[:, sl, :], yp[:, :], ytmp[:, sl, :], ALU.add), reads=[self.pst[6], t_y[sl]], writes=[t_y[sl]])
            yield
            c.op("pool", lambda e, sl=sl, j=j: e.tensor_tensor(ytmp[:, sl, :], ytmp[:, sl, :], sz[:, j, :], ALU.mult), reads=[t_y[sl], t_sz[j]], writes=[t_y[sl]])
            for g in range(2):
                c.op("act", lambda e, sl=sl, g=g: e.activation(ysq[:, g * 256:(g + 1) * 256], ytmp[:, sl, g * 256:(g + 1) * 256], AF.Square, accum_out=yss[:, sl, g:g + 1]),
                     reads=[t_y[sl], t_yss], writes=[t_ysq, t_yss])
            c.op("act", lambda e, sl=sl: e.activation(yss[:, sl, 2:4], yss[:, sl, 0:2], AF.Ln, bias=EPS, scale=1.0 / 256), reads=[t_yss], writes=[t_yss])
            c.op("act", lambda e, sl=sl: e.activation(yss[:, sl, 2:4], yss[:, sl, 2:4], AF.Exp, scale=-0.5), reads=[t_yss], writes=[t_yss])
            c.op("dve", lambda e, sl=sl: e.tensor_tensor(ytmp[:, sl, :].rearrange("p (g n) -> p g n", g=2), ytmp[:, sl, :].rearrange("p (g n) -> p g n", g=2),
                                                         yss[:, sl, 2:4].unsqueeze(2).to_broadcast([128, 2, 256]), ALU.mult), reads=[t_y[sl], t_yss], writes=[t_y[sl]])
            c.op("pool", lambda e, sl=sl: e.tensor_tensor(ytk[:, sl, :], ytmp[:, sl, :], self.ssdnw[:, :], ALU.mult), reads=[t_y[sl], self.PT_], writes=[t_ytk[sl]])
            yield
            tpy = self.psb(1, BF16)[:, 0:1024]
            for a in range(4):
                c.op("pe", lambda e, a=a, tpy=tpy, sl=sl: e.transpose(tpy[:, a * 128:(a + 1) * 128], ytk[:, sl, a * 128:(a + 1) * 128], self.cb("ident")), reads=[t_ytk[sl], self.CT], writes=[self.pst[1]])
            for a in range(4):
                c.op("pe", lambda e, a=a, tpy=tpy: e.transpose(tpy[:, (4 + a) * 128:(5 + a) * 128], self.ytok[:, j, a * 128:(a + 1) * 128], self.cb("ident")), reads=[self.t_ytok, self.CT], writes=[self.pst[1]])
            c.op("act", lambda e, tpy=tpy, cs=cs: e.copy(self.uT[:, 0:8, cs], tpy.rearrange("p (a t) -> p a t", a=8)), reads=[self.pst[1]], writes=[self.t_uTg[gi]])
        self.pipeline_sched([body(j) for j in range(T)], [(6, 4), (4, 2), (0, 0), (1, 0), (3, 1), (2, 0), (5, 2)])
        c.release(alltoks + t_XP + t_XS)

    def conv_out(self, src, toks, nrow, cvo, cvt, t_cv, dst):
        c = self.c
        if nrow == 3:
            c.op("dve", lambda e: e.tensor_copy(cvo[:, :, 0:3], src), reads=toks, writes=[t_cv])
        else:
            c.op("dve", lambda e: e.tensor_copy(cvo[:, :, :].rearrange("p c (b k) -> p c b k", b=16), src), reads=toks, writes=[t_cv])
        for half, (ch0, nch) in enumerate(((0, 4), (4, 2))):
            ps = self.psb(half)
            for cc in range(nch):
                ch = ch0 + cc
                c.op("pe", lambda e, ps=ps, cc=cc, ch=ch: e.transpose(ps[0:nrow, cc * 128:(cc + 1) * 128], cvo[:, ch, 0:nrow], self.cf("identf")), reads=[t_cv, self.CT], writes=[self.pst[half]])
            c.op("act", lambda e, ps=ps, ch0=ch0, nch=nch: e.copy(cvt[0:nrow, ch0 * 128:(ch0 + nch) * 128], ps[0:nrow, 0:nch * 128]), reads=[self.pst[half]], writes=[t_cv])
        c.dma("sp", dst, cvt[0:nrow, :], reads=[t_cv], key="ocv")

    def y_transposes(self):
        c = self.c
        for j in range(self.T):
            cs = slice(j * 128, (j + 1) * 128)
            tp = self.psb(2, BF16)[:, 0:512]
            for a in range(4):
                c.op("pe", lambda e, a=a, j=j, tp=tp: e.transpose(tp[:, a * 128:(a + 1) * 128], self.ytok[:, j, a * 128:(a + 1) * 128], self.cb("ident")),
                     reads=[self.t_ytok, self.CT], writes=[self.pst[2]])
            c.op("act", lambda e, tp=tp, cs=cs: e.copy(self.uT[:, 4:8, cs], tp.rearrange("p (a t) -> p a t", a=4)), reads=[self.pst[2]], writes=[self.t_uTg[j // 4]])
        c.release([self.t_ytok])

    def out_proj(self, MOFF):
        c, d, l = self.c, self.d, self.l
        self.hbT = self.av(0, [128, 8, 1152], BF16)
        self.t_hb = [c.new_tok() for _ in self.groups]
        self.ptok = self.av(18432, [128, 9, 256], BF16)
        self.t_pt = c.new_tok()
        c.dma("pool", self.ptok[:, 0:self.T, :], d["p"].ap()[l][self.row0:self.row0 + self.NT, :].rearrange("(j p) n -> p j n", p=128), writes=[self.t_pt], key="ptk")
        W, wt = self.getw("w_out", l)
        for dch in range(8):
            def ev(ps, gi, c0, c1, bt, dch=dch):
                c.op("dve", lambda e: e.tensor_tensor(self.hT[:, dch, c0:c1], self.hT[:, dch, c0:c1], ps, ALU.add), reads=[bt, self.t_hT[dch][gi]], writes=[self.t_hT[dch][gi]])
                c.op("act", lambda e: e.copy(self.hbT[:, dch, c0:c1], self.hT[:, dch, c0:c1]), reads=[self.t_hT[dch][gi]], writes=[self.t_hb[gi]])
            self.proj_fm(W, wt, dch * 128, 128, self.uT, lambda gi: [self.t_uTg[gi]], ev)
        self.prefetch("w_pe", l)

    def ple(self, MOFF):
        c, d, l, T, NT = self.c, self.d, self.l, self.T, self.NT
        ptok = self.ptok
        sgt = [self.av(18432 + 4608 + i * 2048, [128, 512], F32) for i in range(2)]
        t_pt = self.t_pt
        t_sg = [c.new_tok(), c.new_tok()]
        t_pT = [c.new_tok() for _ in self.groups]
        for j in range(T):
            cs = slice(j * 128, (j + 1) * 128)
            tp = self.psb(2, BF16)[:, 0:256]
            for a in range(2):
                c.op("pe", lambda e, a=a, j=j, tp=tp: e.transpose(tp[:, a * 128:(a + 1) * 128], ptok[:, j, a * 128:(a + 1) * 128], self.cb("ident")), reads=[t_pt, self.CT], writes=[self.pst[2]])
            c.op("act", lambda e, tp=tp, cs=cs: e.copy(self.pT[:, :, cs], tp.rearrange("p (a t) -> p a t", a=2)), reads=[self.pst[2]], writes=[t_pT[j // 4]])
        Wg, wtg = self.getw("w_pg", l)
        We, wte = self.getw("w_pe", l)
        it = 0
        for dch in range(8):
            for gi, (c0, c1) in enumerate(self.groups):
                n = c1 - c0
                s = it % 2
                it += 1
                pa, pb = self.psb(0 + 2 * s), self.psb(1 + 2 * s)
                ba, bb = 0 + 2 * s, 1 + 2 * s
                for k in range(8):
                    c.op("pe", lambda e, k=k, pa=pa, n=n, c0=c0, c1=c1, dch=dch: e.matmul(pa[:, 0:n], lhsT=Wg[:, k, dch * 128:(dch + 1) * 128], rhs=self.hbT[:, k, c0:c1], start=(k == 0), stop=(k == 7)),
                         reads=[wtg, self.t_hb[gi]], writes=[self.pst[ba]])
                for k in range(2):
                    c.op("pe", lambda e, k=k, pb=pb, n=n, c0=c0, c1=c1, dch=dch: e.matmul(pb[:, 0:n], lhsT=We[:, k, dch * 128:(dch + 1) * 128], rhs=self.pT[:, k, c0:c1], start=(k == 0), stop=(k == 1)),
                         reads=[wte, t_pT[gi]], writes=[self.pst[bb]])
                c.op("act", lambda e, s=s, pa=pa, n=n: e.activation(sgt[s][:, 0:n], pa[:, 0:n], AF.Sigmoid), reads=[self.pst[ba]], writes=[t_sg[s]])
                c.op("dve", lambda e, s=s, pb=pb, n=n: e.tensor_tensor(sgt[s][:, 0:n], sgt[s][:, 0:n], pb[:, 0:n], ALU.mult), reads=[self.pst[bb], t_sg[s]], writes=[t_sg[s]])
                c.op("pool", lambda e, s=s, n=n, dch=dch, c0=c0, c1=c1: e.tensor_tensor(self.hT[:, dch, c0:c1], self.hT[:, dch, c0:c1], sgt[s][:, 0:n], ALU.add),
                     reads=[t_sg[s], self.t_hT[dch][gi]], writes=[self.t_hT[dch][gi]])
        if self.next_unit is not None:
            self.prefetch("swa", self.next_unit[0])
        c.release([t_pt] + t_sg + t_pT + self.t_hb)

    def store_y(self, MOFF):
        c, T = self.c, self.T
        base = 18432 + 4608 + 4096
        ost = [self.av(base + i * 4096, [128, 1024], F32) for i in range(2)]
        t_o = [c.new_tok(), c.new_tok()]
        for j in range(T):
            s = j % 2
            gi = j // 4
            cs = slice(j * 128, (j + 1) * 128)
            for half in range(2):
                bank = 4 + half
                ps = self.psb(bank)
                for kk in range(4):
                    k = half * 4 + kk
                    c.op("pe", lambda e, ps=ps, kk=kk, k=k, cs=cs: e.transpose(ps[:, kk * 128:(kk + 1) * 128], self.hT[:, k, cs], self.cf("identf")),
                         reads=[self.t_hT[k][gi], self.CT], writes=[self.pst[bank]])
                if half == 0:
                    c.op("act", lambda e, ps=ps, s=s: e.copy(ost[s][:, 0:512], ps[:, :]), reads=[self.pst[bank]], writes=[t_o[s]])
                else:
                    c.op("dve", lambda e, ps=ps, s=s: e.tensor_copy(ost[s][:, 512:1024], ps[:, :]), reads=[self.pst[bank]], writes=[t_o[s]])
            c.dma("sp", self.o["y"].ap()[self.row0 + j * 128:self.row0 + (j + 1) * 128, :], ost[s], reads=[t_o[s]], key="oy%d" % s)
        c.release(t_o + [t for row in self.t_hT for t in row])


_CACHE = {}


def _get_nc():
    if "nc" not in _CACHE:
        k = K()
        _CACHE["nc"] = k.build()
        _CACHE["k"] = k
    return _CACHE["nc"]


def kernel(x_prompt, x_sample, state_ssm, state_conv, state_gla, cache_swa_k, cache_swa_v, p_prompt, p_sample,
           rel_bias, norm_w, w_in, conv_w, conv_b, dt_bias, a_log, d_skip, ssd_norm_w, gla_w_gk, gla_b_gk,
           gla_norm_w, q_norm_w, k_norm_w, attn_sinks, w_out, w_pe, w_pg):
    f = lambda a: np.ascontiguousarray(np.asarray(a, dtype=np.float32))
    x_prompt, x_sample, state_ssm, state_conv, state_gla = map(f, (x_prompt, x_sample, state_ssm, state_conv, state_gla))
    cache_swa_k, cache_swa_v, p_prompt, p_sample = map(f, (cache_swa_k, cache_swa_v, p_prompt, p_sample))
    w_in = f(w_in)
    sq0 = 2072
    perm = np.concatenate([
        sq0 + np.concatenate([np.arange(0, 64), np.arange(128, 192), np.arange(64, 128), np.arange(192, 256)]),
        np.arange(2328, 2456), np.arange(2456, 2584), np.arange(2584, 2840),
        np.arange(1288, 1416), np.arange(1416, 1544), np.arange(2056, 2072), np.arange(1416, 1544), np.arange(1544, 1800), np.arange(1800, 2056),
        np.arange(512, 1280), np.arange(0, 512), np.arange(1280, 1288)])
    assert perm.shape[0] == 2968
    w_in_p = np.ascontiguousarray(w_in[:, :, perm])
    cstf, cstb, oh = make_consts()
    rep8 = np.concatenate([f(dt_bias), f(a_log), f(d_skip)], axis=1)
    wgk = np.concatenate([f(gla_w_gk), f(gla_b_gk)[:, None, :]], axis=1)
    hdw = np.concatenate([np.tile(f(q_norm_w), (1, 4)), np.tile(f(k_norm_w), (1, 2)), f(gla_norm_w)], axis=1)
    shared = dict(rel_bias=f(rel_bias), norm_w=f(norm_w), w_in=w_in_p, conv_w=f(conv_w), conv_b=f(conv_b), rep8=rep8,
                  ssd_norm_w=f(ssd_norm_w), wgk=np.ascontiguousarray(wgk), hdw=np.ascontiguousarray(hdw), sinks=f(attn_sinks),
                  w_out=f(w_out), w_pe=f(w_pe), w_pg=f(w_pg), cstf=cstf, cstb=cstb, oh=oh)
    in_maps = []
    for core in range(8):
        sl = slice(16 * core, 16 * core + 16)
        m = dict(shared)
        m["x_tok"] = np.ascontiguousarray(np.concatenate([x_prompt[core], x_sample[sl].reshape(128, 1024)], axis=0))
        m["p_tok"] = np.ascontiguousarray(np.concatenate([p_prompt[:, core], p_sample[:, sl].reshape(2, 128, 256)], axis=1))
        m["st_ssm"] = np.ascontiguousarray(state_ssm[:, sl])
        m["st_conv"] = np.ascontiguousarray(state_conv[:, sl])
        m["st_gla"] = np.ascontiguousarray(state_gla[:, sl])
        m["ck"] = np.ascontiguousarray(cache_swa_k[:, sl].reshape(2, 16, 128, 128))
        m["cv"] = np.ascontiguousarray(cache_swa_v[:, sl].reshape(2, 16, 128, 128))
        in_maps.append(m)
    nc = _get_nc()
    res = run_bass_kernel_spmd(nc, in_maps, core_ids=list(range(8)))
    R = res.results
    cat = lambda name, ax: np.concatenate([np.asarray(r[name]) for r in R], axis=ax)
    stk = lambda name: np.stack([np.asarray(r[name]) for r in R], axis=1)
    y_all = np.stack([np.asarray(r["y_tok"]) for r in R], axis=0)
    y_prompt = np.ascontiguousarray(y_all[:, :2048])
    y_sample = np.ascontiguousarray(y_all[:, 2048:].reshape(128, 8, 1024))
    ssm_p = stk("ssm_p")
    conv_p = stk("conv_p")
    gla_p = stk("gla_p")
    swak_p = stk("swak_p").reshape(2, 8, 128, 2, 64)
    swav_p = stk("swav_p").reshape(2, 8, 128, 2, 64)
    ssm_s = cat("ssm_s", 1)
    conv_s = cat("conv_s", 1)
    gla_s = cat("gla_s", 1)
    swak_s = cat("swak_s", 1).reshape(2, 128, 128, 2, 64)
    swav_s = cat("swav_s", 1).reshape(2, 128, 128, 2, 64)
    outs = (y_prompt, y_sample, ssm_p, conv_p, gla_p, swak_p, swav_p, ssm_s, conv_s, gla_s, swak_s, swav_s)
    if DEBUG:
        _CACHE["dbg"] = {nm: [np.asarray(r[nm]) for r in R] for nm in DEBUG}
    return tuple(np.ascontiguousarray(o.astype(np.float32)) for o in outs)
```
